# Optimizing a Trainium2 kernel written in Bass

```python
import jax
import jax.numpy as jnp
from jax import lax
import numpy as np

D_MODEL = 1024
BATCH = 16
SEQ = 2048
DEPTH = 2

CTX_LEN = 256
GRID_W = 64
N_MOD = 9
D_FF = ((8 * D_MODEL // 3 + 127) // 128) * 128
MIX_WIDTH = D_MODEL
POOL_WIDTH = D_MODEL // 4
POOL_WINDOWS = (2, 4, 8, 16)
POOL_GROUPS = len(POOL_WINDOWS)
POOL_GROUP = POOL_WIDTH // POOL_GROUPS
RET_WIDTH = D_MODEL // 2
RET_HEADS = 4
RET_HEAD_DIM = RET_WIDTH // RET_HEADS
RET_CHUNK = 128
CONV_WIDTH = D_MODEL // 4
CONV_K = 31
IN_WIDTH = POOL_WIDTH + 4 * RET_WIDTH + 2 * CONV_WIDTH
Q_OFF = POOL_WIDTH
K_OFF = Q_OFF + RET_WIDTH
V_OFF = K_OFF + RET_WIDTH
G_OFF = V_OFF + RET_WIDTH
C_OFF = G_OFF + RET_WIDTH
SPLITS = (Q_OFF, K_OFF, V_OFF, G_OFF, C_OFF)
ROPE_BASE = 10000.0
EPS = 1e-6
F32 = jnp.float32

kernel_name = 'hybrid_pool_retention_conv_dit'


def _rmsnorm(x, g):
    x32 = x.astype(F32)
    y = x32 * lax.rsqrt(jnp.mean(x32 * x32, axis=-1, keepdims=True) + EPS)
    return (y * g.astype(F32)).astype(x.dtype)


def _modulate(h, shift, scale):
    return h * (1 + scale) + shift


def _swiglu(h, w1, w3, w2):
    return (jax.nn.silu(h @ w1) * (h @ w3)) @ w2


def _multiscale_pool(p, pool_w, pool_scale):
    B, L, _ = p.shape
    p32 = p.astype(F32)
    cs = jnp.concatenate([jnp.zeros((B, 1, POOL_WIDTH), F32), jnp.cumsum(p32, axis=1)], axis=1)
    t = jnp.arange(L)
    outs = []
    for gi, w in enumerate(POOL_WINDOWS):
        lo = jnp.clip(t - w // 2, 0, L)
        hi = jnp.clip(t + w // 2, 0, L)
        csg = cs[:, :, gi * POOL_GROUP:(gi + 1) * POOL_GROUP]
        cnt = (hi - lo).astype(F32)[None, :, None]
        outs.append((csg[:, hi] - csg[:, lo]) / cnt - p32[:, :, gi * POOL_GROUP:(gi + 1) * POOL_GROUP])
    pooled = jnp.stack(outs, axis=2)
    mixed = jnp.einsum('blgc,gcd->blgd', pooled, pool_w.astype(F32)).reshape(B, L, POOL_WIDTH)
    return (mixed * pool_scale.astype(F32)).astype(p.dtype)


def _conv_module(u, dw, db, ln_g, ln_b):
    a, gt = jnp.split(u, 2, axis=-1)
    z = a * jax.nn.sigmoid(gt)
    z = lax.conv_general_dilated(z, dw[:, None, :].astype(z.dtype), window_strides=(1,),
                                 padding=[(CONV_K // 2, CONV_K // 2)],
                                 dimension_numbers=('NWC', 'WIO', 'NWC'),
                                 feature_group_count=CONV_WIDTH) + db
    z32 = z.astype(F32)
    mu = jnp.mean(z32, axis=-1, keepdims=True)
    var = jnp.mean(jnp.square(z32 - mu), axis=-1, keepdims=True)
    z32 = (z32 - mu) * lax.rsqrt(var + EPS) * ln_g.astype(F32) + ln_b.astype(F32)
    return jax.nn.silu(z32).astype(u.dtype)


def _heads(t):
    return t.reshape(t.shape[0], t.shape[1], RET_HEADS, RET_HEAD_DIM)


def _rotate(u, ang):
    u1, u2 = jnp.split(u, 2, axis=-1)
    cos = jnp.cos(ang)[None, :, None, :]
    sin = jnp.sin(ang)[None, :, None, :]
    return jnp.concatenate([u1 * cos - u2 * sin, u1 * sin + u2 * cos], axis=-1)


def _axial_rope(t, row, col):
    n_freq = RET_HEAD_DIM // 4
    inv = ROPE_BASE ** (-jnp.arange(n_freq, dtype=F32) / n_freq)
    tr, tc = jnp.split(t.astype(F32), 2, axis=-1)
    return jnp.concatenate([_rotate(tr, row[:, None] * inv[None]),
                            _rotate(tc, col[:, None] * inv[None])], axis=-1)


def _retention_dir(q, k, v, log_gamma, s0, strict):
    B, L, H, Dk = q.shape
    Dv = v.shape[-1]
    n = L // RET_CHUNK
    qc = q.astype(F32).reshape(B, n, RET_CHUNK, H, Dk)
    kc = k.astype(F32).reshape(B, n, RET_CHUNK, H, Dk)
    vc = v.astype(F32).reshape(B, n, RET_CHUNK, H, Dv)
    pos = jnp.arange(RET_CHUNK, dtype=F32)
    diff = pos[:, None] - pos[None, :]
    mask = (diff > 0) if strict else (diff >= 0)
    decay = jnp.where(mask[None], jnp.exp(jnp.maximum(diff, 0.0)[None] * log_gamma[:, None, None]), 0.0)
    scores = jnp.einsum('bnihd,bnjhd->bnhij', qc, kc) * decay[None, None]
    intra = jnp.einsum('bnhij,bnjhe->bnihe', scores, vc)
    w_k = jnp.exp((RET_CHUNK - 1 - pos)[:, None] * log_gamma[None, :])
    kv = jnp.einsum('bnjhd,jh,bnjhe->bnhde', kc, w_k, vc)
    chunk_decay = jnp.exp(RET_CHUNK * log_gamma)[None, :, None, None]

    def step(s, kv_n):
        return chunk_decay * s + kv_n, s

    s_final, s_starts = lax.scan(step, s0, jnp.moveaxis(kv, 1, 0))
    s_starts = jnp.moveaxis(s_starts, 0, 1)
    w_q = jnp.exp((pos + 1.0)[:, None] * log_gamma[None, :])
    cross = jnp.einsum('bnihd,ih,bnhde->bnihe', qc, w_q, s_starts)
    return (intra + cross).reshape(B, L, H, Dv), s_final


def _context_states(k, v, log_gamma_f, log_gamma_b):
    Lc = k.shape[1]
    pos = jnp.arange(Lc, dtype=F32)
    w_f = jnp.exp((Lc - 1 - pos)[:, None] * log_gamma_f[None, :])
    w_b = jnp.exp(pos[:, None] * log_gamma_b[None, :])
    k32 = k.astype(F32)
    v32 = v.astype(F32)
    s_f = jnp.einsum('bjhd,jh,bjhe->bhde', k32, w_f, v32)
    s_b = jnp.einsum('bjhd,jh,bjhe->bhde', k32, w_b, v32)
    return s_f, s_b


def _retention_readout(o, g, gn_g):
    B, L, H, Dv = o.shape
    mu = jnp.mean(o, axis=-1, keepdims=True)
    var = jnp.mean(jnp.square(o - mu), axis=-1, keepdims=True)
    y = ((o - mu) * lax.rsqrt(var + EPS)).reshape(B, L, H * Dv) * gn_g.astype(F32)
    return (y * jax.nn.silu(g.astype(F32))).astype(g.dtype)


def _mixer(hx, hy, w_in, w_out, pool_w, pool_scale, dec_f, dec_b, gn_g,
           conv_dw, conv_b, conv_ln_g, conv_ln_b, need_ctx_out):
    B, L, _ = hx.shape
    lg_f = jax.nn.log_sigmoid(dec_f.astype(F32))
    lg_b = jax.nn.log_sigmoid(dec_b.astype(F32))
    k_scale = RET_HEAD_DIM ** -0.5
    rows = L // GRID_W
    row = jnp.repeat(jnp.arange(rows, dtype=F32), GRID_W)
    col = jnp.tile(jnp.arange(GRID_W, dtype=F32), rows)

    if need_ctx_out:
        pool_y, qy, ky, vy, gy, conv_y = jnp.split(hy @ w_in, SPLITS, axis=-1)
        qy = _heads(qy)
        ky = _heads(ky) * k_scale
        vy = _heads(vy)
        s0 = jnp.zeros((hy.shape[0], RET_HEADS, RET_HEAD_DIM, RET_HEAD_DIM), F32)
        oy_f, s_f = _retention_dir(qy, ky, vy, lg_f, s0, False)
        oy_b, s_b = _retention_dir(jnp.flip(qy, 1), jnp.flip(ky, 1), jnp.flip(vy, 1), lg_b, s0, True)
        ret_y = _retention_readout(oy_f + jnp.flip(oy_b, 1), gy, gn_g)
        cat_y = jnp.concatenate([_multiscale_pool(pool_y, pool_w, pool_scale), ret_y,
                                 _conv_module(conv_y, conv_dw, conv_b, conv_ln_g, conv_ln_b)], axis=-1)
        out_y = cat_y @ w_out
    else:
        ky, vy = jnp.split(hy @ w_in[:, K_OFF:G_OFF], 2, axis=-1)
        s_f, s_b = _context_states(_heads(ky) * k_scale, _heads(vy), lg_f, lg_b)
        out_y = None

    pool_x, qx, kx, vx, gx, conv_x = jnp.split(hx @ w_in, SPLITS, axis=-1)
    qx = _axial_rope(_heads(qx), row, col)
    kx = _axial_rope(_heads(kx), row, col) * k_scale
    vx = _heads(vx)
    ox_f, _ = _retention_dir(qx, kx, vx, lg_f, s_f, False)
    ox_b, _ = _retention_dir(jnp.flip(qx, 1), jnp.flip(kx, 1), jnp.flip(vx, 1), lg_b, s_b, True)
    ret_x = _retention_readout(ox_f + jnp.flip(ox_b, 1), gx, gn_g)
    cat_x = jnp.concatenate([_multiscale_pool(pool_x, pool_w, pool_scale), ret_x,
                             _conv_module(conv_x, conv_dw, conv_b, conv_ln_g, conv_ln_b)], axis=-1)
    return cat_x @ w_out, out_y


def setup_inputs(seed: int = 0) -> dict:
    key = jax.random.key(seed)
    ks = jax.random.split(key, 24)

    def nrm(k, shape, scale):
        return jax.random.normal(k, shape, F32) * scale

    decay_base = jnp.log(2.0 ** (5.0 + jnp.arange(RET_HEADS, dtype=F32)) - 1.0)
    return {
        'x': nrm(ks[0], (BATCH, SEQ, D_MODEL), 1.0),
        'c': nrm(ks[1], (BATCH, D_MODEL), 1.0),
        'ctx': nrm(ks[2], (BATCH, CTX_LEN, D_MODEL), 1.0),
        'c_ctx': nrm(ks[3], (D_MODEL,), 1.0),
        'w_mod': nrm(ks[4], (DEPTH, D_MODEL, N_MOD * D_MODEL), 0.02),
        'b_mod': nrm(ks[5], (DEPTH, N_MOD * D_MODEL), 0.02),
        'norm_g': 1.0 + nrm(ks[6], (DEPTH, 3, D_MODEL), 0.02),
        'ffn_w1': nrm(ks[7], (DEPTH, 2, D_MODEL, D_FF), D_MODEL ** -0.5),
        'ffn_w3': nrm(ks[8], (DEPTH, 2, D_MODEL, D_FF), D_MODEL ** -0.5),
        'ffn_w2': nrm(ks[9], (DEPTH, 2, D_FF, D_MODEL), D_FF ** -0.5),
        'w_in': nrm(ks[10], (DEPTH, D_MODEL, IN_WIDTH), D_MODEL ** -0.5),
        'w_out': nrm(ks[11], (DEPTH, MIX_WIDTH, D_MODEL), MIX_WIDTH ** -0.5),
        'pool_w': nrm(ks[12], (DEPTH, POOL_GROUPS, POOL_GROUP, POOL_GROUP), POOL_GROUP ** -0.5),
        'pool_scale': 1.0 + nrm(ks[13], (DEPTH, POOL_WIDTH), 0.02),
        'ret_decay_fwd': decay_base + nrm(ks[14], (DEPTH, RET_HEADS), 0.1),
        'ret_decay_bwd': decay_base + nrm(ks[15], (DEPTH, RET_HEADS), 0.1),
        'ret_gn_g': 1.0 + nrm(ks[16], (DEPTH, RET_WIDTH), 0.02),
        'conv_dw': nrm(ks[17], (DEPTH, CONV_K, CONV_WIDTH), CONV_K ** -0.5),
        'conv_b': nrm(ks[18], (DEPTH, CONV_WIDTH), 0.02),
        'conv_ln_g': 1.0 + nrm(ks[19], (DEPTH, CONV_WIDTH), 0.02),
        'conv_ln_b': nrm(ks[20], (DEPTH, CONV_WIDTH), 0.02),
        'final_g': 1.0 + nrm(ks[21], (D_MODEL,), 0.02),
    }


def reference(x, c, ctx, c_ctx, w_mod, b_mod, norm_g, ffn_w1, ffn_w3, ffn_w2, w_in, w_out,
              pool_w, pool_scale, ret_decay_fwd, ret_decay_bwd, ret_gn_g, conv_dw, conv_b,
              conv_ln_g, conv_ln_b, final_g):
    y = ctx
    for l in range(DEPTH):
        last = l == DEPTH - 1
        mx = jnp.split((jax.nn.silu(c) @ w_mod[l] + b_mod[l])[:, None, :], N_MOD, axis=-1)
        my = jnp.split((jax.nn.silu(c_ctx) @ w_mod[l] + b_mod[l])[None, None, :], N_MOD, axis=-1)

        x = x + 0.5 * mx[2] * _swiglu(_modulate(_rmsnorm(x, norm_g[l, 0]), mx[0], mx[1]),
                                      ffn_w1[l, 0], ffn_w3[l, 0], ffn_w2[l, 0])
        y = y + 0.5 * my[2] * _swiglu(_modulate(_rmsnorm(y, norm_g[l, 0]), my[0], my[1]),
                                      ffn_w1[l, 0], ffn_w3[l, 0], ffn_w2[l, 0])

        hx = _modulate(_rmsnorm(x, norm_g[l, 1]), mx[3], mx[4])
        hy = _modulate(_rmsnorm(y, norm_g[l, 1]), my[3], my[4])
        ox, oy = _mixer(hx, hy, w_in[l], w_out[l], pool_w[l], pool_scale[l], ret_decay_fwd[l],
                        ret_decay_bwd[l], ret_gn_g[l], conv_dw[l], conv_b[l], conv_ln_g[l],
                        conv_ln_b[l], not last)
        x = x + mx[5] * ox
        if not last:
            y = y + my[5] * oy

        x = x + 0.5 * mx[8] * _swiglu(_modulate(_rmsnorm(x, norm_g[l, 2]), mx[6], mx[7]),
                                      ffn_w1[l, 1], ffn_w3[l, 1], ffn_w2[l, 1])
        if not last:
            y = y + 0.5 * my[8] * _swiglu(_modulate(_rmsnorm(y, norm_g[l, 2]), my[6], my[7]),
                                          ffn_w1[l, 1], ffn_w3[l, 1], ffn_w2[l, 1])
    return _rmsnorm(x, final_g)
```

```python
import numpy as np
import concourse.bass as bass
import concourse.mybir as mybir
from concourse.bass_utils import run_bass_kernel_spmd

F32 = mybir.dt.float32
BF16 = mybir.dt.bfloat16
ALU = mybir.AluOpType
AF = mybir.ActivationFunctionType

D = 1024
KC = 8
DFF = 2816
FC = 22
SEQ = 2048
CTX = 256
NL = 2
EPS = 1e-6
NCORES = 8


class Buf:
    __slots__ = ("name", "last_w", "readers")

    def __init__(self, name, inherit=None):
        self.name = name
        self.last_w = None
        self.readers = dict(inherit) if inherit else {}


class Tok:
    __slots__ = ("key", "ord", "clock", "op")

    def __init__(self, key, ord_, clock, op):
        self.key = key
        self.ord = ord_
        self.clock = clock
        self.op = op


class Op:
    __slots__ = ("eng", "fn", "waits", "tok", "signal", "semval", "dma_sem")


class Prog:
    ENGS = ("pe", "act", "dve", "pool", "sp")

    def __init__(self, nc):
        self.nc = nc
        self.h = {"pe": nc.tensor, "act": nc.scalar, "dve": nc.vector,
                  "pool": nc.gpsimd, "sp": nc.sync}
        self.ops = []
        self.nops = {e: 0 for e in self.ENGS}
        self.seen = {e: {} for e in self.ENGS}
        self.dma_count = {}
        self.sems = {}

    def op(self, eng, fn, reads=(), writes=(), dma=None):
        seen = self.seen[eng]
        deps = {}

        def add(t, raw):
            if t is None:
                return
            if t.key == ("e", eng) and eng in ("pe", "sp"):
                return
            k = t.key
            if k not in deps or deps[k].ord < t.ord:
                deps[k] = t

        for b in reads:
            add(b.last_w, True)
        for b in writes:
            add(b.last_w, True)
            for t in b.readers.values():
                add(t, False)
        o = Op()
        o.eng = eng
        o.fn = fn
        o.waits = []
        o.signal = False
        o.semval = None
        o.dma_sem = dma
        for k, t in deps.items():
            if seen.get(k, 0) >= t.ord:
                continue
            o.waits.append(t)
            if t.op is not None:
                t.op.signal = True
            seen[k] = t.ord
            for kk, vv in t.clock.items():
                if seen.get(kk, 0) < vv:
                    seen[kk] = vv
        if dma is None:
            self.nops[eng] += 1
            tok = Tok(("e", eng), self.nops[eng], dict(seen), o)
        else:
            self.dma_count[dma] = self.dma_count.get(dma, 0) + 16
            tok = Tok(("d", dma), self.dma_count[dma], dict(seen), None)
        o.tok = tok
        self.ops.append(o)
        for b in writes:
            b.last_w = tok
            b.readers = {}
        for b in reads:
            if b in writes:
                continue
            b.readers[tok.key] = tok
        return tok

    def _sem(self, name):
        if name not in self.sems:
            self.sems[name] = self.nc.alloc_semaphore(name)
        return self.sems[name]

    def emit(self):
        cnt = {e: 0 for e in self.ENGS}
        for o in self.ops:
            h = self.h[o.eng]
            for t in o.waits:
                if t.key[0] == "e":
                    h.wait_ge(self._sem("s_" + t.key[1]), t.op.semval)
                else:
                    h.wait_ge(self._sem("d_" + t.key[1]), t.ord)
            ins = o.fn(h)
            if o.dma_sem is not None:
                ins.then_inc(self._sem("d_" + o.dma_sem), 16)
            elif o.signal:
                cnt[o.eng] += 1
                o.semval = cnt[o.eng]
                ins.then_inc(self._sem("s_" + o.eng), 1)


def _cp_layout():
    off = {}
    c = 0

    def add(name, n):
        nonlocal c
        off[name] = (c, n)
        c += n

    add("ident", 128)
    add("perm", 128)
    add("maskf", 128)
    add("maskb", 128)
    add("pos1", 128)
    add("posr", 128)
    add("eps", 1)
    add("one", 1)
    add("edge", 4 * 16)
    add("final_g", 8)
    for l in range(NL):
        add(f"pool_scale{l}", 2)
        add(f"gn_g{l}", 4)
        add(f"conv_b{l}", 2)
        add(f"ln_g{l}", 2)
        add(f"ln_b{l}", 2)
        add(f"conv_dw{l}", 62)
        add(f"dec{l}", 8)
    return off, c


def _cp2_layout():
    off = {}
    c = 0
    for name, n in (("ones1024", 128), ("ones128", 128), ("ones256", 128), ("cvec", 24),
                    ("normg0", 72), ("bmod0", 216), ("normg1", 72), ("bmod1", 216)):
        off[name] = (c, n)
        c += n
    return off, c


CP_OFF, CP_N = _cp_layout()
CP2_OFF, CP2_N = _cp2_layout()
POOL_WINDOWS = (2, 4, 8, 16)


def _fm(v):
    return np.ascontiguousarray(np.asarray(v, np.float32).reshape(-1, 128).T)


def build_cp(inp, core):
    cp = np.zeros((128, CP_N), np.float32)
    cp2 = np.zeros((128, CP2_N), np.float32)

    def put(name, arr):
        if name in CP2_OFF:
            o, n = CP2_OFF[name]
            cp2[:, o:o + n] = np.asarray(arr, np.float32).reshape(128, n)
            return
        o, n = CP_OFF[name]
        arr = np.asarray(arr, np.float32).reshape(128, n)
        cp[:, o:o + n] = arr

    put("ident", np.eye(128, dtype=np.float32))
    perm = np.zeros((128, 128), np.float32)
    for m in range(128):
        partner = m + 32 if (m % 64) < 32 else m - 32
        perm[partner, m] = 1.0
    put("perm", perm)
    put("ones1024", np.full((128, 128), 1.0 / 1024, np.float32))
    put("ones128", np.full((128, 128), 1.0 / 128, np.float32))
    put("ones256", np.full((128, 128), 1.0 / 256, np.float32))
    jj = np.arange(128)[:, None]
    ii = np.arange(128)[None, :]
    put("maskf", (ii >= jj).astype(np.float32))
    put("maskb", (jj > ii).astype(np.float32))
    put("pos1", np.tile(np.arange(1, 129, dtype=np.float32)[None, :], (128, 1)))
    put("posr", np.tile((128 - np.arange(128, dtype=np.float32))[None, :], (128, 1)))
    put("eps", np.full((128, 1), EPS, np.float32))
    put("one", np.ones((128, 1), np.float32))
    edge = np.zeros((4, 16), np.float32)
    for gi, w in enumerate(POOL_WINDOWS):
        for t in range(8):
            cl = min(t + w // 2, w) if True else w
            cl = (t + w // 2) - max(t - w // 2, 0)
            edge[gi, t] = w / cl
            d = 8 - t
            cr = min(d, w // 2) + w // 2
            edge[gi, 8 + t] = w / cr
    put("edge", np.tile(edge.reshape(1, 64), (128, 1)))
    b0 = 2 * core
    cv = np.stack([np.asarray(inp["c_ctx"], np.float32),
                   np.asarray(inp["c"][b0], np.float32),
                   np.asarray(inp["c"][b0 + 1], np.float32)], axis=0)
    put("cvec", cv.reshape(3, 8, 128).transpose(2, 1, 0).reshape(128, 24))
    put("final_g", _fm(inp["final_g"]))
    for l in range(NL):
        ng = np.asarray(inp["norm_g"][l], np.float32).reshape(3, 8, 128).transpose(2, 0, 1)
        put(f"normg{l}", np.repeat(ng[:, :, :, None], 3, axis=3).reshape(128, 72))
        bm = _fm(inp["b_mod"][l])
        put(f"bmod{l}", np.repeat(bm[:, :, None], 3, axis=2).reshape(128, 216))
        put(f"pool_scale{l}", _fm(inp["pool_scale"][l]))
        put(f"gn_g{l}", _fm(inp["ret_gn_g"][l]))
        put(f"conv_b{l}", _fm(inp["conv_b"][l]))
        put(f"ln_g{l}", _fm(inp["conv_ln_g"][l]))
        put(f"ln_b{l}", _fm(inp["conv_ln_b"][l]))
        dw = np.asarray(inp["conv_dw"][l], np.float32)
        put(f"conv_dw{l}", dw.reshape(31, 2, 128).transpose(2, 1, 0).reshape(128, 62))
        dec = np.concatenate([np.asarray(inp["ret_decay_fwd"][l], np.float32),
                              np.asarray(inp["ret_decay_bwd"][l], np.float32)])
        put(f"dec{l}", np.tile(dec[None, :], (128, 1)))
    return cp, cp2


def build_poolw(inp):
    out = np.zeros((128, NL, 2, 128), np.float32)
    for l in range(NL):
        pw = np.asarray(inp["pool_w"][l], np.float32)
        for ch in range(2):
            out[0:64, l, ch, 0:64] = pw[2 * ch]
            out[64:128, l, ch, 64:128] = pw[2 * ch + 1]
    return out.reshape(128, NL * 256)


def build_rope():
    n_freq = 32
    inv = (10000.0 ** (-np.arange(n_freq, dtype=np.float32) / n_freq)).astype(np.float32)
    t = np.arange(SEQ)
    row = (t // 64).astype(np.float32)
    col = (t % 64).astype(np.float32)
    tab = np.zeros((128, 2, SEQ), np.float32)
    for f in range(128):
        pos = row if f < 64 else col
        ang = (pos * inv[f % 32]).astype(np.float32)
        tab[f, 0] = np.cos(ang)
        s = np.sin(ang)
        tab[f, 1] = -s if (f % 64) < 32 else s
    return tab


def relayout_weights(inp):
    w = {}
    w1 = np.asarray(inp["ffn_w1"], np.float32).reshape(4, 8, 128, 11, 256)
    w["w1"] = np.ascontiguousarray(w1.transpose(0, 3, 2, 1, 4)).reshape(44, 128, 2048)
    w3 = np.asarray(inp["ffn_w3"], np.float32).reshape(4, 8, 128, 11, 256)
    w["w3"] = np.ascontiguousarray(w3.transpose(0, 3, 2, 1, 4)).reshape(44, 128, 2048)
    w2 = np.asarray(inp["ffn_w2"], np.float32).reshape(4, 22, 128, 8, 128)
    w["w2"] = np.ascontiguousarray(w2.transpose(0, 3, 2, 1, 4)).reshape(32, 128, 2816)
    wi = np.asarray(inp["w_in"], np.float32).reshape(2, 8, 128, 22, 128)
    w["win"] = np.ascontiguousarray(wi.transpose(0, 3, 2, 1, 4)).reshape(44, 128, 1024)
    w["wout"] = np.ascontiguousarray(np.asarray(inp["w_out"], np.float32).reshape(16, 128, 1024))
    wm = np.asarray(inp["w_mod"], np.float32).reshape(2, 8, 128, 18, 512)
    w["wmod"] = np.ascontiguousarray(wm.transpose(0, 3, 2, 1, 4)).reshape(36, 128, 4096)
    return w


class K:
    def __init__(self, dbg_stop=None):
        self.dbg_stop = dbg_stop
        nc = bass.Bass("TRN2", target_bir_lowering=False)
        self.nc = nc
        self.P = Prog(nc)
        dt = nc.dram_tensor
        self.x_d = dt("x", [2, SEQ, D], F32, kind="ExternalInput").ap()
        self.ctx_d = dt("ctx", [2 * CTX, D], F32, kind="ExternalInput").ap()
        self.cp_d = dt("cp", [128, CP_N], F32, kind="ExternalInput").ap()
        self.cp2_d = dt("cp2", [128, CP2_N], F32, kind="ExternalInput").ap()
        self.rope_d = dt("rope", [128, 2, SEQ], F32, kind="ExternalInput").ap()
        self.poolw_d = dt("poolw", [128, NL * 256], F32, kind="ExternalInput").ap()
        self.w1_d = dt("w1", [44, 128, 2048], F32, kind="ExternalInput").ap()
        self.w3_d = dt("w3", [44, 128, 2048], F32, kind="ExternalInput").ap()
        self.w2_d = dt("w2", [32, 128, 2816], F32, kind="ExternalInput").ap()
        self.win_d = dt("win", [44, 128, 1024], F32, kind="ExternalInput").ap()
        self.wout_d = dt("wout", [16, 128, 1024], F32, kind="ExternalInput").ap()
        self.wmod_d = dt("wmod", [36, 128, 4096], F32, kind="ExternalInput").ap()
        self.out_d = dt("out", [2, SEQ, D], F32, kind="ExternalOutput").ap()
        if dbg_stop is not None:
            self.dbgy_d = dt("dbgy", [2 * CTX, D], F32, kind="ExternalOutput").ap()

        A = nc.alloc_sbuf_tensor
        self.xT = A("xT", [128, KC * SEQ], F32)
        self.cp = A("cp_s", [128, CP_N], F32)
        self.mod = A("mod", [128, NL * 216], F32)
        self.gs = A("gs", [128, NL * 72], F32)
        self.hg = A("hg", [128, NL * 72], F32)
        self.sc = A("sc", [128, 24], F32)
        self.sc_b = A("sc_b", [128, 24], BF16)
        self.lg = A("lg", [128, NL * 8], F32)
        self.nlg = A("nlg", [128, NL * 8], F32)
        self.g128 = A("g128", [128, NL * 8], F32)
        self.ones_b = A("ones_b", [128, 128], BF16)
        self.ident_b = A("ident_b", [128, 128], BF16)
        self.ones128_b = A("ones128_b", [128, 128], BF16)
        self.ones256_b = A("ones256_b", [128, 128], BF16)
        self.poolw_b = A("poolw_b", [128, NL * 256], BF16)
        self.states = A("states", [128, NL * 2 * 2 * 4 * 128], BF16)
        self.sgall = A("sgall", [128, SEQ], BF16)
        self.hprev = A("hprev", [128, SEQ], BF16)
        self.NTMP = 7
        self.tmp = [A(f"tmp{i}", [128, 512], F32) for i in range(self.NTMP)]
        self.sqb = [A(f"sqb{i}", [128, 512], BF16) for i in range(2)]
        self.rstd = [A(f"rstd{i}", [128, 512], F32) for i in range(2)]
        self.rstdb = [Buf(f"rstd{i}") for i in range(2)]
        self.rstd_i = 0
        self.ropes = [A(f"rope{i}", [128, 2 * 512], F32) for i in range(2)]
        self.SCR_BYTES = 89088 + 2048
        self.scr = A("scr", [128, self.SCR_BYTES // 4], F32)
        self.ps = [nc.alloc_psum_tensor(f"ps{i}", [128, 512], F32) for i in range(8)]
        self.psb = [Buf(f"ps{i}") for i in range(8)]
        self.bank_i = 0
        self.bank_set = [0, 1, 2, 3, 4, 5]
        self.tmp_i = 0
        self.tmpb = [Buf(f"tmp{i}") for i in range(self.NTMP)]
        self.sqb_i = 0
        self.sqbb = [Buf(f"sqb{i}") for i in range(2)]
        self.bufs = {}
        self.scr_bufs = {}
        self.inherit = {}
        self.slot_ctr = {}

    def b(self, *key):
        if key not in self.bufs:
            self.bufs[key] = Buf(str(key))
        return self.bufs[key]

    def sb(self, *key):
        if key not in self.scr_bufs:
            self.scr_bufs[key] = Buf(str(key), self.inherit)
        return self.scr_bufs[key]

    def new_phase(self, keep=()):
        keepd = {}
        for key, bf in self.scr_bufs.items():
            if key[0] in keep:
                keepd[key] = bf
                continue
            toks = list(bf.readers.values())
            if bf.last_w is not None:
                toks.append(bf.last_w)
            for t in toks:
                if t.key not in self.inherit or self.inherit[t.key].ord < t.ord:
                    self.inherit[t.key] = t
        self.scr_bufs = keepd

    def scr_f32(self, off_bytes, n):
        assert off_bytes % 4 == 0 and off_bytes + 4 * n <= self.SCR_BYTES, (off_bytes, n)
        return self.scr[:, off_bytes // 4: off_bytes // 4 + n]

    def scr_bf(self, off_bytes, n):
        assert off_bytes % 4 == 0 and n % 2 == 0 and off_bytes + 2 * n <= self.SCR_BYTES, (off_bytes, n)
        return self.scr[:, off_bytes // 4: off_bytes // 4 + n // 2].bitcast(BF16)

    def bank(self):
        bs = self.bank_set
        self.bank_i = (self.bank_i + 1) % len(bs)
        return bs[self.bank_i]

    def gettmp(self):
        i = self.tmp_i
        self.tmp_i = (i + 1) % self.NTMP
        return self.tmp[i], self.tmpb[i]

    def getsq(self):
        i = self.sqb_i
        self.sqb_i = (i + 1) % 2
        return self.sqb[i], self.sqbb[i]

    def cpc(self, name, a=0, n=None):
        o, nn = CP_OFF[name]
        if n is None:
            n = nn - a
        return self.cp[:, o + a: o + a + n]

    def mm(self, out, lhsT, rhs, start, stop, R, W):
        self.P.op("pe", lambda h: h.matmul(out, lhsT, rhs, start=start, stop=stop), reads=R, writes=W)

    def tr(self, out, in_, ident, R, W):
        self.P.op("pe", lambda h: h.transpose(out, in_, ident), reads=R, writes=W)

    def act(self, out, in_, func, R, W, scale=1.0, bias=0.0):
        self.P.op("act", lambda h: h.activation(out=out, in_=in_, func=func, bias=bias, scale=scale),
                  reads=R, writes=W)

    def tt(self, out, in0, in1, op, R, W, eng="dve"):
        self.P.op(eng, lambda h: h.tensor_tensor(out=out, in0=in0, in1=in1, op=op), reads=R, writes=W)

    def stt(self, out, in0, scalar, in1, op0, op1, R, W, eng="dve"):
        self.P.op(eng, lambda h: h.scalar_tensor_tensor(out=out, in0=in0, scalar=scalar, in1=in1,
                                                        op0=op0, op1=op1), reads=R, writes=W)

    def ts(self, out, in0, s1, s2, op0, op1, R, W, eng="dve"):
        if s2 is None:
            self.P.op(eng, lambda h: h.tensor_scalar(out=out, in0=in0, scalar1=s1, scalar2=None, op0=op0),
                      reads=R, writes=W)
        else:
            self.P.op(eng, lambda h: h.tensor_scalar(out=out, in0=in0, scalar1=s1, scalar2=s2, op0=op0, op1=op1),
                      reads=R, writes=W)

    def cpy(self, out, in_, R, W, eng="dve"):
        self.P.op(eng, lambda h: h.tensor_copy(out=out, in_=in_), reads=R, writes=W)

    def rsqrt(self, out, in_, R, wb, clamp=False):
        if clamp:
            self.ts(out, in_, 0.0, EPS, ALU.max, ALU.add, list(R), [wb])
            self.act(out, out, AF.Ln, [wb], [wb])
        else:
            self.act(out, in_, AF.Ln, list(R) + [self.b("cp")], [wb], bias=self.cpc("eps"))
        self.act(out, out, AF.Exp, [wb], [wb], scale=-0.5)

    def recip(self, out, in_, R, W):
        self.P.op("dve", lambda h: h.reciprocal(out=out, in_=in_), reads=R, writes=W)

    def memset(self, ap, val, W, eng="dve"):
        self.P.op(eng, lambda h: h.memset(ap, val), writes=W)

    def dma(self, q, out, in_, sem, R, W):
        self.P.op(q, lambda h: h.dma_start(out=out, in_=in_), reads=R, writes=W, dma=sem)

    def xv(self, k, t0, n):
        return self.xT[:, k * SEQ + t0: k * SEQ + t0 + n]

    def xb(self, k, blk):
        return self.b("x", k, blk)

    def modcol(self, l, j, k, grp):
        c = l * 216 + (j * 8 + k) * 3 + grp
        return self.mod[:, c:c + 1]

    def gscol(self, l, jn, k, grp):
        c = l * 72 + (jn * 8 + k) * 3 + grp
        return self.gs[:, c:c + 1]

    def hgcol(self, l, jn, k, grp):
        c = l * 72 + (jn * 8 + k) * 3 + grp
        return self.hg[:, c:c + 1]

    def prologue(self):
        P = self.P
        bcp = self.b("cp")
        self.dma("sp", self.cp[:], self.cp_d, "cp", [], [bcp])
        cp2 = self.scr_f32(32 * 1024, CP2_N)
        bcp2 = self.sb("cp2")
        self.dma("sp", cp2, self.cp2_d, "cp2", [], [bcp2])

        def c2(name, a=0, n=None):
            o, nn = CP2_OFF[name]
            if n is None:
                n = nn - a
            return cp2[:, o + a: o + a + n]
        self.cpy(self.ones_b[:], c2("ones1024"), [bcp2], [self.b("ones_b")])
        self.cpy(self.ident_b[:], self.cpc("ident"), [bcp], [self.b("ident_b")])
        self.cpy(self.ones128_b[:], c2("ones128"), [bcp2], [self.b("ones_b")])
        self.cpy(self.ones256_b[:], c2("ones256"), [bcp2], [self.b("ones_b")])
        self.dma("pool", self.poolw_b[:], self.poolw_d, "poolw", [], [self.b("poolw_b")])
        bsc = self.b("sc")
        self.act(self.sc[:], c2("cvec"), AF.Silu, [bcp2], [bsc])
        self.cpy(self.sc_b[:], self.sc[:], [bsc], [bsc])
        blg = self.b("lg")
        for l in range(NL):
            sl = slice(l * 8, (l + 1) * 8)
            self.act(self.nlg[:, sl], self.cpc(f"dec{l}"), AF.Exp, [bcp], [blg], scale=-1.0)
            self.act(self.nlg[:, sl], self.nlg[:, sl], AF.Ln, [blg, bcp], [blg], bias=self.cpc("one"))
            self.ts(self.lg[:, sl], self.nlg[:, sl], -1.0, None, ALU.mult, None, [blg], [blg])
            self.act(self.g128[:, sl], self.lg[:, sl], AF.Exp, [blg], [blg], scale=128.0)
        for l in range(NL):
            bank = 6 + l
            for cg in range(18):
                s = (l * 18 + cg) % 4
                wt = self.scr_bf(s * 8192, 4096)
                wb = self.sb("wm", s)
                self.dma("pool", wt, self.wmod_d[l * 18 + cg], f"wm{s}", [], [wb])
                for jj in range(4):
                    j = cg * 4 + jj
                    for k in range(KC):
                        self.mm(self.ps[bank][:, j * 3:(j + 1) * 3],
                                wt[:, k * 512 + jj * 128: k * 512 + (jj + 1) * 128],
                                self.sc_b[:, k * 3:(k + 1) * 3], k == 0, k == KC - 1,
                                [wb, bsc], [self.psb[bank]])
            bm = self.b("mod", l)
            self.tt(self.mod[:, l * 216:(l + 1) * 216], self.ps[bank][:, 0:216], c2(f"bmod{l}"),
                    ALU.add, [self.psb[bank], bcp2], [bm])
            for jn in range(3):
                j = 3 * jn + 1
                self.stt(self.gs[:, l * 72 + jn * 24: l * 72 + (jn + 1) * 24],
                         self.mod[:, l * 216 + j * 24: l * 216 + (j + 1) * 24], 1.0,
                         c2(f"normg{l}", jn * 24, 24), ALU.add, ALU.mult, [bm, bcp2], [self.b("gs", l)])
                jg = 3 * jn + 2
                self.ts(self.hg[:, l * 72 + jn * 24: l * 72 + (jn + 1) * 24],
                        self.mod[:, l * 216 + jg * 24: l * 216 + (jg + 1) * 24],
                        1.0 if jn == 1 else 0.5, None, ALU.mult, None, [bm], [self.b("hg", l)])
        self.new_phase()

    def load_seg(self, src, ntok):
        bident = self.b("cp")
        for tt_ in range(ntok // 128):
            s = tt_ % 8
            st = self.scr_f32(s * 4096, 1024)
            stb = self.sb("xin", s)
            self.dma("sp", st, src[tt_ * 128:(tt_ + 1) * 128, :], f"xin{s}", [], [stb])
            blk = tt_ // 4
            for half in range(2):
                bk = self.bank()
                for kk in range(4):
                    k = half * 4 + kk
                    self.tr(self.ps[bk][:, kk * 128:(kk + 1) * 128], st[:, k * 128:(k + 1) * 128],
                            self.cpc("ident"), [stb, bident], [self.psb[bk]])
                out = self.xT[:, half * 4 * SEQ:(half * 4 + 4) * SEQ].rearrange("p (k t) -> p k t", k=4)[
                    :, :, tt_ * 128:(tt_ + 1) * 128]
                in_ = self.ps[bk][:, :].rearrange("p (k t) -> p k t", k=4)
                eng = "dve" if half == 0 else "act"
                W = [self.xb(half * 4 + kk, blk) for kk in range(4)]
                if eng == "dve":
                    self.cpy(out, in_, [self.psb[bk]], W)
                else:
                    self.act(out, in_, AF.Identity, [self.psb[bk]], W)
        self.new_phase()

    def store_seg(self, dst, ntok):
        bident = self.b("cp")
        for tt_ in range(ntok // 128):
            s = tt_ % 8
            st = self.scr_f32(s * 4096, 1024)
            stb = self.sb("xout", s)
            blk = tt_ // 4
            for half in range(2):
                bk = self.bank()
                for kk in range(4):
                    k = half * 4 + kk
                    self.tr(self.ps[bk][:, kk * 128:(kk + 1) * 128], self.xv(k, tt_ * 128, 128),
                            self.cpc("ident"), [self.xb(k, blk), bident], [self.psb[bk]])
                if half == 0:
                    self.cpy(st[:, 0:512], self.ps[bk][:, :], [self.psb[bk]], [stb])
                else:
                    self.act(st[:, 512:1024], self.ps[bk][:, :], AF.Identity, [self.psb[bk]], [stb])
            self.dma("sp", dst[tt_ * 128:(tt_ + 1) * 128, :], st, f"xout{s}", [stb], [self.b("outd", s)])
        self.new_phase()

    def rms_rstd(self, blk):
        bk = 6 + (blk % 2)
        for k in range(KC):
            sq, sqb = self.getsq()
            self.act(sq[:], self.xv(k, blk * 512, 512), AF.Square, [self.xb(k, blk)], [sqb])
            self.mm(self.ps[bk][:], self.ones_b[:], sq[:], k == 0, k == KC - 1,
                    [sqb, self.b("ones_b")], [self.psb[bk]])
        ri = self.rstd_i
        self.rstd_i = 1 - ri
        rs, rsb = self.rstd[ri], self.rstdb[ri]
        self.rsqrt(rs[:], self.ps[bk][:], [self.psb[bk]], rsb)
        return rs, rsb

    def norm_mod(self, blk, l, jn, grp, hT, hoff, hbuf, rs=None):
        rs, rsb = rs if rs is not None else self.rms_rstd(blk)
        for k in range(KC):
            t, tb = self.gettmp()
            self.tt(t[:], self.xv(k, blk * 512, 512), rs[:], ALU.mult, [self.xb(k, blk), rsb], [tb])
            self.act(hT(k, hoff), t[:], AF.Identity, [tb, self.b("gs", l), self.b("mod", l)], [hbuf(k)],
                     scale=self.gscol(l, jn, k, grp), bias=self.modcol(l, 3 * jn, k, grp))

    def ffn(self, l, f, nblk, grp):
        jn = 0 if f == 0 else 2
        lf = l * 2 + f
        HT = 0
        GT = 16 * 1024
        W1 = 60 * 1024
        W3 = 68 * 1024
        W2 = 76 * 1024
        for g0 in range(0, nblk, 2):
            blks = list(range(g0, min(g0 + 2, nblk)))
            rss = [self.rms_rstd(blk) for blk in blks]
            for bi, blk in enumerate(blks):
                self.norm_mod(blk, l, jn, grp,
                              lambda k, off: self.scr_bf(HT + k * 2048 + off * 2, 512), bi * 512,
                              lambda k, bi=bi: self.sb("h", k, bi), rs=rss[bi])
            for cg in range(11):
                s = self.slot("w13")
                w1t = self.scr_bf(W1 + s * 4096, 2048)
                w3t = self.scr_bf(W3 + s * 4096, 2048)
                self.dma("pool", w1t, self.w1_d[lf * 11 + cg], f"w1{s}", [], [self.sb("w1", s)])
                self.dma("pool", w3t, self.w3_d[lf * 11 + cg], f"w3{s}", [], [self.sb("w3", s)])
                for bi, blk in enumerate(blks):
                    for m in range(2):
                        pa = self.bank()
                        pb = self.bank()
                        for k in range(KC):
                            self.mm(self.ps[pa][:], w1t[:, k * 256 + m * 128: k * 256 + (m + 1) * 128],
                                    self.scr_bf(HT + k * 2048 + bi * 1024, 512), k == 0, k == KC - 1,
                                    [self.sb("w1", s), self.sb("h", k, bi)], [self.psb[pa]])
                        for k in range(KC):
                            self.mm(self.ps[pb][:], w3t[:, k * 256 + m * 128: k * 256 + (m + 1) * 128],
                                    self.scr_bf(HT + k * 2048 + bi * 1024, 512), k == 0, k == KC - 1,
                                    [self.sb("w3", s), self.sb("h", k, bi)], [self.psb[pb]])
                        t, tb = self.gettmp()
                        self.act(t[:], self.ps[pa][:], AF.Silu, [self.psb[pa]], [tb])
                        j = cg * 2 + m
                        self.tt(self.scr_bf(GT + j * 2048 + bi * 1024, 512), t[:], self.ps[pb][:], ALU.mult,
                                [tb, self.psb[pb]], [self.sb("g", j, bi)])
            for dg in range(KC):
                s = self.slot("w2")
                w2t = self.scr_bf(W2 + s * 5632, 2816)
                self.dma("pool", w2t, self.w2_d[lf * 8 + dg], f"w2{s}", [], [self.sb("w2", s)])
                for bi, blk in enumerate(blks):
                    pc = self.bank()
                    for j in range(FC):
                        self.mm(self.ps[pc][:], w2t[:, j * 128:(j + 1) * 128],
                                self.scr_bf(GT + j * 2048 + bi * 1024, 512), j == 0, j == FC - 1,
                                [self.sb("w2", s), self.sb("g", j, bi)], [self.psb[pc]])
                    xs = self.xv(dg, blk * 512, 512)
                    self.stt(xs, self.ps[pc][:], self.hgcol(l, jn, dg, grp), xs, ALU.mult, ALU.add,
                             [self.psb[pc], self.b("hg", l), self.xb(dg, blk)], [self.xb(dg, blk)])
        self.new_phase()

    def slot(self, name):
        v = self.slot_ctr.get(name, 0)
        self.slot_ctr[name] = v + 1
        return v % 2

    def mixer(self, l, nblk, grp, seqs, rope, full, bidx):
        T = nblk * 512
        HT = 0
        PH = 32 * 1024
        WI = 77 * 1024
        hT = lambda k, off: self.scr_bf(HT + k * (2 * T) + off * 2, 512)
        for blk in range(nblk):
            self.norm_mod(blk, l, 1, grp, hT, blk * 512, lambda k, blk=blk: self.sb("h", k, blk))
        KEEP = ("h", "wi")

        def load_wi(slot, chunk):
            wt = self.scr_bf(WI + slot * 2048, 1024)
            self.dma("pool", wt, self.win_d[l * 22 + chunk], f"wi{slot}", [], [self.sb("wi", slot)])
            return wt

        def load_wo(chunk, slot=4):
            wt = self.scr_bf(WI + slot * 2048, 1024)
            self.dma("pool", wt, self.wout_d[l * 8 + chunk], f"wi{slot}", [], [self.sb("wi", slot)])
            return wt

        def proj_fm(wt, slot, blk, bk):
            for k in range(KC):
                self.mm(self.ps[bk][:], wt[:, k * 128:(k + 1) * 128], hT(k, blk * 512), k == 0, k == KC - 1,
                        [self.sb("wi", slot), self.sb("h", k, blk)], [self.psb[bk]])

        def wout_partial(terms, blk):
            for dg in range(KC):
                pc = self.bank()
                for ti, (wo, wslot, cat_ap, catb) in enumerate(terms):
                    self.mm(self.ps[pc][:], wo[:, dg * 128:(dg + 1) * 128], cat_ap, ti == 0, ti == len(terms) - 1,
                            [self.sb("wi", wslot), catb], [self.psb[pc]])
                xs = self.xv(dg, blk * 512, 512)
                self.stt(xs, self.ps[pc][:], self.hgcol(l, 1, dg, grp), xs, ALU.mult, ALU.add,
                         [self.psb[pc], self.b("hg", l), self.xb(dg, blk)], [self.xb(dg, blk)])

        bcp = self.b("cp")
        if full:
            for ch in range(2):
                self.new_phase(KEEP + ("pcat",))
                PADL = 8
                LP = T + 16 * len(seqs)
                pp = self.scr_f32(PH, LP)
                ta = self.scr_f32(PH + 4 * LP, LP)
                tb_ = self.scr_f32(PH + 8 * LP, LP)
                pooled = self.scr_bf(PH + 12 * LP, T)
                CATP = PH + 45 * 1024 - 4 * T
                assert PH + 12 * LP + 2 * T <= CATP
                pcat = [self.scr_bf(CATP + c_ * 2 * T, T) for c_ in range(2)]
                cat = pcat[ch]
                bpp, bta, btb = self.sb("pp"), self.sb("ta"), self.sb("tb")
                for si_, (c0_, ncn_, _) in enumerate(seqs):
                    b0_ = c0_ * 128 + 16 * si_
                    L_ = ncn_ * 128
                    self.memset(pp[:, b0_: b0_ + 8], 0.0, [bpp])
                    self.memset(pp[:, b0_ + 8 + L_: b0_ + 16 + L_], 0.0, [bpp])
                    self.memset(ta[:, b0_: b0_ + 1], 0.0, [bta])
                    self.memset(tb_[:, b0_: b0_ + 1], 0.0, [btb])
                    self.memset(tb_[:, b0_ + L_ + 15: b0_ + L_ + 16], 0.0, [btb])
                wt = load_wi(0, ch)
                wo_p = load_wo(ch, 4 + ch)
                if ch == 0:
                    wo_p0 = wo_p

                def ppos(tok):
                    si = 0
                    for i, (c0, ncn, _) in enumerate(seqs):
                        if tok >= c0 * 128:
                            si = i
                    return tok + 16 * si + PADL

                for blk in range(nblk):
                    bk = self.bank()
                    proj_fm(wt, 0, blk, bk)
                    for piece in range(2):
                        t0 = blk * 512 + piece * 256
                        p0 = ppos(t0)
                        self.act(pp[:, p0:p0 + 256], self.ps[bk][:, piece * 256:(piece + 1) * 256], AF.Identity,
                                 [self.psb[bk]], [bpp])
                for (c0, ncn, _) in seqs:
                    L = ncn * 128
                    base = ppos(c0 * 128) - PADL
                    LL = L + 16
                    for half in range(2):
                        gi = ch * 2 + half
                        w = POOL_WINDOWS[gi]
                        ps_ = slice(half * 64, half * 64 + 64)
                        cur, curb = ta, bta
                        self.tt(ta[ps_, base + 1: base + LL], pp[ps_, base: base + LL - 1], pp[ps_, base + 1: base + LL],
                                ALU.add, [bpp], [bta])
                        other, otherb = tb_, btb
                        sh = 1
                        ww = 2
                        while ww < w:
                            self.tt(other[ps_, base + sh: base + LL - sh], cur[ps_, base: base + LL - 2 * sh],
                                    cur[ps_, base + 2 * sh: base + LL], ALU.add, [curb], [otherb])
                            cur, curb, other, otherb = other, otherb, cur, curb
                            sh *= 2
                            ww *= 2
                        e0 = gi * 16
                        self.tt(cur[ps_, base + PADL: base + PADL + 8], cur[ps_, base + PADL: base + PADL + 8],
                                self.cpc("edge", e0, 8)[ps_, :], ALU.mult, [curb, bcp], [curb])
                        self.tt(cur[ps_, base + PADL + L - 8: base + PADL + L], cur[ps_, base + PADL + L - 8: base + PADL + L],
                                self.cpc("edge", e0 + 8, 8)[ps_, :], ALU.mult, [curb, bcp], [curb])
                        self.stt(pooled[ps_, c0 * 128: c0 * 128 + L], cur[ps_, base + PADL: base + PADL + L], 1.0 / w,
                                 pp[ps_, base + PADL: base + PADL + L], ALU.mult, ALU.subtract,
                                 [curb, bpp], [self.sb("pooled")])
                for blk in range(nblk):
                    bk = self.bank()
                    self.mm(self.ps[bk][:], self.poolw_b[:, l * 256 + ch * 128: l * 256 + (ch + 1) * 128],
                            pooled[:, blk * 512:(blk + 1) * 512], True, True,
                            [self.b("poolw_b"), self.sb("pooled")], [self.psb[bk]])
                    self.act(cat[:, blk * 512:(blk + 1) * 512], self.ps[bk][:], AF.Identity, [self.psb[bk], bcp],
                             [self.sb("pcat", ch, blk)], scale=self.cpc(f"pool_scale{l}", ch, 1))
                    if ch == 1:
                        wout_partial([(wo_p0, 4, pcat[0][:, blk * 512:(blk + 1) * 512], self.sb("pcat", 0, blk)),
                                      (wo_p, 5, pcat[1][:, blk * 512:(blk + 1) * 512], self.sb("pcat", 1, blk))], blk)
            self.new_phase(KEEP)
            nsq = len(seqs)
            ZP = T + 30 * nsq
            if (2 * ZP) % 4:
                ZP += 1
            ZB = PH
            DG = ZB + 4 * ZP
            AC = DG + 2 * 31 * 256
            AB = AC + 2 * 2 * 2048
            CC = AB + 4 * 1024
            assert CC + 4 * T <= WI, (CC + 4 * T, WI)
            zb_ = [self.scr_bf(ZB + c * 2 * ZP, ZP) for c in range(2)]
            cat = [self.scr_bf(CC + c * 2 * T, T) for c in range(2)]

            def zpos(tok):
                si = 0
                for i, (c0, ncn, _) in enumerate(seqs):
                    if tok >= c0 * 128:
                        si = i
                return tok + 30 * si + 15

            for c in range(2):
                self.memset(zb_[c], 0.0, [self.sb("z", c)])
                for kk in range(31):
                    dgm = self.scr_bf(DG + (c * 31 + kk) * 256, 128)
                    self.ts(dgm, self.ident_b[:], self.cpc(f"conv_dw{l}", c * 31 + kk, 1), None, ALU.mult, None,
                            [self.b("ident_b"), bcp], [self.sb("dg", c)])
                wa = load_wi(0, 18 + c)
                wg = load_wi(1, 20 + c)
                for blk in range(nblk):
                    ba = self.bank()
                    bg = self.bank()
                    proj_fm(wa, 0, blk, ba)
                    proj_fm(wg, 1, blk, bg)
                    t, tb = self.gettmp()
                    self.act(t[:], self.ps[bg][:], AF.Sigmoid, [self.psb[bg]], [tb])
                    for piece in range(2):
                        p0 = zpos(blk * 512 + piece * 256)
                        self.tt(zb_[c][:, p0:p0 + 256], self.ps[ba][:, piece * 256:(piece + 1) * 256],
                                t[:, piece * 256:(piece + 1) * 256], ALU.mult, [self.psb[ba], tb], [self.sb("z", c)])
            for blk in range(nblk):
                par = blk % 2
                accs = []
                for c in range(2):
                    bk = self.bank()
                    if nsq == 1:
                        pieces = [(0, 512, zpos(blk * 512) - 15)]
                    else:
                        pieces = [(0, 256, zpos(blk * 512) - 15), (256, 256, zpos(blk * 512 + 256) - 15)]
                    for (co, n, zs) in pieces:
                        for kk in range(31):
                            dgm = self.scr_bf(DG + (c * 31 + kk) * 256, 128)
                            self.mm(self.ps[bk][:, co:co + n], dgm, zb_[c][:, zs + kk: zs + kk + n], kk == 0, kk == 30,
                                    [self.sb("dg", c), self.sb("z", c)], [self.psb[bk]])
                    a32 = self.scr_f32(AC + (par * 2 + c) * 2048, 512)
                    a32b = self.sb("a32", par, c)
                    ab = self.scr_bf(AB + (c * 2) * 1024, 512)
                    sq = self.scr_bf(AB + (c * 2 + 1) * 1024, 512)
                    abb = self.sb("ab", c)
                    cb = self.cpc(f"conv_b{l}", c, 1)
                    self.act(a32, self.ps[bk][:], AF.Identity, [self.psb[bk], bcp], [a32b], bias=cb)
                    self.act(ab, self.ps[bk][:], AF.Identity, [self.psb[bk], bcp], [abb], bias=cb)
                    self.act(sq, self.ps[bk][:], AF.Square, [self.psb[bk], bcp], [abb], bias=cb)
                    accs.append((a32, a32b, ab, sq, abb))
                bm_, bv_ = 6, 7
                for c in range(2):
                    self.mm(self.ps[bm_][:], self.ones256_b[:], accs[c][2], c == 0, c == 1,
                            [self.b("ones_b"), accs[c][4]], [self.psb[bm_]])
                for c in range(2):
                    self.mm(self.ps[bv_][:], self.ones256_b[:], accs[c][3], c == 0, c == 1,
                            [self.b("ones_b"), accs[c][4]], [self.psb[bv_]])
                mu, mub = self.gettmp()
                self.act(mu[:], self.ps[bm_][:], AF.Identity, [self.psb[bm_]], [mub])
                var, varb = self.gettmp()
                self.act(var[:], self.ps[bm_][:], AF.Square, [self.psb[bm_]], [varb])
                self.tt(var[:], self.ps[bv_][:], var[:], ALU.subtract, [self.psb[bv_], varb], [varb])
                self.rsqrt(var[:], var[:], [varb], varb, clamp=True)
                for c in range(2):
                    d, db = self.gettmp()
                    self.tt(d[:], accs[c][0], mu[:], ALU.subtract, [accs[c][1], mub], [db])
                    self.tt(d[:], d[:], var[:], ALU.mult, [db, varb], [db])
                    self.act(cat[c][:, blk * 512:(blk + 1) * 512], d[:], AF.Silu, [db, bcp], [self.sb("ccat", c, blk)],
                             scale=self.cpc(f"ln_g{l}", c, 1), bias=self.cpc(f"ln_b{l}", c, 1))
            wo_c = [load_wo(6 + c, 4 + c) for c in range(2)]
            for blk in range(nblk):
                wout_partial([(wo_c[c], 4 + c, cat[c][:, blk * 512:(blk + 1) * 512], self.sb("ccat", c, blk))
                              for c in range(2)], blk)

        nch = T // 128
        QF, QB, KF, KB = PH, PH + 2 * T, PH + 4 * T, PH + 6 * T
        KFT, KBT = PH + 8 * T, PH + 10 * T
        SF, SB = PH + 12 * T, PH + 14 * T
        VT = PH + 16 * T
        RF = PH + 18 * T
        GT = RF + 1024
        ST = GT + 2048
        CAT = ST + 2048
        OB = CAT + 2048
        assert OB + 2048 <= WI, (OB, WI)
        for h in range(4):
            self.new_phase(KEEP)
            qf = self.scr_bf(QF, T)
            qb = self.scr_bf(QB, T)
            kf = self.scr_bf(KF, T)
            kb = self.scr_bf(KB, T)
            kfT = self.scr_bf(KFT, T)
            kbT = self.scr_bf(KBT, T)
            sf = self.scr_bf(SF, T)
            sbk = self.scr_bf(SB, T)
            vT = self.scr_bf(VT, T)
            Rst = self.scr_f32(RF, 256)
            gt = self.scr_f32(GT, 512)
            bgt = self.sb("gt")
            lgf = self.lg[:, l * 8 + h: l * 8 + h + 1]
            lgb = self.lg[:, l * 8 + 4 + h: l * 8 + 4 + h + 1]
            nlgf = self.nlg[:, l * 8 + h: l * 8 + h + 1]
            nlgb = self.nlg[:, l * 8 + 4 + h: l * 8 + 4 + h + 1]
            blg = self.b("lg")
            self.act(gt[:, 0:128], self.cpc("pos1"), AF.Exp, [bcp, blg], [bgt], scale=lgf)
            self.act(gt[:, 128:256], self.cpc("posr"), AF.Exp, [bcp, blg], [bgt], scale=lgb)
            self.act(gt[:, 256:384], self.cpc("pos1"), AF.Exp, [bcp, blg], [bgt], scale=nlgf)
            self.act(gt[:, 384:512], self.cpc("posr"), AF.Exp, [bcp, blg], [bgt], scale=nlgb)
            ksc = 128.0 ** -0.5
            self.ts(gt[:, 256:512], gt[:, 256:512], ksc, None, ALU.mult, None, [bgt], [bgt])
            wq = load_wi(0, 2 + h)
            wk = load_wi(1, 6 + h)
            wv = load_wi(2, 10 + h)
            for blk in range(nblk):
                if rope:
                    rs_ = self.slot("rope")
                    rt = self.ropes[rs_]
                    rtb = self.b("rope", rs_)
                    self.dma("sp", rt[:, :].rearrange("p (a t) -> p a t", a=2),
                             self.rope_d[:, :, blk * 512:(blk + 1) * 512], f"rope{rs_}", [], [rtb])
                items = []
                for which, wt, slot, outs in (("q", wq, 0, ((qf, 0), (qb, 128))), ("k", wk, 1, ((kf, 256), (kb, 384)))):
                    if not full and which == "q":
                        continue
                    bk = self.bank()
                    proj_fm(wt, slot, blk, bk)
                    if rope:
                        q32, q32b = self.gettmp()
                        self.act(q32[:], self.ps[bk][:], AF.Identity, [self.psb[bk]], [q32b])
                        items.append((which, outs, q32, q32b))
                    else:
                        items.append((which, outs, self.ps[bk], self.psb[bk]))
                for (which, outs, q32, q32b) in items:
                    if rope:
                        bp = self.bank()
                        self.mm(self.ps[bp][:], self.cpc("perm"), q32[:], True, True, [bcp, q32b], [self.psb[bp]])
                        t2, t2b = self.gettmp()
                        self.tt(t2[:], self.ps[bp][:], rt[:, 512:1024], ALU.mult, [self.psb[bp], rtb], [t2b])
                        self.tt(q32[:], q32[:], rt[:, 0:512], ALU.mult, [q32b, rtb], [q32b])
                        self.tt(q32[:], q32[:], t2[:], ALU.add, [q32b, t2b], [q32b])
                    src, srcb = q32[:], q32b
                    for (dst, go) in outs:
                        o3 = dst[:, blk * 512:(blk + 1) * 512].rearrange("p (c i) -> p c i", c=4)
                        i3 = src.rearrange("p (c i) -> p c i", c=4)
                        g3 = gt[:, go:go + 128].unsqueeze(1).to_broadcast([128, 4, 128])
                        self.tt(o3, i3, g3, ALU.mult, [srcb, bgt], [self.sb(which, go, blk)])
            for cgp in range(nch // 4):
                bk = self.bank()
                for cc in range(4):
                    n = cgp * 4 + cc
                    blk = n // 4
                    for k in range(KC):
                        self.mm(self.ps[bk][:, cc * 128:(cc + 1) * 128], self.scr_bf(HT + k * (2 * T) + n * 256, 128), wv[:, k * 128:(k + 1) * 128],
                                k == 0, k == KC - 1, [self.sb("h", k, blk), self.sb("wi", 2)], [self.psb[bk]])
                self.act(vT[:, cgp * 512:(cgp + 1) * 512], self.ps[bk][:], AF.Identity, [self.psb[bk]], [self.sb("vT", cgp)])
            for (src, srcgo, dstT, nm) in ((kf, 256, kfT, "kfT"), (kb, 384, kbT, "kbT")):
                for cgp in range(nch // 4):
                    bk = self.bank()
                    pst = self.ps[bk][:, :].bitcast(BF16)
                    for cc in range(4):
                        n = cgp * 4 + cc
                        self.tr(pst[:, cc * 128:(cc + 1) * 128], src[:, n * 128:(n + 1) * 128], self.ident_b[:],
                                [self.sb("k", srcgo, n // 4), self.b("ident_b")], [self.psb[bk]])
                    self.cpy(dstT[:, cgp * 512:(cgp + 1) * 512], pst[:, 0:512], [self.psb[bk]], [self.sb(nm, cgp)])
            if full:
                wg = load_wi(3, 14 + h)
                for blk in range(nblk):
                    bg = self.bank()
                    proj_fm(wg, 3, blk, bg)
                    self.act(self.sgall[:, blk * 512:(blk + 1) * 512], self.ps[bg][:], AF.Silu, [self.psb[bg]],
                             [self.b("sg", blk)])
            for (c0, ncn, bi_) in seqs:
                for idx in range(ncn):
                    for d_ in range(2):
                        kT, nm = (kfT, "kfT") if d_ == 0 else (kbT, "kbT")
                        sdst = sf if d_ == 0 else sbk
                        g128c = self.g128[:, l * 8 + d_ * 4 + h: l * 8 + d_ * 4 + h + 1]
                        R = Rst[:, d_ * 128:(d_ + 1) * 128]
                        Rb = self.sb("R", d_)
                        st_off = (((l * 2 + bi_) * 2 + d_) * 4 + h) * 128
                        S0 = self.states[:, st_off: st_off + 128]
                        S0b = self.b("state", l, bi_, d_, h)
                        n = c0 + idx if d_ == 0 else c0 + ncn - 1 - idx
                        if full:
                            if idx == 0:
                                if rope:
                                    self.cpy(sdst[:, n * 128:(n + 1) * 128], S0, [S0b], [self.sb("S", d_, n)])
                                else:
                                    self.memset(sdst[:, n * 128:(n + 1) * 128], 0.0, [self.sb("S", d_, n)])
                            else:
                                self.act(sdst[:, n * 128:(n + 1) * 128], R, AF.Identity, [Rb, blg], [self.sb("S", d_, n)],
                                         scale=g128c)
                        last = idx == ncn - 1
                        if last and rope:
                            continue
                        bk = self.bank()
                        self.mm(self.ps[bk][:, 0:128], kT[:, n * 128:(n + 1) * 128], vT[:, n * 128:(n + 1) * 128], True, True,
                                [self.sb(nm, n // 4), self.sb("vT", n // 4)], [self.psb[bk]])
                        if idx == 0:
                            if rope:
                                self.tt(R, self.ps[bk][:, 0:128], S0, ALU.add, [self.psb[bk], S0b], [Rb])
                            else:
                                self.cpy(R, self.ps[bk][:, 0:128], [self.psb[bk]], [Rb])
                        else:
                            self.stt(R, R, g128c, self.ps[bk][:, 0:128], ALU.mult, ALU.add, [Rb, blg, self.psb[bk]], [Rb])
                        if last and not rope:
                            self.act(S0, R, AF.Identity, [Rb, blg], [S0b], scale=g128c)
            if not full:
                continue
            odd = h % 2 == 1
            if odd:
                wo_prev = load_wo(2 + h - 1, 4)
                wo_cur = load_wo(2 + h, 5)

            def stageA1(blk):
                sts = []
                for d_ in range(2):
                    kk_, kgo = (kf, 256) if d_ == 0 else (kb, 384)
                    qq_, qgo = (qf, 0) if d_ == 0 else (qb, 128)
                    bk = self.bank()
                    for cc in range(4):
                        n = blk * 4 + cc
                        self.mm(self.ps[bk][:, cc * 128:(cc + 1) * 128], kk_[:, n * 128:(n + 1) * 128],
                                qq_[:, n * 128:(n + 1) * 128], True, True,
                                [self.sb("k", kgo, blk), self.sb("q", qgo, blk)], [self.psb[bk]])
                    sT = self.scr_bf(ST + d_ * 1024, 512)
                    mk = self.cpc("maskf" if d_ == 0 else "maskb").unsqueeze(1).to_broadcast([128, 4, 128])
                    self.tt(sT.rearrange("p (c i) -> p c i", c=4), self.ps[bk][:, :].rearrange("p (c i) -> p c i", c=4),
                            mk, ALU.mult, [self.psb[bk], bcp], [self.sb("sT", d_)])
                    sts.append(sT)
                bo = 4 + blk % 2
                for cc in range(4):
                    n = blk * 4 + cc
                    oc = self.ps[bo][:, cc * 128:(cc + 1) * 128]
                    vch = vT[:, n * 128:(n + 1) * 128]
                    self.mm(oc, vch, sts[0][:, cc * 128:(cc + 1) * 128], True, False,
                            [self.sb("vT", n // 4), self.sb("sT", 0)], [self.psb[bo]])
                    self.mm(oc, vch, sts[1][:, cc * 128:(cc + 1) * 128], False, False,
                            [self.sb("vT", n // 4), self.sb("sT", 1)], [self.psb[bo]])
                    self.mm(oc, sf[:, n * 128:(n + 1) * 128], qf[:, n * 128:(n + 1) * 128], False, False,
                            [self.sb("S", 0, n), self.sb("q", 0, blk)], [self.psb[bo]])
                    self.mm(oc, sbk[:, n * 128:(n + 1) * 128], qb[:, n * 128:(n + 1) * 128], False, True,
                            [self.sb("S", 1, n), self.sb("q", 128, blk)], [self.psb[bo]])
                return bo

            def stageA2(blk, bo):
                par = blk % 2
                rb = self.b("rope", par)
                o32 = self.ropes[par][:, 0:512]
                mu = self.ropes[par][:, 512:1024]
                var, varb = self.rstd[par], self.rstdb[par]
                self.act(o32, self.ps[bo][:], AF.Identity, [self.psb[bo]], [rb])
                ob = self.scr_bf(OB, 512)
                osq = self.scr_bf(OB + 1024, 512)
                obb = self.sb("ob")
                self.act(ob, self.ps[bo][:], AF.Identity, [self.psb[bo]], [obb])
                self.act(osq, self.ps[bo][:], AF.Square, [self.psb[bo]], [obb])
                self.mm(self.ps[6][:], self.ones128_b[:], ob, True, True, [self.b("ones_b"), obb], [self.psb[6]])
                self.mm(self.ps[7][:], self.ones128_b[:], osq, True, True, [self.b("ones_b"), obb], [self.psb[7]])
                self.act(mu, self.ps[6][:], AF.Identity, [self.psb[6]], [rb])
                self.act(var[:], self.ps[6][:], AF.Square, [self.psb[6]], [varb])
                self.tt(var[:], self.ps[7][:], var[:], ALU.subtract, [self.psb[7], varb], [varb])
                self.rsqrt(var[:], var[:], [varb], varb, clamp=True)

            def stageB(blk):
                par = blk % 2
                rb = self.b("rope", par)
                o32 = self.ropes[par][:, 0:512]
                mu = self.ropes[par][:, 512:1024]
                var, varb = self.rstd[par], self.rstdb[par]
                self.tt(o32, o32, mu, ALU.subtract, [rb], [rb])
                self.tt(o32, o32, var[:], ALU.mult, [rb, varb], [rb])
                if odd:
                    cat, catb = self.scr_bf(CAT + par * 1024, 512), self.sb("hcat", par)
                else:
                    cat, catb = self.hprev[:, blk * 512:(blk + 1) * 512], self.b("hprev", blk)
                self.stt(cat, o32, self.cpc(f"gn_g{l}", h, 1), self.sgall[:, blk * 512:(blk + 1) * 512], ALU.mult, ALU.mult,
                         [rb, bcp, self.b("sg", blk)], [catb])

            def stageC(blk):
                if not odd:
                    return
                par = blk % 2
                wout_partial([(wo_prev, 4, self.hprev[:, blk * 512:(blk + 1) * 512], self.b("hprev", blk)),
                              (wo_cur, 5, self.scr_bf(CAT + par * 1024, 512), self.sb("hcat", par))], blk)

            self.bank_set = [0, 1, 2, 3]
            bo_ = stageA1(0)
            stageA2(0, bo_)
            for blk in range(nblk):
                if blk + 1 < nblk:
                    bo_ = stageA1(blk + 1)
                stageB(blk)
                stageC(blk)
                if blk + 1 < nblk:
                    stageA2(blk + 1, bo_)
            self.bank_set = [0, 1, 2, 3, 4, 5]
        self.new_phase()

    def final_norm(self, nblk):
        for blk in range(nblk):
            rs, rsb = self.rms_rstd(blk)
            for k in range(KC):
                xs = self.xv(k, blk * 512, 512)
                self.stt(xs, xs, self.cpc("final_g", k, 1), rs[:], ALU.mult, ALU.mult,
                         [self.xb(k, blk), self.b("cp"), rsb], [self.xb(k, blk)])

    def build(self):
        stop = self.dbg_stop
        self.prologue()
        ctx_seqs = [(0, 2, 0), (2, 2, 1)]
        self.load_seg(self.ctx_d, 512)
        stage = 0
        done = False
        for l in range(NL):
            last = l == NL - 1
            self.ffn(l, 0, 1, 0)
            stage += 1
            if stop == ("c", stage):
                done = True
                break
            self.mixer(l, 1, 0, ctx_seqs, rope=False, full=not last, bidx=None)
            stage += 1
            if stop == ("c", stage):
                done = True
                break
            if not last:
                self.ffn(l, 1, 1, 0)
                stage += 1
                if stop == ("c", stage):
                    done = True
                    break
        if stop is not None and stop[0] == "c":
            self.store_seg(self.dbgy_d, 512)
        else:
            for bi in range(2):
                self.load_seg(self.x_d[bi], SEQ)
                stage = 0
                done = False
                for l in range(NL):
                    self.ffn(l, 0, 4, 1 + bi)
                    stage += 1
                    if stop == ("x", stage):
                        done = True
                        break
                    self.mixer(l, 4, 1 + bi, [(0, 16, bi)], rope=True, full=True, bidx=bi)
                    stage += 1
                    if stop == ("x", stage):
                        done = True
                        break
                    self.ffn(l, 1, 4, 1 + bi)
                    stage += 1
                    if stop == ("x", stage):
                        done = True
                        break
                if not done:
                    self.final_norm(4)
                self.store_seg(self.out_d[bi], SEQ)
        outs = [self.b("outd", i) for i in range(8)]
        self.P.op("sp", lambda h: h.nop(), reads=outs)
        self.P.emit()
        return self.nc


_CACHE = {}


def kernel(x, c, ctx, c_ctx, w_mod, b_mod, norm_g, ffn_w1, ffn_w3, ffn_w2, w_in, w_out,
           pool_w, pool_scale, ret_decay_fwd, ret_decay_bwd, ret_gn_g, conv_dw, conv_b,
           conv_ln_g, conv_ln_b, final_g, _dbg_stop=None, _cores=None):
    inp = dict(x=x, c=c, ctx=ctx, c_ctx=c_ctx, w_mod=w_mod, b_mod=b_mod, norm_g=norm_g, ffn_w1=ffn_w1,
               ffn_w3=ffn_w3, ffn_w2=ffn_w2, w_in=w_in, w_out=w_out, pool_w=pool_w, pool_scale=pool_scale,
               ret_decay_fwd=ret_decay_fwd, ret_decay_bwd=ret_decay_bwd, ret_gn_g=ret_gn_g, conv_dw=conv_dw,
               conv_b=conv_b, conv_ln_g=conv_ln_g, conv_ln_b=conv_ln_b, final_g=final_g)
    inp = {k: np.asarray(v) for k, v in inp.items()}
    cores = list(range(NCORES)) if _cores is None else _cores
    W = relayout_weights(inp)
    rope = build_rope()
    poolw = build_poolw(inp)
    k = K(dbg_stop=_dbg_stop)
    nc = k.build()
    in_maps = []
    xs = np.asarray(inp["x"], np.float32)
    cs = np.asarray(inp["ctx"], np.float32)
    for core in cores:
        m = dict(W)
        m["x"] = np.ascontiguousarray(xs[2 * core: 2 * core + 2])
        m["ctx"] = np.ascontiguousarray(cs[2 * core: 2 * core + 2].reshape(2 * CTX, D))
        m["cp"], m["cp2"] = build_cp(inp, core)
        m["rope"] = rope
        m["poolw"] = poolw
        in_maps.append(m)
    res = run_bass_kernel_spmd(nc, in_maps, core_ids=list(range(len(cores))))
    if _dbg_stop is not None:
        return res.results
    out = np.concatenate([np.asarray(r["out"], np.float32) for r in res.results], axis=0)
    return out
```

```python
import numpy as np
import concourse.bass as bass
import concourse.mybir as mybir
from concourse.bass_utils import run_bass_kernel_spmd

F32 = mybir.dt.float32
BF16 = mybir.dt.bfloat16
ALU = mybir.AluOpType
AF = mybir.ActivationFunctionType

D = 1024
KC = 8
DFF = 2816
FC = 22
SEQ = 2048
CTX = 256
NL = 2
EPS = 1e-6
NCORES = 8


class Buf:
    __slots__ = ("name", "last_w", "readers")

    def __init__(self, name, inherit=None):
        self.name = name
        self.last_w = None
        self.readers = dict(inherit) if inherit else {}


class Tok:
    __slots__ = ("key", "ord", "clock", "op")

    def __init__(self, key, ord_, clock, op):
        self.key = key
        self.ord = ord_
        self.clock = clock
        self.op = op


class Op:
    __slots__ = ("eng", "fn", "waits", "tok", "signal", "semval", "dma_sem")


class Prog:
    ENGS = ("pe", "act", "dve", "pool", "sp")

    def __init__(self, nc):
        self.nc = nc
        self.h = {"pe": nc.tensor, "act": nc.scalar, "dve": nc.vector,
                  "pool": nc.gpsimd, "sp": nc.sync}
        self.ops = []
        self.nops = {e: 0 for e in self.ENGS}
        self.seen = {e: {} for e in self.ENGS}
        self.dma_count = {}
        self.sems = {}

    def op(self, eng, fn, reads=(), writes=(), dma=None):
        seen = self.seen[eng]
        deps = {}

        def add(t, raw):
            if t is None:
                return
            if t.key == ("e", eng) and eng in ("pe", "sp"):
                return
            k = t.key
            if k not in deps or deps[k].ord < t.ord:
                deps[k] = t

        for b in reads:
            add(b.last_w, True)
        for b in writes:
            add(b.last_w, True)
            for t in b.readers.values():
                add(t, False)
        o = Op()
        o.eng = eng
        o.fn = fn
        o.waits = []
        o.signal = False
        o.semval = None
        o.dma_sem = dma
        for k, t in deps.items():
            if seen.get(k, 0) >= t.ord:
                continue
            o.waits.append(t)
            if t.op is not None:
                t.op.signal = True
            seen[k] = t.ord
            for kk, vv in t.clock.items():
                if seen.get(kk, 0) < vv:
                    seen[kk] = vv
        if dma is None:
            self.nops[eng] += 1
            tok = Tok(("e", eng), self.nops[eng], dict(seen), o)
        else:
            self.dma_count[dma] = self.dma_count.get(dma, 0) + 16
            tok = Tok(("d", dma), self.dma_count[dma], dict(seen), None)
        o.tok = tok
        self.ops.append(o)
        for b in writes:
            b.last_w = tok
            b.readers = {}
        for b in reads:
            if b in writes:
                continue
            b.readers[tok.key] = tok
        return tok

    def _sem(self, name):
        if name not in self.sems:
            self.sems[name] = self.nc.alloc_semaphore(name)
        return self.sems[name]

    def emit(self):
        cnt = {e: 0 for e in self.ENGS}
        for o in self.ops:
            h = self.h[o.eng]
            for t in o.waits:
                if t.key[0] == "e":
                    h.wait_ge(self._sem("s_" + t.key[1]), t.op.semval)
                else:
                    h.wait_ge(self._sem("d_" + t.key[1]), t.ord)
            ins = o.fn(h)
            if o.dma_sem is not None:
                ins.then_inc(self._sem("d_" + o.dma_sem), 16)
            elif o.signal:
                cnt[o.eng] += 1
                o.semval = cnt[o.eng]
                ins.then_inc(self._sem("s_" + o.eng), 1)


def _cp_layout():
    off = {}
    c = 0

    def add(name, n):
        nonlocal c
        off[name] = (c, n)
        c += n

    add("ident", 128)
    add("perm", 128)
    add("maskf", 128)
    add("maskb", 128)
    add("pos1", 128)
    add("posr", 128)
    add("pos1c", 1)
    add("posrc", 1)
    add("eps", 1)
    add("one", 1)
    add("edge", 4 * 16)
    add("final_g", 8)
    for l in range(NL):
        add(f"pool_scale{l}", 2)
        add(f"gn_g{l}", 4)
        add(f"conv_b{l}", 2)
        add(f"ln_g{l}", 2)
        add(f"ln_b{l}", 2)
        add(f"conv_dw{l}", 62)
        add(f"dec{l}", 8)
    return off, c


def _cp2_layout():
    off = {}
    c = 0
    for name, n in (("ones1024", 128), ("ones128", 128), ("ones256", 128), ("cvec", 24),
                    ("normg0", 72), ("bmod0", 216), ("normg1", 72), ("bmod1", 216)):
        off[name] = (c, n)
        c += n
    return off, c


CP_OFF, CP_N = _cp_layout()
CP2_OFF, CP2_N = _cp2_layout()
POOL_WINDOWS = (2, 4, 8, 16)


def _fm(v):
    return np.ascontiguousarray(np.asarray(v, np.float32).reshape(-1, 128).T)


def build_cp(inp, core):
    cp = np.zeros((128, CP_N), np.float32)
    cp2 = np.zeros((128, CP2_N), np.float32)

    def put(name, arr):
        if name in CP2_OFF:
            o, n = CP2_OFF[name]
            cp2[:, o:o + n] = np.asarray(arr, np.float32).reshape(128, n)
            return
        o, n = CP_OFF[name]
        arr = np.asarray(arr, np.float32).reshape(128, n)
        cp[:, o:o + n] = arr

    put("ident", np.eye(128, dtype=np.float32))
    perm = np.zeros((128, 128), np.float32)
    for m in range(128):
        partner = m + 32 if (m % 64) < 32 else m - 32
        perm[partner, m] = 1.0
    put("perm", perm)
    put("ones1024", np.full((128, 128), 1.0 / 1024, np.float32))
    put("ones128", np.full((128, 128), 1.0 / 128, np.float32))
    put("ones256", np.full((128, 128), 1.0 / 256, np.float32))
    jj = np.arange(128)[:, None]
    ii = np.arange(128)[None, :]
    put("maskf", (ii >= jj).astype(np.float32))
    put("maskb", (jj > ii).astype(np.float32))
    put("pos1", np.tile(np.arange(1, 129, dtype=np.float32)[None, :], (128, 1)))
    put("posr", np.tile((128 - np.arange(128, dtype=np.float32))[None, :], (128, 1)))
    put("pos1c", np.arange(1, 129, dtype=np.float32)[:, None])
    put("posrc", (128 - np.arange(128, dtype=np.float32))[:, None])
    put("eps", np.full((128, 1), EPS, np.float32))
    put("one", np.ones((128, 1), np.float32))
    edge = np.zeros((4, 16), np.float32)
    for gi, w in enumerate(POOL_WINDOWS):
        for t in range(8):
            cl = min(t + w // 2, w) if True else w
            cl = (t + w // 2) - max(t - w // 2, 0)
            edge[gi, t] = w / cl
            d = 8 - t
            cr = min(d, w // 2) + w // 2
            edge[gi, 8 + t] = w / cr
    put("edge", np.tile(edge.reshape(1, 64), (128, 1)))
    b0 = 2 * core
    cv = np.stack([np.asarray(inp["c_ctx"], np.float32),
                   np.asarray(inp["c"][b0], np.float32),
                   np.asarray(inp["c"][b0 + 1], np.float32)], axis=0)
    put("cvec", cv.reshape(3, 8, 128).transpose(2, 1, 0).reshape(128, 24))
    put("final_g", _fm(inp["final_g"]))
    for l in range(NL):
        ng = np.asarray(inp["norm_g"][l], np.float32).reshape(3, 8, 128).transpose(2, 0, 1)
        put(f"normg{l}", np.repeat(ng[:, :, :, None], 3, axis=3).reshape(128, 72))
        bm = _fm(inp["b_mod"][l])
        put(f"bmod{l}", np.repeat(bm[:, :, None], 3, axis=2).reshape(128, 216))
        put(f"pool_scale{l}", _fm(inp["pool_scale"][l]))
        put(f"gn_g{l}", _fm(inp["ret_gn_g"][l]))
        put(f"conv_b{l}", _fm(inp["conv_b"][l]))
        put(f"ln_g{l}", _fm(inp["conv_ln_g"][l]))
        put(f"ln_b{l}", _fm(inp["conv_ln_b"][l]))
        dw = np.asarray(inp["conv_dw"][l], np.float32)
        put(f"conv_dw{l}", dw.reshape(31, 2, 128).transpose(2, 1, 0).reshape(128, 62))
        dec = np.concatenate([np.asarray(inp["ret_decay_fwd"][l], np.float32),
                              np.asarray(inp["ret_decay_bwd"][l], np.float32)])
        put(f"dec{l}", np.tile(dec[None, :], (128, 1)))
    return cp, cp2


def build_poolw(inp):
    out = np.zeros((128, NL, 2, 128), np.float32)
    for l in range(NL):
        pw = np.asarray(inp["pool_w"][l], np.float32)
        for ch in range(2):
            out[0:64, l, ch, 0:64] = pw[2 * ch]
            out[64:128, l, ch, 64:128] = pw[2 * ch + 1]
    return out.reshape(128, NL * 256)


def build_rope():
    n_freq = 32
    inv = (10000.0 ** (-np.arange(n_freq, dtype=np.float32) / n_freq)).astype(np.float32)
    t = np.arange(SEQ)
    row = (t // 64).astype(np.float32)
    col = (t % 64).astype(np.float32)
    tab = np.zeros((128, 2, SEQ), np.float32)
    for f in range(128):
        pos = row if f < 64 else col
        ang = (pos * inv[f % 32]).astype(np.float32)
        tab[f, 0] = np.cos(ang)
        s = np.sin(ang)
        tab[f, 1] = -s if (f % 64) < 32 else s
    return tab


def relayout_weights(inp):
    w = {}
    w1 = np.asarray(inp["ffn_w1"], np.float32).reshape(4, 8, 128, 11, 256)
    w["w1"] = np.ascontiguousarray(w1.transpose(0, 3, 2, 1, 4)).reshape(44, 128, 2048)
    w3 = np.asarray(inp["ffn_w3"], np.float32).reshape(4, 8, 128, 11, 256)
    w["w3"] = np.ascontiguousarray(w3.transpose(0, 3, 2, 1, 4)).reshape(44, 128, 2048)
    w2 = np.asarray(inp["ffn_w2"], np.float32).reshape(4, 22, 128, 8, 128)
    w["w2"] = np.ascontiguousarray(w2.transpose(0, 3, 2, 1, 4)).reshape(32, 128, 2816)
    wi = np.asarray(inp["w_in"], np.float32).reshape(2, 8, 128, 22, 128)
    w["win"] = np.ascontiguousarray(wi.transpose(0, 3, 2, 1, 4)).reshape(44, 128, 1024)
    w["wout"] = np.ascontiguousarray(np.asarray(inp["w_out"], np.float32).reshape(16, 128, 1024))
    wm = np.asarray(inp["w_mod"], np.float32).reshape(2, 8, 128, 18, 512)
    w["wmod"] = np.ascontiguousarray(wm.transpose(0, 3, 2, 1, 4)).reshape(36, 128, 4096)
    return w


class K:
    def __init__(self, dbg_stop=None):
        self.dbg_stop = dbg_stop
        nc = bass.Bass("TRN2", target_bir_lowering=False)
        self.nc = nc
        self.P = Prog(nc)
        dt = nc.dram_tensor
        self.x_d = dt("x", [2, SEQ, D], F32, kind="ExternalInput").ap()
        self.ctx_d = dt("ctx", [2 * CTX, D], F32, kind="ExternalInput").ap()
        self.cp_d = dt("cp", [128, CP_N], F32, kind="ExternalInput").ap()
        self.cp2_d = dt("cp2", [128, CP2_N], F32, kind="ExternalInput").ap()
        self.rope_d = dt("rope", [128, 2, SEQ], F32, kind="ExternalInput").ap()
        self.poolw_d = dt("poolw", [128, NL * 256], F32, kind="ExternalInput").ap()
        self.w1_d = dt("w1", [44, 128, 2048], F32, kind="ExternalInput").ap()
        self.w3_d = dt("w3", [44, 128, 2048], F32, kind="ExternalInput").ap()
        self.w2_d = dt("w2", [32, 128, 2816], F32, kind="ExternalInput").ap()
        self.win_d = dt("win", [44, 128, 1024], F32, kind="ExternalInput").ap()
        self.wout_d = dt("wout", [16, 128, 1024], F32, kind="ExternalInput").ap()
        self.wmod_d = dt("wmod", [36, 128, 4096], F32, kind="ExternalInput").ap()
        self.out_d = dt("out", [2, SEQ, D], F32, kind="ExternalOutput").ap()
        if dbg_stop is not None:
            self.dbgy_d = dt("dbgy", [2 * CTX, D], F32, kind="ExternalOutput").ap()

        A = nc.alloc_sbuf_tensor
        self.xT = A("xT", [128, KC * SEQ], F32)
        self.cp = A("cp_s", [128, CP_N], F32)
        self.mod = A("mod", [128, NL * 216], F32)
        self.gs = A("gs", [128, NL * 72], F32)
        self.hg = A("hg", [128, NL * 72], F32)
        self.sc = A("sc", [128, 24], F32)
        self.sc_b = A("sc_b", [128, 24], BF16)
        self.lg = A("lg", [128, NL * 8], F32)
        self.nlg = A("nlg", [128, NL * 8], F32)
        self.g128 = A("g128", [128, NL * 8], F32)
        self.ones_b = A("ones_b", [128, 128], BF16)
        self.ident_b = A("ident_b", [128, 128], BF16)
        self.ones128_b = A("ones128_b", [128, 128], BF16)
        self.ones256_b = A("ones256_b", [128, 128], BF16)
        self.poolw_b = A("poolw_b", [128, NL * 256], BF16)
        self.states = A("states", [128, NL * 2 * 2 * 4 * 128], BF16)
        self.sgall = A("sgall", [128, SEQ], BF16)
        self.hprev = A("hprev", [128, SEQ], BF16)
        self.NTMP = 7
        self.tmp = [A(f"tmp{i}", [128, 512], F32) for i in range(self.NTMP)]
        self.sqb = [A(f"sqb{i}", [128, 512], BF16) for i in range(2)]
        self.rstd = [A(f"rstd{i}", [128, 512], F32) for i in range(2)]
        self.rstdb = [Buf(f"rstd{i}") for i in range(2)]
        self.rstd_i = 0
        self.ropes = [A(f"rope{i}", [128, 2 * 512], F32) for i in range(2)]
        self.SCR_BYTES = 89088 + 2048
        self.scr = A("scr", [128, self.SCR_BYTES // 4], F32)
        self.ps = [nc.alloc_psum_tensor(f"ps{i}", [128, 512], F32) for i in range(8)]
        self.psb = [Buf(f"ps{i}") for i in range(8)]
        self.bank_i = 0
        self.bank_set = [0, 1, 2, 3, 4, 5]
        self.tmp_i = 0
        self.tmpb = [Buf(f"tmp{i}") for i in range(self.NTMP)]
        self.sqb_i = 0
        self.sqbb = [Buf(f"sqb{i}") for i in range(2)]
        self.bufs = {}
        self.scr_bufs = {}
        self.inherit = {}
        self.slot_ctr = {}

    def b(self, *key):
        if key not in self.bufs:
            self.bufs[key] = Buf(str(key))
        return self.bufs[key]

    def sb(self, *key):
        if key not in self.scr_bufs:
            self.scr_bufs[key] = Buf(str(key), self.inherit)
        return self.scr_bufs[key]

    def new_phase(self, keep=()):
        keepd = {}
        for key, bf in self.scr_bufs.items():
            if key[0] in keep:
                keepd[key] = bf
                continue
            toks = list(bf.readers.values())
            if bf.last_w is not None:
                toks.append(bf.last_w)
            for t in toks:
                if t.key not in self.inherit or self.inherit[t.key].ord < t.ord:
                    self.inherit[t.key] = t
        self.scr_bufs = keepd

    def scr_f32(self, off_bytes, n):
        assert off_bytes % 4 == 0 and off_bytes + 4 * n <= self.SCR_BYTES, (off_bytes, n)
        return self.scr[:, off_bytes // 4: off_bytes // 4 + n]

    def scr_bf(self, off_bytes, n):
        assert off_bytes % 4 == 0 and n % 2 == 0 and off_bytes + 2 * n <= self.SCR_BYTES, (off_bytes, n)
        return self.scr[:, off_bytes // 4: off_bytes // 4 + n // 2].bitcast(BF16)

    def bank(self):
        bs = self.bank_set
        self.bank_i = (self.bank_i + 1) % len(bs)
        return bs[self.bank_i]

    def gettmp(self):
        i = self.tmp_i
        self.tmp_i = (i + 1) % self.NTMP
        return self.tmp[i], self.tmpb[i]

    def getsq(self):
        i = self.sqb_i
        self.sqb_i = (i + 1) % 2
        return self.sqb[i], self.sqbb[i]

    def cpc(self, name, a=0, n=None):
        o, nn = CP_OFF[name]
        if n is None:
            n = nn - a
        return self.cp[:, o + a: o + a + n]

    def mm(self, out, lhsT, rhs, start, stop, R, W):
        self.P.op("pe", lambda h: h.matmul(out, lhsT, rhs, start=start, stop=stop), reads=R, writes=W)

    def tr(self, out, in_, ident, R, W):
        self.P.op("pe", lambda h: h.transpose(out, in_, ident), reads=R, writes=W)

    def act(self, out, in_, func, R, W, scale=1.0, bias=0.0):
        self.P.op("act", lambda h: h.activation(out=out, in_=in_, func=func, bias=bias, scale=scale),
                  reads=R, writes=W)

    def tt(self, out, in0, in1, op, R, W, eng="dve"):
        self.P.op(eng, lambda h: h.tensor_tensor(out=out, in0=in0, in1=in1, op=op), reads=R, writes=W)

    def stt(self, out, in0, scalar, in1, op0, op1, R, W, eng="dve"):
        self.P.op(eng, lambda h: h.scalar_tensor_tensor(out=out, in0=in0, scalar=scalar, in1=in1,
                                                        op0=op0, op1=op1), reads=R, writes=W)

    def ts(self, out, in0, s1, s2, op0, op1, R, W, eng="dve"):
        if s2 is None:
            self.P.op(eng, lambda h: h.tensor_scalar(out=out, in0=in0, scalar1=s1, scalar2=None, op0=op0),
                      reads=R, writes=W)
        else:
            self.P.op(eng, lambda h: h.tensor_scalar(out=out, in0=in0, scalar1=s1, scalar2=s2, op0=op0, op1=op1),
                      reads=R, writes=W)

    def cpy(self, out, in_, R, W, eng="dve"):
        self.P.op(eng, lambda h: h.tensor_copy(out=out, in_=in_), reads=R, writes=W)

    def rsqrt(self, out, in_, R, wb, clamp=False):
        if clamp:
            self.ts(out, in_, 0.0, EPS, ALU.max, ALU.add, list(R), [wb])
            self.act(out, out, AF.Ln, [wb], [wb])
        else:
            self.act(out, in_, AF.Ln, list(R) + [self.b("cp")], [wb], bias=self.cpc("eps"))
        self.act(out, out, AF.Exp, [wb], [wb], scale=-0.5)

    def recip(self, out, in_, R, W):
        self.P.op("dve", lambda h: h.reciprocal(out=out, in_=in_), reads=R, writes=W)

    def memset(self, ap, val, W, eng="dve"):
        self.P.op(eng, lambda h: h.memset(ap, val), writes=W)

    def dma(self, q, out, in_, sem, R, W):
        self.P.op(q, lambda h: h.dma_start(out=out, in_=in_), reads=R, writes=W, dma=sem)

    def xv(self, k, t0, n):
        return self.xT[:, k * SEQ + t0: k * SEQ + t0 + n]

    def xb(self, k, blk):
        return self.b("x", k, blk)

    def modcol(self, l, j, k, grp):
        c = l * 216 + (j * 8 + k) * 3 + grp
        return self.mod[:, c:c + 1]

    def gscol(self, l, jn, k, grp):
        c = l * 72 + (jn * 8 + k) * 3 + grp
        return self.gs[:, c:c + 1]

    def hgcol(self, l, jn, k, grp):
        c = l * 72 + (jn * 8 + k) * 3 + grp
        return self.hg[:, c:c + 1]

    def prologue(self):
        P = self.P
        bcp = self.b("cp")
        self.dma("sp", self.cp[:], self.cp_d, "cp", [], [bcp])
        cp2 = self.scr_f32(32 * 1024, CP2_N)
        bcp2 = self.sb("cp2")
        self.dma("sp", cp2, self.cp2_d, "cp2", [], [bcp2])

        def c2(name, a=0, n=None):
            o, nn = CP2_OFF[name]
            if n is None:
                n = nn - a
            return cp2[:, o + a: o + a + n]
        self.cpy(self.ones_b[:], c2("ones1024"), [bcp2], [self.b("ones_b")])
        self.cpy(self.ident_b[:], self.cpc("ident"), [bcp], [self.b("ident_b")])
        self.cpy(self.ones128_b[:], c2("ones128"), [bcp2], [self.b("ones_b")])
        self.cpy(self.ones256_b[:], c2("ones256"), [bcp2], [self.b("ones_b")])
        self.dma("pool", self.poolw_b[:], self.poolw_d, "poolw", [], [self.b("poolw_b")])
        bsc = self.b("sc")
        self.act(self.sc[:], c2("cvec"), AF.Silu, [bcp2], [bsc])
        self.cpy(self.sc_b[:], self.sc[:], [bsc], [bsc])
        blg = self.b("lg")
        for l in range(NL):
            sl = slice(l * 8, (l + 1) * 8)
            self.act(self.nlg[:, sl], self.cpc(f"dec{l}"), AF.Exp, [bcp], [blg], scale=-1.0)
            self.act(self.nlg[:, sl], self.nlg[:, sl], AF.Ln, [blg, bcp], [blg], bias=self.cpc("one"))
            self.ts(self.lg[:, sl], self.nlg[:, sl], -1.0, None, ALU.mult, None, [blg], [blg])
            self.act(self.g128[:, sl], self.lg[:, sl], AF.Exp, [blg], [blg], scale=128.0)
        for l in range(NL):
            bank = 6 + l
            for cg in range(18):
                s = (l * 18 + cg) % 4
                wt = self.scr_bf(s * 8192, 4096)
                wb = self.sb("wm", s)
                self.dma("pool", wt, self.wmod_d[l * 18 + cg], f"wm{s}", [], [wb])
                for jj in range(4):
                    j = cg * 4 + jj
                    for k in range(KC):
                        self.mm(self.ps[bank][:, j * 3:(j + 1) * 3],
                                wt[:, k * 512 + jj * 128: k * 512 + (jj + 1) * 128],
                                self.sc_b[:, k * 3:(k + 1) * 3], k == 0, k == KC - 1,
                                [wb, bsc], [self.psb[bank]])
            bm = self.b("mod", l)
            self.tt(self.mod[:, l * 216:(l + 1) * 216], self.ps[bank][:, 0:216], c2(f"bmod{l}"),
                    ALU.add, [self.psb[bank], bcp2], [bm])
            for jn in range(3):
                j = 3 * jn + 1
                self.stt(self.gs[:, l * 72 + jn * 24: l * 72 + (jn + 1) * 24],
                         self.mod[:, l * 216 + j * 24: l * 216 + (j + 1) * 24], 1.0,
                         c2(f"normg{l}", jn * 24, 24), ALU.add, ALU.mult, [bm, bcp2], [self.b("gs", l)])
                jg = 3 * jn + 2
                self.ts(self.hg[:, l * 72 + jn * 24: l * 72 + (jn + 1) * 24],
                        self.mod[:, l * 216 + jg * 24: l * 216 + (jg + 1) * 24],
                        1.0 if jn == 1 else 0.5, None, ALU.mult, None, [bm], [self.b("hg", l)])
        self.new_phase()

    def load_seg(self, src, ntok):
        bident = self.b("cp")
        for tt_ in range(ntok // 128):
            s = tt_ % 8
            st = self.scr_f32(s * 4096, 1024)
            stb = self.sb("xin", s)
            self.dma("sp", st, src[tt_ * 128:(tt_ + 1) * 128, :], f"xin{s}", [], [stb])
            blk = tt_ // 4
            for half in range(2):
                bk = self.bank()
                for kk in range(4):
                    k = half * 4 + kk
                    self.tr(self.ps[bk][:, kk * 128:(kk + 1) * 128], st[:, k * 128:(k + 1) * 128],
                            self.cpc("ident"), [stb, bident], [self.psb[bk]])
                out = self.xT[:, half * 4 * SEQ:(half * 4 + 4) * SEQ].rearrange("p (k t) -> p k t", k=4)[
                    :, :, tt_ * 128:(tt_ + 1) * 128]
                in_ = self.ps[bk][:, :].rearrange("p (k t) -> p k t", k=4)
                eng = "dve" if half == 0 else "act"
                W = [self.xb(half * 4 + kk, blk) for kk in range(4)]
                if eng == "dve":
                    self.cpy(out, in_, [self.psb[bk]], W)
                else:
                    self.act(out, in_, AF.Identity, [self.psb[bk]], W)
        self.new_phase()

    def store_seg(self, dst, ntok):
        bident = self.b("cp")
        for tt_ in range(ntok // 128):
            s = tt_ % 8
            st = self.scr_f32(s * 4096, 1024)
            stb = self.sb("xout", s)
            blk = tt_ // 4
            for half in range(2):
                bk = self.bank()
                for kk in range(4):
                    k = half * 4 + kk
                    self.tr(self.ps[bk][:, kk * 128:(kk + 1) * 128], self.xv(k, tt_ * 128, 128),
                            self.cpc("ident"), [self.xb(k, blk), bident], [self.psb[bk]])
                if half == 0:
                    self.cpy(st[:, 0:512], self.ps[bk][:, :], [self.psb[bk]], [stb])
                else:
                    self.act(st[:, 512:1024], self.ps[bk][:, :], AF.Identity, [self.psb[bk]], [stb])
            self.dma("sp", dst[tt_ * 128:(tt_ + 1) * 128, :], st, f"xout{s}", [stb], [self.b("outd", s)])
        self.new_phase()

    def rms_rstd(self, blk):
        bk = 6 + (blk % 2)
        for k in range(KC):
            sq, sqb = self.getsq()
            self.act(sq[:], self.xv(k, blk * 512, 512), AF.Square, [self.xb(k, blk)], [sqb])
            self.mm(self.ps[bk][:], self.ones_b[:], sq[:], k == 0, k == KC - 1,
                    [sqb, self.b("ones_b")], [self.psb[bk]])
        ri = self.rstd_i
        self.rstd_i = 1 - ri
        rs, rsb = self.rstd[ri], self.rstdb[ri]
        self.rsqrt(rs[:], self.ps[bk][:], [self.psb[bk]], rsb)
        return rs, rsb

    def norm_mod(self, blk, l, jn, grp, hT, hoff, hbuf, rs=None):
        rs, rsb = rs if rs is not None else self.rms_rstd(blk)
        for k in range(KC):
            t, tb = self.gettmp()
            self.tt(t[:], self.xv(k, blk * 512, 512), rs[:], ALU.mult, [self.xb(k, blk), rsb], [tb])
            self.act(hT(k, hoff), t[:], AF.Identity, [tb, self.b("gs", l), self.b("mod", l)], [hbuf(k)],
                     scale=self.gscol(l, jn, k, grp), bias=self.modcol(l, 3 * jn, k, grp))

    def ffn(self, l, f, nblk, grp):
        jn = 0 if f == 0 else 2
        lf = l * 2 + f
        HT = 0
        GT = 16 * 1024
        W1 = 60 * 1024
        W3 = 68 * 1024
        W2 = 76 * 1024
        for g0 in range(0, nblk, 2):
            blks = list(range(g0, min(g0 + 2, nblk)))
            rss = [self.rms_rstd(blk) for blk in blks]
            for bi, blk in enumerate(blks):
                self.norm_mod(blk, l, jn, grp,
                              lambda k, off: self.scr_bf(HT + k * 2048 + off * 2, 512), bi * 512,
                              lambda k, bi=bi: self.sb("h", k, bi), rs=rss[bi])
            for cg in range(11):
                s = self.slot("w13")
                w1t = self.scr_bf(W1 + s * 4096, 2048)
                w3t = self.scr_bf(W3 + s * 4096, 2048)
                self.dma("pool", w1t, self.w1_d[lf * 11 + cg], f"w1{s}", [], [self.sb("w1", s)])
                self.dma("pool", w3t, self.w3_d[lf * 11 + cg], f"w3{s}", [], [self.sb("w3", s)])
                for bi, blk in enumerate(blks):
                    for m in range(2):
                        pa = self.bank()
                        pb = self.bank()
                        for k in range(KC):
                            self.mm(self.ps[pa][:], w1t[:, k * 256 + m * 128: k * 256 + (m + 1) * 128],
                                    self.scr_bf(HT + k * 2048 + bi * 1024, 512), k == 0, k == KC - 1,
                                    [self.sb("w1", s), self.sb("h", k, bi)], [self.psb[pa]])
                        for k in range(KC):
                            self.mm(self.ps[pb][:], w3t[:, k * 256 + m * 128: k * 256 + (m + 1) * 128],
                                    self.scr_bf(HT + k * 2048 + bi * 1024, 512), k == 0, k == KC - 1,
                                    [self.sb("w3", s), self.sb("h", k, bi)], [self.psb[pb]])
                        t, tb = self.gettmp()
                        self.act(t[:], self.ps[pa][:], AF.Silu, [self.psb[pa]], [tb])
                        j = cg * 2 + m
                        self.tt(self.scr_bf(GT + j * 2048 + bi * 1024, 512), t[:], self.ps[pb][:], ALU.mult,
                                [tb, self.psb[pb]], [self.sb("g", j, bi)])
            for dg in range(KC):
                s = self.slot("w2")
                w2t = self.scr_bf(W2 + s * 5632, 2816)
                self.dma("pool", w2t, self.w2_d[lf * 8 + dg], f"w2{s}", [], [self.sb("w2", s)])
                for bi, blk in enumerate(blks):
                    pc = self.bank()
                    for j in range(FC):
                        self.mm(self.ps[pc][:], w2t[:, j * 128:(j + 1) * 128],
                                self.scr_bf(GT + j * 2048 + bi * 1024, 512), j == 0, j == FC - 1,
                                [self.sb("w2", s), self.sb("g", j, bi)], [self.psb[pc]])
                    xs = self.xv(dg, blk * 512, 512)
                    self.stt(xs, self.ps[pc][:], self.hgcol(l, jn, dg, grp), xs, ALU.mult, ALU.add,
                             [self.psb[pc], self.b("hg", l), self.xb(dg, blk)], [self.xb(dg, blk)])
        self.new_phase()

    def slot(self, name):
        v = self.slot_ctr.get(name, 0)
        self.slot_ctr[name] = v + 1
        return v % 2

    def mixer(self, l, nblk, grp, seqs, rope, full, bidx):
        T = nblk * 512
        HT = 0
        PH = 32 * 1024
        WI = 77 * 1024
        hT = lambda k, off: self.scr_bf(HT + k * (2 * T) + off * 2, 512)
        for blk in range(nblk):
            self.norm_mod(blk, l, 1, grp, hT, blk * 512, lambda k, blk=blk: self.sb("h", k, blk))
        KEEP = ("h", "wi")

        def load_wi(slot, chunk):
            wt = self.scr_bf(WI + slot * 2048, 1024)
            self.dma("pool", wt, self.win_d[l * 22 + chunk], f"wi{slot}", [], [self.sb("wi", slot)])
            return wt

        def load_wo(chunk, slot=4):
            wt = self.scr_bf(WI + slot * 2048, 1024)
            self.dma("pool", wt, self.wout_d[l * 8 + chunk], f"wi{slot}", [], [self.sb("wi", slot)])
            return wt

        def proj_fm(wt, slot, blk, bk):
            for k in range(KC):
                self.mm(self.ps[bk][:], wt[:, k * 128:(k + 1) * 128], hT(k, blk * 512), k == 0, k == KC - 1,
                        [self.sb("wi", slot), self.sb("h", k, blk)], [self.psb[bk]])

        def wout_partial(terms, blk):
            for dg in range(KC):
                pc = self.bank()
                for ti, (wo, wslot, cat_ap, catb) in enumerate(terms):
                    self.mm(self.ps[pc][:], wo[:, dg * 128:(dg + 1) * 128], cat_ap, ti == 0, ti == len(terms) - 1,
                            [self.sb("wi", wslot), catb], [self.psb[pc]])
                xs = self.xv(dg, blk * 512, 512)
                self.stt(xs, self.ps[pc][:], self.hgcol(l, 1, dg, grp), xs, ALU.mult, ALU.add,
                         [self.psb[pc], self.b("hg", l), self.xb(dg, blk)], [self.xb(dg, blk)])

        bcp = self.b("cp")
        if full:
            for ch in range(2):
                self.new_phase(KEEP + ("pcat",))
                PADL = 8
                LP = T + 16 * len(seqs)
                pp = self.scr_f32(PH, LP)
                ta = self.scr_f32(PH + 4 * LP, LP)
                tb_ = self.scr_f32(PH + 8 * LP, LP)
                pooled = self.scr_bf(PH + 12 * LP, T)
                CATP = PH + 45 * 1024 - 4 * T
                assert PH + 12 * LP + 2 * T <= CATP
                pcat = [self.scr_bf(CATP + c_ * 2 * T, T) for c_ in range(2)]
                cat = pcat[ch]
                bpp, bta, btb = self.sb("pp"), self.sb("ta"), self.sb("tb")
                for si_, (c0_, ncn_, _) in enumerate(seqs):
                    b0_ = c0_ * 128 + 16 * si_
                    L_ = ncn_ * 128
                    self.memset(pp[:, b0_: b0_ + 8], 0.0, [bpp])
                    self.memset(pp[:, b0_ + 8 + L_: b0_ + 16 + L_], 0.0, [bpp])
                    self.memset(ta[:, b0_: b0_ + 1], 0.0, [bta])
                    self.memset(tb_[:, b0_: b0_ + 1], 0.0, [btb])
                    self.memset(tb_[:, b0_ + L_ + 15: b0_ + L_ + 16], 0.0, [btb])
                wt = load_wi(0, ch)
                wo_p = load_wo(ch, 4 + ch)
                if ch == 0:
                    wo_p0 = wo_p

                def ppos(tok):
                    si = 0
                    for i, (c0, ncn, _) in enumerate(seqs):
                        if tok >= c0 * 128:
                            si = i
                    return tok + 16 * si + PADL

                for blk in range(nblk):
                    bk = self.bank()
                    proj_fm(wt, 0, blk, bk)
                    for piece in range(2):
                        t0 = blk * 512 + piece * 256
                        p0 = ppos(t0)
                        self.act(pp[:, p0:p0 + 256], self.ps[bk][:, piece * 256:(piece + 1) * 256], AF.Identity,
                                 [self.psb[bk]], [bpp])
                for (c0, ncn, _) in seqs:
                    L = ncn * 128
                    base = ppos(c0 * 128) - PADL
                    LL = L + 16
                    for half in range(2):
                        gi = ch * 2 + half
                        w = POOL_WINDOWS[gi]
                        ps_ = slice(half * 64, half * 64 + 64)
                        cur, curb = ta, bta
                        self.tt(ta[ps_, base + 1: base + LL], pp[ps_, base: base + LL - 1], pp[ps_, base + 1: base + LL],
                                ALU.add, [bpp], [bta])
                        other, otherb = tb_, btb
                        sh = 1
                        ww = 2
                        while ww < w:
                            self.tt(other[ps_, base + sh: base + LL - sh], cur[ps_, base: base + LL - 2 * sh],
                                    cur[ps_, base + 2 * sh: base + LL], ALU.add, [curb], [otherb])
                            cur, curb, other, otherb = other, otherb, cur, curb
                            sh *= 2
                            ww *= 2
                        e0 = gi * 16
                        self.tt(cur[ps_, base + PADL: base + PADL + 8], cur[ps_, base + PADL: base + PADL + 8],
                                self.cpc("edge", e0, 8)[ps_, :], ALU.mult, [curb, bcp], [curb])
                        self.tt(cur[ps_, base + PADL + L - 8: base + PADL + L], cur[ps_, base + PADL + L - 8: base + PADL + L],
                                self.cpc("edge", e0 + 8, 8)[ps_, :], ALU.mult, [curb, bcp], [curb])
                        self.stt(pooled[ps_, c0 * 128: c0 * 128 + L], cur[ps_, base + PADL: base + PADL + L], 1.0 / w,
                                 pp[ps_, base + PADL: base + PADL + L], ALU.mult, ALU.subtract,
                                 [curb, bpp], [self.sb("pooled")])
                for blk in range(nblk):
                    bk = self.bank()
                    self.mm(self.ps[bk][:], self.poolw_b[:, l * 256 + ch * 128: l * 256 + (ch + 1) * 128],
                            pooled[:, blk * 512:(blk + 1) * 512], True, True,
                            [self.b("poolw_b"), self.sb("pooled")], [self.psb[bk]])
                    self.act(cat[:, blk * 512:(blk + 1) * 512], self.ps[bk][:], AF.Identity, [self.psb[bk], bcp],
                             [self.sb("pcat", ch, blk)], scale=self.cpc(f"pool_scale{l}", ch, 1))
                    if ch == 1:
                        wout_partial([(wo_p0, 4, pcat[0][:, blk * 512:(blk + 1) * 512], self.sb("pcat", 0, blk)),
                                      (wo_p, 5, pcat[1][:, blk * 512:(blk + 1) * 512], self.sb("pcat", 1, blk))], blk)
            self.new_phase(KEEP)
            nsq = len(seqs)
            ZP = T + 30 * nsq
            if (2 * ZP) % 4:
                ZP += 1
            ZB = PH
            DG = ZB + 4 * ZP
            AC = DG + 2 * 31 * 256
            AB = AC + 2 * 2 * 2048
            CC = AB + 4 * 1024
            assert CC + 4 * T <= WI, (CC + 4 * T, WI)
            zb_ = [self.scr_bf(ZB + c * 2 * ZP, ZP) for c in range(2)]
            cat = [self.scr_bf(CC + c * 2 * T, T) for c in range(2)]

            def zpos(tok):
                si = 0
                for i, (c0, ncn, _) in enumerate(seqs):
                    if tok >= c0 * 128:
                        si = i
                return tok + 30 * si + 15

            for c in range(2):
                self.memset(zb_[c], 0.0, [self.sb("z", c)])
                for kk in range(31):
                    dgm = self.scr_bf(DG + (c * 31 + kk) * 256, 128)
                    self.ts(dgm, self.ident_b[:], self.cpc(f"conv_dw{l}", c * 31 + kk, 1), None, ALU.mult, None,
                            [self.b("ident_b"), bcp], [self.sb("dg", c)])
                wa = load_wi(0, 18 + c)
                wg = load_wi(1, 20 + c)
                for blk in range(nblk):
                    ba = self.bank()
                    bg = self.bank()
                    proj_fm(wa, 0, blk, ba)
                    proj_fm(wg, 1, blk, bg)
                    t, tb = self.gettmp()
                    self.act(t[:], self.ps[bg][:], AF.Sigmoid, [self.psb[bg]], [tb])
                    for piece in range(2):
                        p0 = zpos(blk * 512 + piece * 256)
                        self.tt(zb_[c][:, p0:p0 + 256], self.ps[ba][:, piece * 256:(piece + 1) * 256],
                                t[:, piece * 256:(piece + 1) * 256], ALU.mult, [self.psb[ba], tb], [self.sb("z", c)])
            for blk in range(nblk):
                par = blk % 2
                accs = []
                for c in range(2):
                    bk = self.bank()
                    if nsq == 1:
                        pieces = [(0, 512, zpos(blk * 512) - 15)]
                    else:
                        pieces = [(0, 256, zpos(blk * 512) - 15), (256, 256, zpos(blk * 512 + 256) - 15)]
                    for (co, n, zs) in pieces:
                        for kk in range(31):
                            dgm = self.scr_bf(DG + (c * 31 + kk) * 256, 128)
                            self.mm(self.ps[bk][:, co:co + n], dgm, zb_[c][:, zs + kk: zs + kk + n], kk == 0, kk == 30,
                                    [self.sb("dg", c), self.sb("z", c)], [self.psb[bk]])
                    a32 = self.scr_f32(AC + (par * 2 + c) * 2048, 512)
                    a32b = self.sb("a32", par, c)
                    ab = self.scr_bf(AB + (c * 2) * 1024, 512)
                    sq = self.scr_bf(AB + (c * 2 + 1) * 1024, 512)
                    abb = self.sb("ab", c)
                    cb = self.cpc(f"conv_b{l}", c, 1)
                    self.act(a32, self.ps[bk][:], AF.Identity, [self.psb[bk], bcp], [a32b], bias=cb)
                    self.act(ab, self.ps[bk][:], AF.Identity, [self.psb[bk], bcp], [abb], bias=cb)
                    self.act(sq, self.ps[bk][:], AF.Square, [self.psb[bk], bcp], [abb], bias=cb)
                    accs.append((a32, a32b, ab, sq, abb))
                bm_, bv_ = 6, 7
                for c in range(2):
                    self.mm(self.ps[bm_][:], self.ones256_b[:], accs[c][2], c == 0, c == 1,
                            [self.b("ones_b"), accs[c][4]], [self.psb[bm_]])
                for c in range(2):
                    self.mm(self.ps[bv_][:], self.ones256_b[:], accs[c][3], c == 0, c == 1,
                            [self.b("ones_b"), accs[c][4]], [self.psb[bv_]])
                mu, mub = self.gettmp()
                self.act(mu[:], self.ps[bm_][:], AF.Identity, [self.psb[bm_]], [mub])
                var, varb = self.gettmp()
                self.act(var[:], self.ps[bm_][:], AF.Square, [self.psb[bm_]], [varb])
                self.tt(var[:], self.ps[bv_][:], var[:], ALU.subtract, [self.psb[bv_], varb], [varb])
                self.rsqrt(var[:], var[:], [varb], varb, clamp=True)
                for c in range(2):
                    d, db = self.gettmp()
                    self.tt(d[:], accs[c][0], mu[:], ALU.subtract, [accs[c][1], mub], [db])
                    self.tt(d[:], d[:], var[:], ALU.mult, [db, varb], [db])
                    self.act(cat[c][:, blk * 512:(blk + 1) * 512], d[:], AF.Silu, [db, bcp], [self.sb("ccat", c, blk)],
                             scale=self.cpc(f"ln_g{l}", c, 1), bias=self.cpc(f"ln_b{l}", c, 1))
            wo_c = [load_wo(6 + c, 4 + c) for c in range(2)]
            for blk in range(nblk):
                wout_partial([(wo_c[c], 4 + c, cat[c][:, blk * 512:(blk + 1) * 512], self.sb("ccat", c, blk))
                              for c in range(2)], blk)

        nch = T // 128
        QF, QB, KF, KB = PH, PH + 2 * T, PH + 4 * T, PH + 6 * T
        KFT, KBT = PH + 8 * T, PH + 10 * T
        SF, SB = PH + 12 * T, PH + 14 * T
        VT = PH + 16 * T
        RF = PH + 18 * T
        GT = RF + 1024
        ST = GT + 2048
        CAT = ST + 2048
        OB = CAT + 2048
        assert OB + 2048 <= WI, (OB, WI)
        for h in range(4):
            self.new_phase(KEEP)
            qf = self.scr_bf(QF, T)
            qb = self.scr_bf(QB, T)
            kf = self.scr_bf(KF, T)
            kb = self.scr_bf(KB, T)
            kfT = self.scr_bf(KFT, T)
            kbT = self.scr_bf(KBT, T)
            sf = self.scr_bf(SF, T)
            sbk = self.scr_bf(SB, T)
            vT = self.scr_bf(VT, T)
            Rst = self.scr_f32(RF, 256)
            gt = self.scr_f32(GT, 512)
            bgt = self.sb("gt")
            lgf = self.lg[:, l * 8 + h: l * 8 + h + 1]
            lgb = self.lg[:, l * 8 + 4 + h: l * 8 + 4 + h + 1]
            nlgf = self.nlg[:, l * 8 + h: l * 8 + h + 1]
            nlgb = self.nlg[:, l * 8 + 4 + h: l * 8 + 4 + h + 1]
            blg = self.b("lg")
            self.act(gt[:, 0:128], self.cpc("pos1"), AF.Exp, [bcp, blg], [bgt], scale=lgf)
            self.act(gt[:, 128:256], self.cpc("posr"), AF.Exp, [bcp, blg], [bgt], scale=lgb)
            self.act(gt[:, 256:257], self.cpc("pos1c"), AF.Exp, [bcp, blg], [bgt], scale=nlgf)
            self.act(gt[:, 257:258], self.cpc("posrc"), AF.Exp, [bcp, blg], [bgt], scale=nlgb)
            ksc = 128.0 ** -0.5
            self.ts(gt[:, 256:258], gt[:, 256:258], ksc, None, ALU.mult, None, [bgt], [bgt])
            wkcol = (gt[:, 256:257], gt[:, 257:258])
            wq = load_wi(0, 2 + h)
            wk = load_wi(1, 6 + h)
            wv = load_wi(2, 10 + h)
            for blk in range(nblk):
                if rope:
                    rs_ = self.slot("rope")
                    rt = self.ropes[rs_]
                    rtb = self.b("rope", rs_)
                    self.dma("sp", rt[:, :].rearrange("p (a t) -> p a t", a=2),
                             self.rope_d[:, :, blk * 512:(blk + 1) * 512], f"rope{rs_}", [], [rtb])
                items = []
                for which, wt, slot, outs in (("q", wq, 0, ((qf, 0), (qb, 128))), ("k", wk, 1, ())):
                    if not full and which == "q":
                        continue
                    bk = self.bank()
                    proj_fm(wt, slot, blk, bk)
                    if rope:
                        q32, q32b = self.gettmp()
                        self.act(q32[:], self.ps[bk][:], AF.Identity, [self.psb[bk]], [q32b])
                        items.append((which, outs, q32, q32b))
                    else:
                        items.append((which, outs, self.ps[bk], self.psb[bk]))
                for (which, outs, q32, q32b) in items:
                    if rope:
                        bp = self.bank()
                        self.mm(self.ps[bp][:], self.cpc("perm"), q32[:], True, True, [bcp, q32b], [self.psb[bp]])
                        t2, t2b = self.gettmp()
                        self.tt(t2[:], self.ps[bp][:], rt[:, 512:1024], ALU.mult, [self.psb[bp], rtb], [t2b])
                        self.tt(q32[:], q32[:], rt[:, 0:512], ALU.mult, [q32b, rtb], [q32b])
                        if which == "k":
                            self.tt(kf[:, blk * 512:(blk + 1) * 512], q32[:], t2[:], ALU.add, [q32b, t2b],
                                    [self.sb("k", 256, blk)])
                            continue
                        self.tt(q32[:], q32[:], t2[:], ALU.add, [q32b, t2b], [q32b])
                    elif which == "k":
                        self.cpy(kf[:, blk * 512:(blk + 1) * 512], q32[:], [q32b], [self.sb("k", 256, blk)])
                        continue
                    src, srcb = q32[:], q32b
                    for (dst, go) in outs:
                        o3 = dst[:, blk * 512:(blk + 1) * 512].rearrange("p (c i) -> p c i", c=4)
                        i3 = src.rearrange("p (c i) -> p c i", c=4)
                        g3 = gt[:, go:go + 128].unsqueeze(1).to_broadcast([128, 4, 128])
                        self.tt(o3, i3, g3, ALU.mult, [srcb, bgt], [self.sb(which, go, blk)])
            for cgp in range(nch // 4):
                bk = self.bank()
                for cc in range(4):
                    n = cgp * 4 + cc
                    blk = n // 4
                    for k in range(KC):
                        self.mm(self.ps[bk][:, cc * 128:(cc + 1) * 128], self.scr_bf(HT + k * (2 * T) + n * 256, 128), wv[:, k * 128:(k + 1) * 128],
                                k == 0, k == KC - 1, [self.sb("h", k, blk), self.sb("wi", 2)], [self.psb[bk]])
                self.act(vT[:, cgp * 512:(cgp + 1) * 512], self.ps[bk][:], AF.Identity, [self.psb[bk]], [self.sb("vT", cgp)])
            for cgp in range(nch // 4):
                bk = self.bank()
                pst = self.ps[bk][:, :].bitcast(BF16)
                for cc in range(4):
                    n = cgp * 4 + cc
                    self.tr(pst[:, cc * 128:(cc + 1) * 128], kf[:, n * 128:(n + 1) * 128], self.ident_b[:],
                            [self.sb("k", 256, n // 4), self.b("ident_b")], [self.psb[bk]])
                self.act(kfT[:, cgp * 512:(cgp + 1) * 512], pst[:, 0:512], AF.Identity, [self.psb[bk], bgt],
                         [self.sb("kfT", cgp)], scale=wkcol[0])
                self.act(kbT[:, cgp * 512:(cgp + 1) * 512], pst[:, 0:512], AF.Identity, [self.psb[bk], bgt],
                         [self.sb("kbT", cgp)], scale=wkcol[1])
            if full:
                wg = load_wi(3, 14 + h)
                for blk in range(nblk):
                    bg = self.bank()
                    proj_fm(wg, 3, blk, bg)
                    self.act(self.sgall[:, blk * 512:(blk + 1) * 512], self.ps[bg][:], AF.Silu, [self.psb[bg]],
                             [self.b("sg", blk)])
            for (c0, ncn, bi_) in seqs:
                for idx in range(ncn):
                    for d_ in range(2):
                        kT, nm = (kfT, "kfT") if d_ == 0 else (kbT, "kbT")
                        sdst = sf if d_ == 0 else sbk
                        g128c = self.g128[:, l * 8 + d_ * 4 + h: l * 8 + d_ * 4 + h + 1]
                        R = Rst[:, d_ * 128:(d_ + 1) * 128]
                        Rb = self.sb("R", d_)
                        st_off = (((l * 2 + bi_) * 2 + d_) * 4 + h) * 128
                        S0 = self.states[:, st_off: st_off + 128]
                        S0b = self.b("state", l, bi_, d_, h)
                        n = c0 + idx if d_ == 0 else c0 + ncn - 1 - idx
                        if full:
                            if idx == 0:
                                if rope:
                                    self.cpy(sdst[:, n * 128:(n + 1) * 128], S0, [S0b], [self.sb("S", d_, n)])
                                else:
                                    self.memset(sdst[:, n * 128:(n + 1) * 128], 0.0, [self.sb("S", d_, n)])
                            else:
                                self.act(sdst[:, n * 128:(n + 1) * 128], R, AF.Identity, [Rb, blg], [self.sb("S", d_, n)],
                                         scale=g128c)
                        last = idx == ncn - 1
                        if last and rope:
                            continue
                        bk = self.bank()
                        self.mm(self.ps[bk][:, 0:128], kT[:, n * 128:(n + 1) * 128], vT[:, n * 128:(n + 1) * 128], True, True,
                                [self.sb(nm, n // 4), self.sb("vT", n // 4)], [self.psb[bk]])
                        if idx == 0:
                            if rope:
                                self.tt(R, self.ps[bk][:, 0:128], S0, ALU.add, [self.psb[bk], S0b], [Rb])
                            else:
                                self.cpy(R, self.ps[bk][:, 0:128], [self.psb[bk]], [Rb])
                        else:
                            self.stt(R, R, g128c, self.ps[bk][:, 0:128], ALU.mult, ALU.add, [Rb, blg, self.psb[bk]], [Rb])
                        if last and not rope:
                            self.act(S0, R, AF.Identity, [Rb, blg], [S0b], scale=g128c)
            if not full:
                continue
            odd = h % 2 == 1
            if odd:
                wo_prev = load_wo(2 + h - 1, 4)
                wo_cur = load_wo(2 + h, 5)

            def stageA1(blk):
                sts = []
                for d_ in range(2):
                    kk_, kgo = kf, 256
                    qq_, qgo = (qf, 0) if d_ == 0 else (qb, 128)
                    bk = self.bank()
                    for cc in range(4):
                        n = blk * 4 + cc
                        self.mm(self.ps[bk][:, cc * 128:(cc + 1) * 128], kk_[:, n * 128:(n + 1) * 128],
                                qq_[:, n * 128:(n + 1) * 128], True, True,
                                [self.sb("k", kgo, blk), self.sb("q", qgo, blk)], [self.psb[bk]])
                    sT = self.scr_bf(ST + d_ * 1024, 512)
                    mk = self.cpc("maskf" if d_ == 0 else "maskb").unsqueeze(1).to_broadcast([128, 4, 128])
                    self.stt(sT.rearrange("p (c i) -> p c i", c=4), self.ps[bk][:, :].rearrange("p (c i) -> p c i", c=4),
                             wkcol[d_], mk, ALU.mult, ALU.mult, [self.psb[bk], bcp, bgt], [self.sb("sT", d_)])
                    sts.append(sT)
                bo = 4 + blk % 2
                for cc in range(4):
                    n = blk * 4 + cc
                    oc = self.ps[bo][:, cc * 128:(cc + 1) * 128]
                    vch = vT[:, n * 128:(n + 1) * 128]
                    self.mm(oc, vch, sts[0][:, cc * 128:(cc + 1) * 128], True, False,
                            [self.sb("vT", n // 4), self.sb("sT", 0)], [self.psb[bo]])
                    self.mm(oc, vch, sts[1][:, cc * 128:(cc + 1) * 128], False, False,
                            [self.sb("vT", n // 4), self.sb("sT", 1)], [self.psb[bo]])
                    self.mm(oc, sf[:, n * 128:(n + 1) * 128], qf[:, n * 128:(n + 1) * 128], False, False,
                            [self.sb("S", 0, n), self.sb("q", 0, blk)], [self.psb[bo]])
                    self.mm(oc, sbk[:, n * 128:(n + 1) * 128], qb[:, n * 128:(n + 1) * 128], False, True,
                            [self.sb("S", 1, n), self.sb("q", 128, blk)], [self.psb[bo]])
                return bo

            def stageA2(blk, bo):
                par = blk % 2
                rb = self.b("rope", par)
                o32 = self.ropes[par][:, 0:512]
                mu = self.ropes[par][:, 512:1024]
                var, varb = self.rstd[par], self.rstdb[par]
                self.act(o32, self.ps[bo][:], AF.Identity, [self.psb[bo]], [rb])
                ob = self.scr_bf(OB, 512)
                osq = self.scr_bf(OB + 1024, 512)
                obb = self.sb("ob")
                self.act(ob, self.ps[bo][:], AF.Identity, [self.psb[bo]], [obb])
                self.act(osq, self.ps[bo][:], AF.Square, [self.psb[bo]], [obb])
                self.mm(self.ps[6][:], self.ones128_b[:], ob, True, True, [self.b("ones_b"), obb], [self.psb[6]])
                self.mm(self.ps[7][:], self.ones128_b[:], osq, True, True, [self.b("ones_b"), obb], [self.psb[7]])
                self.act(mu, self.ps[6][:], AF.Identity, [self.psb[6]], [rb])
                self.act(var[:], self.ps[6][:], AF.Square, [self.psb[6]], [varb])
                self.tt(var[:], self.ps[7][:], var[:], ALU.subtract, [self.psb[7], varb], [varb])
                self.rsqrt(var[:], var[:], [varb], varb, clamp=True)

            def stageB(blk):
                par = blk % 2
                rb = self.b("rope", par)
                o32 = self.ropes[par][:, 0:512]
                mu = self.ropes[par][:, 512:1024]
                var, varb = self.rstd[par], self.rstdb[par]
                self.tt(o32, o32, mu, ALU.subtract, [rb], [rb])
                self.tt(o32, o32, var[:], ALU.mult, [rb, varb], [rb])
                if odd:
                    cat, catb = self.scr_bf(CAT + par * 1024, 512), self.sb("hcat", par)
                else:
                    cat, catb = self.hprev[:, blk * 512:(blk + 1) * 512], self.b("hprev", blk)
                self.stt(cat, o32, self.cpc(f"gn_g{l}", h, 1), self.sgall[:, blk * 512:(blk + 1) * 512], ALU.mult, ALU.mult,
                         [rb, bcp, self.b("sg", blk)], [catb])

            def stageC(blk):
                if not odd:
                    return
                par = blk % 2
                wout_partial([(wo_prev, 4, self.hprev[:, blk * 512:(blk + 1) * 512], self.b("hprev", blk)),
                              (wo_cur, 5, self.scr_bf(CAT + par * 1024, 512), self.sb("hcat", par))], blk)

            self.bank_set = [0, 1, 2, 3]
            bo_ = stageA1(0)
            stageA2(0, bo_)
            for blk in range(nblk):
                if blk + 1 < nblk:
                    bo_ = stageA1(blk + 1)
                stageB(blk)
                stageC(blk)
                if blk + 1 < nblk:
                    stageA2(blk + 1, bo_)
            self.bank_set = [0, 1, 2, 3, 4, 5]
        self.new_phase()

    def final_norm(self, nblk):
        for blk in range(nblk):
            rs, rsb = self.rms_rstd(blk)
            for k in range(KC):
                xs = self.xv(k, blk * 512, 512)
                self.stt(xs, xs, self.cpc("final_g", k, 1), rs[:], ALU.mult, ALU.mult,
                         [self.xb(k, blk), self.b("cp"), rsb], [self.xb(k, blk)])

    def build(self):
        stop = self.dbg_stop
        self.prologue()
        ctx_seqs = [(0, 2, 0), (2, 2, 1)]
        self.load_seg(self.ctx_d, 512)
        stage = 0
        done = False
        for l in range(NL):
            last = l == NL - 1
            self.ffn(l, 0, 1, 0)
            stage += 1
            if stop == ("c", stage):
                done = True
                break
            self.mixer(l, 1, 0, ctx_seqs, rope=False, full=not last, bidx=None)
            stage += 1
            if stop == ("c", stage):
                done = True
                break
            if not last:
                self.ffn(l, 1, 1, 0)
                stage += 1
                if stop == ("c", stage):
                    done = True
                    break
        if stop is not None and stop[0] == "c":
            self.store_seg(self.dbgy_d, 512)
        else:
            for bi in range(2):
                self.load_seg(self.x_d[bi], SEQ)
                stage = 0
                done = False
                for l in range(NL):
                    self.ffn(l, 0, 4, 1 + bi)
                    stage += 1
                    if stop == ("x", stage):
                        done = True
                        break
                    self.mixer(l, 4, 1 + bi, [(0, 16, bi)], rope=True, full=True, bidx=bi)
                    stage += 1
                    if stop == ("x", stage):
                        done = True
                        break
                    self.ffn(l, 1, 4, 1 + bi)
                    stage += 1
                    if stop == ("x", stage):
                        done = True
                        break
                if not done:
                    self.final_norm(4)
                self.store_seg(self.out_d[bi], SEQ)
        outs = [self.b("outd", i) for i in range(8)]
        self.P.op("sp", lambda h: h.nop(), reads=outs)
        self.P.emit()
        return self.nc


_CACHE = {}


def kernel(x, c, ctx, c_ctx, w_mod, b_mod, norm_g, ffn_w1, ffn_w3, ffn_w2, w_in, w_out,
           pool_w, pool_scale, ret_decay_fwd, ret_decay_bwd, ret_gn_g, conv_dw, conv_b,
           conv_ln_g, conv_ln_b, final_g, _dbg_stop=None, _cores=None):
    inp = dict(x=x, c=c, ctx=ctx, c_ctx=c_ctx, w_mod=w_mod, b_mod=b_mod, norm_g=norm_g, ffn_w1=ffn_w1,
               ffn_w3=ffn_w3, ffn_w2=ffn_w2, w_in=w_in, w_out=w_out, pool_w=pool_w, pool_scale=pool_scale,
               ret_decay_fwd=ret_decay_fwd, ret_decay_bwd=ret_decay_bwd, ret_gn_g=ret_gn_g, conv_dw=conv_dw,
               conv_b=conv_b, conv_ln_g=conv_ln_g, conv_ln_b=conv_ln_b, final_g=final_g)
    inp = {k: np.asarray(v) for k, v in inp.items()}
    cores = list(range(NCORES)) if _cores is None else _cores
    W = relayout_weights(inp)
    rope = build_rope()
    poolw = build_poolw(inp)
    k = K(dbg_stop=_dbg_stop)
    nc = k.build()
    in_maps = []
    xs = np.asarray(inp["x"], np.float32)
    cs = np.asarray(inp["ctx"], np.float32)
    for core in cores:
        m = dict(W)
        m["x"] = np.ascontiguousarray(xs[2 * core: 2 * core + 2])
        m["ctx"] = np.ascontiguousarray(cs[2 * core: 2 * core + 2].reshape(2 * CTX, D))
        m["cp"], m["cp2"] = build_cp(inp, core)
        m["rope"] = rope
        m["poolw"] = poolw
        in_maps.append(m)
    res = run_bass_kernel_spmd(nc, in_maps, core_ids=list(range(len(cores))))
    if _dbg_stop is not None:
        return res.results
    out = np.concatenate([np.asarray(r["out"], np.float32) for r in res.results], axis=0)
    return out
```

```python
import numpy as np
import concourse.bass as bass
import concourse.mybir as mybir
from concourse.bass_utils import run_bass_kernel_spmd

F32 = mybir.dt.float32
BF16 = mybir.dt.bfloat16
ALU = mybir.AluOpType
AF = mybir.ActivationFunctionType

D = 1024
KC = 8
DFF = 2816
FC = 22
SEQ = 2048
CTX = 256
NL = 2
EPS = 1e-6
NCORES = 8


class Buf:
    __slots__ = ("name", "last_w", "readers")

    def __init__(self, name, inherit=None):
        self.name = name
        self.last_w = None
        self.readers = dict(inherit) if inherit else {}


class Tok:
    __slots__ = ("key", "ord", "clock", "op")

    def __init__(self, key, ord_, clock, op):
        self.key = key
        self.ord = ord_
        self.clock = clock
        self.op = op


class Op:
    __slots__ = ("eng", "fn", "waits", "tok", "signal", "semval", "dma_sem")


class Prog:
    ENGS = ("pe", "act", "dve", "pool", "sp")

    def __init__(self, nc):
        self.nc = nc
        self.h = {"pe": nc.tensor, "act": nc.scalar, "dve": nc.vector,
                  "pool": nc.gpsimd, "sp": nc.sync}
        self.ops = []
        self.nops = {e: 0 for e in self.ENGS}
        self.seen = {e: {} for e in self.ENGS}
        self.dma_count = {}
        self.sems = {}

    def op(self, eng, fn, reads=(), writes=(), dma=None):
        seen = self.seen[eng]
        deps = {}

        def add(t, raw):
            if t is None:
                return
            if t.key == ("e", eng) and eng in ("pe", "sp"):
                return
            k = t.key
            if k not in deps or deps[k].ord < t.ord:
                deps[k] = t

        for b in reads:
            add(b.last_w, True)
        for b in writes:
            add(b.last_w, True)
            for t in b.readers.values():
                add(t, False)
        o = Op()
        o.eng = eng
        o.fn = fn
        o.waits = []
        o.signal = False
        o.semval = None
        o.dma_sem = dma
        for k, t in deps.items():
            if seen.get(k, 0) >= t.ord:
                continue
            o.waits.append(t)
            if t.op is not None:
                t.op.signal = True
            seen[k] = t.ord
            for kk, vv in t.clock.items():
                if seen.get(kk, 0) < vv:
                    seen[kk] = vv
        if dma is None:
            self.nops[eng] += 1
            tok = Tok(("e", eng), self.nops[eng], dict(seen), o)
        else:
            self.dma_count[dma] = self.dma_count.get(dma, 0) + 16
            tok = Tok(("d", dma), self.dma_count[dma], dict(seen), None)
        o.tok = tok
        self.ops.append(o)
        for b in writes:
            b.last_w = tok
            b.readers = {}
        for b in reads:
            if b in writes:
                continue
            b.readers[tok.key] = tok
        return tok

    def _sem(self, name):
        if name not in self.sems:
            self.sems[name] = self.nc.alloc_semaphore(name)
        return self.sems[name]

    def emit(self):
        cnt = {e: 0 for e in self.ENGS}
        for o in self.ops:
            h = self.h[o.eng]
            for t in o.waits:
                if t.key[0] == "e":
                    h.wait_ge(self._sem("s_" + t.key[1]), t.op.semval)
                else:
                    h.wait_ge(self._sem("d_" + t.key[1]), t.ord)
            ins = o.fn(h)
            if o.dma_sem is not None:
                ins.then_inc(self._sem("d_" + o.dma_sem), 16)
            elif o.signal:
                cnt[o.eng] += 1
                o.semval = cnt[o.eng]
                ins.then_inc(self._sem("s_" + o.eng), 1)


def _cp_layout():
    off = {}
    c = 0

    def add(name, n):
        nonlocal c
        off[name] = (c, n)
        c += n

    add("ident", 128)
    add("perm", 128)
    add("maskf", 128)
    add("maskb", 128)
    add("pos1", 128)
    add("posr", 128)
    add("pos1c", 1)
    add("posrc", 1)
    add("eps", 1)
    add("one", 1)
    add("edge", 4 * 16)
    add("final_g", 8)
    for l in range(NL):
        add(f"pool_scale{l}", 2)
        add(f"gn_g{l}", 4)
        add(f"conv_b{l}", 2)
        add(f"ln_g{l}", 2)
        add(f"ln_b{l}", 2)
        add(f"conv_dw{l}", 62)
        add(f"dec{l}", 8)
    return off, c


def _cp2_layout():
    off = {}
    c = 0
    for name, n in (("ones1024", 128), ("ones128", 128), ("ones256", 128), ("cvec", 24),
                    ("normg0", 72), ("bmod0", 216), ("normg1", 72), ("bmod1", 216)):
        off[name] = (c, n)
        c += n
    return off, c


CP_OFF, CP_N = _cp_layout()
CP2_OFF, CP2_N = _cp2_layout()
POOL_WINDOWS = (2, 4, 8, 16)


def _fm(v):
    return np.ascontiguousarray(np.asarray(v, np.float32).reshape(-1, 128).T)


def build_cp(inp, core):
    cp = np.zeros((128, CP_N), np.float32)
    cp2 = np.zeros((128, CP2_N), np.float32)

    def put(name, arr):
        if name in CP2_OFF:
            o, n = CP2_OFF[name]
            cp2[:, o:o + n] = np.asarray(arr, np.float32).reshape(128, n)
            return
        o, n = CP_OFF[name]
        arr = np.asarray(arr, np.float32).reshape(128, n)
        cp[:, o:o + n] = arr

    put("ident", np.eye(128, dtype=np.float32))
    perm = np.zeros((128, 128), np.float32)
    for m in range(128):
        partner = m + 32 if (m % 64) < 32 else m - 32
        perm[partner, m] = 1.0
    put("perm", perm)
    put("ones1024", np.full((128, 128), 1.0 / 1024, np.float32))
    put("ones128", np.full((128, 128), 1.0 / 128, np.float32))
    put("ones256", np.full((128, 128), 1.0 / 256, np.float32))
    jj = np.arange(128)[:, None]
    ii = np.arange(128)[None, :]
    put("maskf", (ii >= jj).astype(np.float32))
    put("maskb", (jj > ii).astype(np.float32))
    put("pos1", np.tile(np.arange(1, 129, dtype=np.float32)[None, :], (128, 1)))
    put("posr", np.tile((128 - np.arange(128, dtype=np.float32))[None, :], (128, 1)))
    put("pos1c", np.arange(1, 129, dtype=np.float32)[:, None])
    put("posrc", (128 - np.arange(128, dtype=np.float32))[:, None])
    put("eps", np.full((128, 1), EPS, np.float32))
    put("one", np.ones((128, 1), np.float32))
    edge = np.zeros((4, 16), np.float32)
    for gi, w in enumerate(POOL_WINDOWS):
        for t in range(8):
            cl = min(t + w // 2, w) if True else w
            cl = (t + w // 2) - max(t - w // 2, 0)
            edge[gi, t] = w / cl
            d = 8 - t
            cr = min(d, w // 2) + w // 2
            edge[gi, 8 + t] = w / cr
    put("edge", np.tile(edge.reshape(1, 64), (128, 1)))
    b0 = 2 * core
    cv = np.stack([np.asarray(inp["c_ctx"], np.float32),
                   np.asarray(inp["c"][b0], np.float32),
                   np.asarray(inp["c"][b0 + 1], np.float32)], axis=0)
    put("cvec", cv.reshape(3, 8, 128).transpose(2, 1, 0).reshape(128, 24))
    put("final_g", _fm(inp["final_g"]))
    for l in range(NL):
        ng = np.asarray(inp["norm_g"][l], np.float32).reshape(3, 8, 128).transpose(2, 0, 1)
        put(f"normg{l}", np.repeat(ng[:, :, :, None], 3, axis=3).reshape(128, 72))
        bm = _fm(inp["b_mod"][l])
        put(f"bmod{l}", np.repeat(bm[:, :, None], 3, axis=2).reshape(128, 216))
        put(f"pool_scale{l}", _fm(inp["pool_scale"][l]))
        put(f"gn_g{l}", _fm(inp["ret_gn_g"][l]))
        put(f"conv_b{l}", _fm(inp["conv_b"][l]))
        put(f"ln_g{l}", _fm(inp["conv_ln_g"][l]))
        put(f"ln_b{l}", _fm(inp["conv_ln_b"][l]))
        dw = np.asarray(inp["conv_dw"][l], np.float32)
        put(f"conv_dw{l}", dw.reshape(31, 2, 128).transpose(2, 1, 0).reshape(128, 62))
        dec = np.concatenate([np.asarray(inp["ret_decay_fwd"][l], np.float32),
                              np.asarray(inp["ret_decay_bwd"][l], np.float32)])
        put(f"dec{l}", np.tile(dec[None, :], (128, 1)))
    return cp, cp2


def build_poolw(inp):
    out = np.zeros((128, NL, 2, 128), np.float32)
    for l in range(NL):
        pw = np.asarray(inp["pool_w"][l], np.float32)
        for ch in range(2):
            out[0:64, l, ch, 0:64] = pw[2 * ch]
            out[64:128, l, ch, 64:128] = pw[2 * ch + 1]
    return out.reshape(128, NL * 256)


def build_rope():
    n_freq = 32
    inv = (10000.0 ** (-np.arange(n_freq, dtype=np.float32) / n_freq)).astype(np.float32)
    t = np.arange(SEQ)
    row = (t // 64).astype(np.float32)
    col = (t % 64).astype(np.float32)
    tab = np.zeros((128, 2, SEQ), np.float32)
    for f in range(128):
        pos = row if f < 64 else col
        ang = (pos * inv[f % 32]).astype(np.float32)
        tab[f, 0] = np.cos(ang)
        s = np.sin(ang)
        tab[f, 1] = -s if (f % 64) < 32 else s
    return tab


def relayout_weights(inp):
    w = {}
    w1 = np.asarray(inp["ffn_w1"], np.float32).reshape(4, 8, 128, 11, 256)
    w["w1"] = np.ascontiguousarray(w1.transpose(0, 3, 2, 1, 4)).reshape(44, 128, 2048)
    w3 = np.asarray(inp["ffn_w3"], np.float32).reshape(4, 8, 128, 11, 256)
    w["w3"] = np.ascontiguousarray(w3.transpose(0, 3, 2, 1, 4)).reshape(44, 128, 2048)
    w2 = np.asarray(inp["ffn_w2"], np.float32).reshape(4, 22, 128, 8, 128)
    w["w2"] = np.ascontiguousarray(w2.transpose(0, 3, 2, 1, 4)).reshape(32, 128, 2816)
    wi = np.asarray(inp["w_in"], np.float32).reshape(2, 8, 128, 22, 128)
    w["win"] = np.ascontiguousarray(wi.transpose(0, 3, 2, 1, 4)).reshape(44, 128, 1024)
    w["wout"] = np.ascontiguousarray(np.asarray(inp["w_out"], np.float32).reshape(16, 128, 1024))
    wm = np.asarray(inp["w_mod"], np.float32).reshape(2, 8, 128, 18, 512)
    w["wmod"] = np.ascontiguousarray(wm.transpose(0, 3, 2, 1, 4)).reshape(36, 128, 4096)
    return w


class K:
    def __init__(self, dbg_stop=None):
        self.dbg_stop = dbg_stop
        nc = bass.Bass("TRN2", target_bir_lowering=False)
        self.nc = nc
        self.P = Prog(nc)
        dt = nc.dram_tensor
        self.x_d = dt("x", [2, SEQ, D], F32, kind="ExternalInput").ap()
        self.ctx_d = dt("ctx", [2 * CTX, D], F32, kind="ExternalInput").ap()
        self.cp_d = dt("cp", [128, CP_N], F32, kind="ExternalInput").ap()
        self.cp2_d = dt("cp2", [128, CP2_N], F32, kind="ExternalInput").ap()
        self.rope_d = dt("rope", [128, 2, SEQ], F32, kind="ExternalInput").ap()
        self.poolw_d = dt("poolw", [128, NL * 256], F32, kind="ExternalInput").ap()
        self.w1_d = dt("w1", [44, 128, 2048], F32, kind="ExternalInput").ap()
        self.w3_d = dt("w3", [44, 128, 2048], F32, kind="ExternalInput").ap()
        self.w2_d = dt("w2", [32, 128, 2816], F32, kind="ExternalInput").ap()
        self.win_d = dt("win", [44, 128, 1024], F32, kind="ExternalInput").ap()
        self.wout_d = dt("wout", [16, 128, 1024], F32, kind="ExternalInput").ap()
        self.wmod_d = dt("wmod", [36, 128, 4096], F32, kind="ExternalInput").ap()
        self.out_d = dt("out", [2, SEQ, D], F32, kind="ExternalOutput").ap()
        if dbg_stop is not None:
            self.dbgy_d = dt("dbgy", [2 * CTX, D], F32, kind="ExternalOutput").ap()

        A = nc.alloc_sbuf_tensor
        self.xT = A("xT", [128, KC * SEQ], F32)
        self.cp = A("cp_s", [128, CP_N], F32)
        self.mod = A("mod", [128, NL * 216], F32)
        self.gs = A("gs", [128, NL * 72], F32)
        self.hg = A("hg", [128, NL * 72], F32)
        self.sc = A("sc", [128, 24], F32)
        self.sc_b = A("sc_b", [128, 24], BF16)
        self.lg = A("lg", [128, NL * 8], F32)
        self.nlg = A("nlg", [128, NL * 8], F32)
        self.g128 = A("g128", [128, NL * 8], F32)
        self.ones_b = A("ones_b", [128, 128], BF16)
        self.ident_b = A("ident_b", [128, 128], BF16)
        self.ones128_b = A("ones128_b", [128, 128], BF16)
        self.ones256_b = A("ones256_b", [128, 128], BF16)
        self.poolw_b = A("poolw_b", [128, NL * 256], BF16)
        self.states = A("states", [128, NL * 2 * 2 * 4 * 128], BF16)
        self.sgall = A("sgall", [128, SEQ], BF16)
        self.hprev = A("hprev", [128, SEQ], BF16)
        self.NTMP = 7
        self.tmp = [A(f"tmp{i}", [128, 512], F32) for i in range(self.NTMP)]
        self.sqb = [A(f"sqb{i}", [128, 512], BF16) for i in range(2)]
        self.rstd = [A(f"rstd{i}", [128, 512], F32) for i in range(2)]
        self.rstdb = [Buf(f"rstd{i}") for i in range(2)]
        self.rstd_i = 0
        self.ropes = [A(f"rope{i}", [128, 2 * 512], F32) for i in range(2)]
        self.SCR_BYTES = 89088 + 2048
        self.scr = A("scr", [128, self.SCR_BYTES // 4], F32)
        self.ps = [nc.alloc_psum_tensor(f"ps{i}", [128, 512], F32) for i in range(8)]
        self.psb = [Buf(f"ps{i}") for i in range(8)]
        self.bank_i = 0
        self.bank_set = [0, 1, 2, 3, 4, 5]
        self.tmp_i = 0
        self.tmpb = [Buf(f"tmp{i}") for i in range(self.NTMP)]
        self.sqb_i = 0
        self.sqbb = [Buf(f"sqb{i}") for i in range(2)]
        self.bufs = {}
        self.scr_bufs = {}
        self.inherit = {}
        self.slot_ctr = {}

    def b(self, *key):
        if key not in self.bufs:
            self.bufs[key] = Buf(str(key))
        return self.bufs[key]

    def sb(self, *key):
        if key not in self.scr_bufs:
            self.scr_bufs[key] = Buf(str(key), self.inherit)
        return self.scr_bufs[key]

    def new_phase(self, keep=()):
        keepd = {}
        for key, bf in self.scr_bufs.items():
            if key[0] in keep:
                keepd[key] = bf
                continue
            toks = list(bf.readers.values())
            if bf.last_w is not None:
                toks.append(bf.last_w)
            for t in toks:
                if t.key not in self.inherit or self.inherit[t.key].ord < t.ord:
                    self.inherit[t.key] = t
        self.scr_bufs = keepd

    def scr_f32(self, off_bytes, n):
        assert off_bytes % 4 == 0 and off_bytes + 4 * n <= self.SCR_BYTES, (off_bytes, n)
        return self.scr[:, off_bytes // 4: off_bytes // 4 + n]

    def scr_bf(self, off_bytes, n):
        assert off_bytes % 4 == 0 and n % 2 == 0 and off_bytes + 2 * n <= self.SCR_BYTES, (off_bytes, n)
        return self.scr[:, off_bytes // 4: off_bytes // 4 + n // 2].bitcast(BF16)

    def bank(self):
        bs = self.bank_set
        self.bank_i = (self.bank_i + 1) % len(bs)
        return bs[self.bank_i]

    def gettmp(self):
        i = self.tmp_i
        self.tmp_i = (i + 1) % self.NTMP
        return self.tmp[i], self.tmpb[i]

    def getsq(self):
        i = self.sqb_i
        self.sqb_i = (i + 1) % 2
        return self.sqb[i], self.sqbb[i]

    def cpc(self, name, a=0, n=None):
        o, nn = CP_OFF[name]
        if n is None:
            n = nn - a
        return self.cp[:, o + a: o + a + n]

    def mm(self, out, lhsT, rhs, start, stop, R, W):
        self.P.op("pe", lambda h: h.matmul(out, lhsT, rhs, start=start, stop=stop), reads=R, writes=W)

    def tr(self, out, in_, ident, R, W):
        self.P.op("pe", lambda h: h.transpose(out, in_, ident), reads=R, writes=W)

    def act(self, out, in_, func, R, W, scale=1.0, bias=0.0):
        self.P.op("act", lambda h: h.activation(out=out, in_=in_, func=func, bias=bias, scale=scale),
                  reads=R, writes=W)

    def tt(self, out, in0, in1, op, R, W, eng="dve"):
        self.P.op(eng, lambda h: h.tensor_tensor(out=out, in0=in0, in1=in1, op=op), reads=R, writes=W)

    def stt(self, out, in0, scalar, in1, op0, op1, R, W, eng="dve"):
        self.P.op(eng, lambda h: h.scalar_tensor_tensor(out=out, in0=in0, scalar=scalar, in1=in1,
                                                        op0=op0, op1=op1), reads=R, writes=W)

    def ts(self, out, in0, s1, s2, op0, op1, R, W, eng="dve"):
        if s2 is None:
            self.P.op(eng, lambda h: h.tensor_scalar(out=out, in0=in0, scalar1=s1, scalar2=None, op0=op0),
                      reads=R, writes=W)
        else:
            self.P.op(eng, lambda h: h.tensor_scalar(out=out, in0=in0, scalar1=s1, scalar2=s2, op0=op0, op1=op1),
                      reads=R, writes=W)

    def cpy(self, out, in_, R, W, eng="dve"):
        self.P.op(eng, lambda h: h.tensor_copy(out=out, in_=in_), reads=R, writes=W)

    def rsqrt(self, out, in_, R, wb, clamp=False):
        if clamp:
            self.act(out, in_, AF.Relu, list(R), [wb])
            self.act(out, out, AF.Ln, [wb, self.b("cp")], [wb], bias=self.cpc("eps"))
        else:
            self.act(out, in_, AF.Ln, list(R) + [self.b("cp")], [wb], bias=self.cpc("eps"))
        self.act(out, out, AF.Exp, [wb], [wb], scale=-0.5)

    def recip(self, out, in_, R, W):
        self.P.op("dve", lambda h: h.reciprocal(out=out, in_=in_), reads=R, writes=W)

    def memset(self, ap, val, W, eng="dve"):
        self.P.op(eng, lambda h: h.memset(ap, val), writes=W)

    def dma(self, q, out, in_, sem, R, W):
        self.P.op(q, lambda h: h.dma_start(out=out, in_=in_), reads=R, writes=W, dma=sem)

    def xv(self, k, t0, n):
        return self.xT[:, k * SEQ + t0: k * SEQ + t0 + n]

    def xb(self, k, blk):
        return self.b("x", k, blk)

    def modcol(self, l, j, k, grp):
        c = l * 216 + (j * 8 + k) * 3 + grp
        return self.mod[:, c:c + 1]

    def gscol(self, l, jn, k, grp):
        c = l * 72 + (jn * 8 + k) * 3 + grp
        return self.gs[:, c:c + 1]

    def hgcol(self, l, jn, k, grp):
        c = l * 72 + (jn * 8 + k) * 3 + grp
        return self.hg[:, c:c + 1]

    def prologue(self):
        P = self.P
        bcp = self.b("cp")
        self.dma("sp", self.cp[:], self.cp_d, "cp", [], [bcp])
        cp2 = self.scr_f32(32 * 1024, CP2_N)
        bcp2 = self.sb("cp2")
        self.dma("sp", cp2, self.cp2_d, "cp2", [], [bcp2])

        def c2(name, a=0, n=None):
            o, nn = CP2_OFF[name]
            if n is None:
                n = nn - a
            return cp2[:, o + a: o + a + n]
        self.cpy(self.ones_b[:], c2("ones1024"), [bcp2], [self.b("ones_b")])
        self.cpy(self.ident_b[:], self.cpc("ident"), [bcp], [self.b("ident_b")])
        self.cpy(self.ones128_b[:], c2("ones128"), [bcp2], [self.b("ones_b")])
        self.cpy(self.ones256_b[:], c2("ones256"), [bcp2], [self.b("ones_b")])
        self.dma("pool", self.poolw_b[:], self.poolw_d, "poolw", [], [self.b("poolw_b")])
        bsc = self.b("sc")
        self.act(self.sc[:], c2("cvec"), AF.Silu, [bcp2], [bsc])
        self.cpy(self.sc_b[:], self.sc[:], [bsc], [bsc])
        blg = self.b("lg")
        for l in range(NL):
            sl = slice(l * 8, (l + 1) * 8)
            self.act(self.nlg[:, sl], self.cpc(f"dec{l}"), AF.Exp, [bcp], [blg], scale=-1.0)
            self.act(self.nlg[:, sl], self.nlg[:, sl], AF.Ln, [blg, bcp], [blg], bias=self.cpc("one"))
            self.ts(self.lg[:, sl], self.nlg[:, sl], -1.0, None, ALU.mult, None, [blg], [blg])
            self.act(self.g128[:, sl], self.lg[:, sl], AF.Exp, [blg], [blg], scale=128.0)
        for l in range(NL):
            bank = 6 + l
            for cg in range(18):
                s = (l * 18 + cg) % 4
                wt = self.scr_bf(s * 8192, 4096)
                wb = self.sb("wm", s)
                self.dma("pool", wt, self.wmod_d[l * 18 + cg], f"wm{s}", [], [wb])
                for jj in range(4):
                    j = cg * 4 + jj
                    for k in range(KC):
                        self.mm(self.ps[bank][:, j * 3:(j + 1) * 3],
                                wt[:, k * 512 + jj * 128: k * 512 + (jj + 1) * 128],
                                self.sc_b[:, k * 3:(k + 1) * 3], k == 0, k == KC - 1,
                                [wb, bsc], [self.psb[bank]])
            bm = self.b("mod", l)
            self.tt(self.mod[:, l * 216:(l + 1) * 216], self.ps[bank][:, 0:216], c2(f"bmod{l}"),
                    ALU.add, [self.psb[bank], bcp2], [bm])
            for jn in range(3):
                j = 3 * jn + 1
                self.stt(self.gs[:, l * 72 + jn * 24: l * 72 + (jn + 1) * 24],
                         self.mod[:, l * 216 + j * 24: l * 216 + (j + 1) * 24], 1.0,
                         c2(f"normg{l}", jn * 24, 24), ALU.add, ALU.mult, [bm, bcp2], [self.b("gs", l)])
                jg = 3 * jn + 2
                self.ts(self.hg[:, l * 72 + jn * 24: l * 72 + (jn + 1) * 24],
                        self.mod[:, l * 216 + jg * 24: l * 216 + (jg + 1) * 24],
                        1.0 if jn == 1 else 0.5, None, ALU.mult, None, [bm], [self.b("hg", l)])
        self.new_phase()

    def load_seg(self, src, ntok):
        bident = self.b("cp")
        for tt_ in range(ntok // 128):
            s = tt_ % 8
            st = self.scr_f32(s * 4096, 1024)
            stb = self.sb("xin", s)
            self.dma("sp", st, src[tt_ * 128:(tt_ + 1) * 128, :], f"xin{s}", [], [stb])
            blk = tt_ // 4
            for half in range(2):
                bk = self.bank()
                for kk in range(4):
                    k = half * 4 + kk
                    self.tr(self.ps[bk][:, kk * 128:(kk + 1) * 128], st[:, k * 128:(k + 1) * 128],
                            self.cpc("ident"), [stb, bident], [self.psb[bk]])
                out = self.xT[:, half * 4 * SEQ:(half * 4 + 4) * SEQ].rearrange("p (k t) -> p k t", k=4)[
                    :, :, tt_ * 128:(tt_ + 1) * 128]
                in_ = self.ps[bk][:, :].rearrange("p (k t) -> p k t", k=4)
                eng = "dve" if half == 0 else "act"
                W = [self.xb(half * 4 + kk, blk) for kk in range(4)]
                if eng == "dve":
                    self.cpy(out, in_, [self.psb[bk]], W)
                else:
                    self.act(out, in_, AF.Identity, [self.psb[bk]], W)
        self.new_phase()

    def store_seg(self, dst, ntok):
        bident = self.b("cp")
        for tt_ in range(ntok // 128):
            s = tt_ % 8
            st = self.scr_f32(s * 4096, 1024)
            stb = self.sb("xout", s)
            blk = tt_ // 4
            for half in range(2):
                bk = self.bank()
                for kk in range(4):
                    k = half * 4 + kk
                    self.tr(self.ps[bk][:, kk * 128:(kk + 1) * 128], self.xv(k, tt_ * 128, 128),
                            self.cpc("ident"), [self.xb(k, blk), bident], [self.psb[bk]])
                if half == 0:
                    self.cpy(st[:, 0:512], self.ps[bk][:, :], [self.psb[bk]], [stb])
                else:
                    self.act(st[:, 512:1024], self.ps[bk][:, :], AF.Identity, [self.psb[bk]], [stb])
            self.dma("sp", dst[tt_ * 128:(tt_ + 1) * 128, :], st, f"xout{s}", [stb], [self.b("outd", s)])
        self.new_phase()

    def rms_rstd(self, blk, sq_eng="act", defer=False):
        bk = 6 + (blk % 2)
        for k in range(KC):
            sq, sqb = self.getsq()
            xs = self.xv(k, blk * 512, 512)
            if sq_eng == "act":
                self.act(sq[:], xs, AF.Square, [self.xb(k, blk)], [sqb])
            else:
                self.tt(sq[:], xs, xs, ALU.mult, [self.xb(k, blk)], [sqb])
            self.mm(self.ps[bk][:], self.ones_b[:], sq[:], k == 0, k == KC - 1,
                    [sqb, self.b("ones_b")], [self.psb[bk]])
        ri = self.rstd_i
        self.rstd_i = 1 - ri
        rs, rsb = self.rstd[ri], self.rstdb[ri]

        def fin():
            self.rsqrt(rs[:], self.ps[bk][:], [self.psb[bk]], rsb)
        if defer:
            return rs, rsb, fin
        fin()
        return rs, rsb

    def norm_mod(self, blk, l, jn, grp, hT, hoff, hbuf, rs=None):
        rs, rsb = rs if rs is not None else self.rms_rstd(blk)
        for k in range(KC):
            t, tb = self.gettmp()
            self.tt(t[:], self.xv(k, blk * 512, 512), rs[:], ALU.mult, [self.xb(k, blk), rsb], [tb])
            self.act(hT(k, hoff), t[:], AF.Identity, [tb, self.b("gs", l), self.b("mod", l)], [hbuf(k)],
                     scale=self.gscol(l, jn, k, grp), bias=self.modcol(l, 3 * jn, k, grp))

    def ffn(self, l, f, nblk, grp):
        jn = 0 if f == 0 else 2
        lf = l * 2 + f
        HT = 0
        GT = 16 * 1024
        W1 = 60 * 1024
        W3 = 68 * 1024
        W2 = 76 * 1024
        for g0 in range(0, nblk, 2):
            blks = list(range(g0, min(g0 + 2, nblk)))
            hTf = lambda k, off: self.scr_bf(HT + k * 2048 + off * 2, 512)
            rs0 = self.rms_rstd(blks[0])
            if len(blks) > 1:
                rs1, rs1b, fin1 = self.rms_rstd(blks[1], sq_eng="dve", defer=True)
            self.norm_mod(blks[0], l, jn, grp, hTf, 0, lambda k: self.sb("h", k, 0), rs=rs0)
            if len(blks) > 1:
                fin1()
                self.norm_mod(blks[1], l, jn, grp, hTf, 512, lambda k: self.sb("h", k, 1), rs=(rs1, rs1b))
            for cg in range(11):
                s = self.slot("w13")
                w1t = self.scr_bf(W1 + s * 4096, 2048)
                w3t = self.scr_bf(W3 + s * 4096, 2048)
                self.dma("pool", w1t, self.w1_d[lf * 11 + cg], f"w1{s}", [], [self.sb("w1", s)])
                self.dma("pool", w3t, self.w3_d[lf * 11 + cg], f"w3{s}", [], [self.sb("w3", s)])
                for bi, blk in enumerate(blks):
                    for m in range(2):
                        pa = self.bank()
                        pb = self.bank()
                        for k in range(KC):
                            self.mm(self.ps[pa][:], w1t[:, k * 256 + m * 128: k * 256 + (m + 1) * 128],
                                    self.scr_bf(HT + k * 2048 + bi * 1024, 512), k == 0, k == KC - 1,
                                    [self.sb("w1", s), self.sb("h", k, bi)], [self.psb[pa]])
                        for k in range(KC):
                            self.mm(self.ps[pb][:], w3t[:, k * 256 + m * 128: k * 256 + (m + 1) * 128],
                                    self.scr_bf(HT + k * 2048 + bi * 1024, 512), k == 0, k == KC - 1,
                                    [self.sb("w3", s), self.sb("h", k, bi)], [self.psb[pb]])
                        t, tb = self.gettmp()
                        self.act(t[:], self.ps[pa][:], AF.Silu, [self.psb[pa]], [tb])
                        j = cg * 2 + m
                        self.tt(self.scr_bf(GT + j * 2048 + bi * 1024, 512), t[:], self.ps[pb][:], ALU.mult,
                                [tb, self.psb[pb]], [self.sb("g", j, bi)])
            for dg in range(KC):
                s = self.slot("w2")
                w2t = self.scr_bf(W2 + s * 5632, 2816)
                self.dma("pool", w2t, self.w2_d[lf * 8 + dg], f"w2{s}", [], [self.sb("w2", s)])
                for bi, blk in enumerate(blks):
                    pc = self.bank()
                    for j in range(FC):
                        self.mm(self.ps[pc][:], w2t[:, j * 128:(j + 1) * 128],
                                self.scr_bf(GT + j * 2048 + bi * 1024, 512), j == 0, j == FC - 1,
                                [self.sb("w2", s), self.sb("g", j, bi)], [self.psb[pc]])
                    xs = self.xv(dg, blk * 512, 512)
                    self.stt(xs, self.ps[pc][:], self.hgcol(l, jn, dg, grp), xs, ALU.mult, ALU.add,
                             [self.psb[pc], self.b("hg", l), self.xb(dg, blk)], [self.xb(dg, blk)])
        self.new_phase()

    def slot(self, name):
        v = self.slot_ctr.get(name, 0)
        self.slot_ctr[name] = v + 1
        return v % 2

    def mixer(self, l, nblk, grp, seqs, rope, full, bidx):
        T = nblk * 512
        HT = 0
        PH = 32 * 1024
        WI = 77 * 1024
        hT = lambda k, off: self.scr_bf(HT + k * (2 * T) + off * 2, 512)
        pend = {}
        for blk in range(min(2, nblk)):
            pend[blk] = self.rms_rstd(blk, sq_eng="act" if blk % 2 == 0 else "dve", defer=True)
        for blk in range(nblk):
            rs_, rsb_, fin_ = pend.pop(blk)
            fin_()
            self.norm_mod(blk, l, 1, grp, hT, blk * 512, lambda k, blk=blk: self.sb("h", k, blk), rs=(rs_, rsb_))
            if blk + 2 < nblk:
                pend[blk + 2] = self.rms_rstd(blk + 2, sq_eng="act" if blk % 2 == 0 else "dve", defer=True)
        KEEP = ("h", "wi")

        def load_wi(slot, chunk):
            wt = self.scr_bf(WI + slot * 2048, 1024)
            self.dma("pool", wt, self.win_d[l * 22 + chunk], f"wi{slot}", [], [self.sb("wi", slot)])
            return wt

        def load_wo(chunk, slot=4):
            wt = self.scr_bf(WI + slot * 2048, 1024)
            self.dma("pool", wt, self.wout_d[l * 8 + chunk], f"wi{slot}", [], [self.sb("wi", slot)])
            return wt

        def proj_fm(wt, slot, blk, bk):
            for k in range(KC):
                self.mm(self.ps[bk][:], wt[:, k * 128:(k + 1) * 128], hT(k, blk * 512), k == 0, k == KC - 1,
                        [self.sb("wi", slot), self.sb("h", k, blk)], [self.psb[bk]])

        def wout_partial(terms, blk):
            for dg in range(KC):
                pc = self.bank()
                for ti, (wo, wslot, cat_ap, catb) in enumerate(terms):
                    self.mm(self.ps[pc][:], wo[:, dg * 128:(dg + 1) * 128], cat_ap, ti == 0, ti == len(terms) - 1,
                            [self.sb("wi", wslot), catb], [self.psb[pc]])
                xs = self.xv(dg, blk * 512, 512)
                self.stt(xs, self.ps[pc][:], self.hgcol(l, 1, dg, grp), xs, ALU.mult, ALU.add,
                         [self.psb[pc], self.b("hg", l), self.xb(dg, blk)], [self.xb(dg, blk)])

        bcp = self.b("cp")
        if full:
            for ch in range(2):
                self.new_phase(KEEP + ("pcat",))
                PADL = 8
                LP = T + 16 * len(seqs)
                pp = self.scr_f32(PH, LP)
                ta = self.scr_f32(PH + 4 * LP, LP)
                tb_ = self.scr_f32(PH + 8 * LP, LP)
                pooled = self.scr_bf(PH + 12 * LP, T)
                CATP = PH + 45 * 1024 - 4 * T
                assert PH + 12 * LP + 2 * T <= CATP
                pcat = [self.scr_bf(CATP + c_ * 2 * T, T) for c_ in range(2)]
                cat = pcat[ch]
                bpp, bta, btb = self.sb("pp"), self.sb("ta"), self.sb("tb")
                for si_, (c0_, ncn_, _) in enumerate(seqs):
                    b0_ = c0_ * 128 + 16 * si_
                    L_ = ncn_ * 128
                    self.memset(pp[:, b0_: b0_ + 8], 0.0, [bpp])
                    self.memset(pp[:, b0_ + 8 + L_: b0_ + 16 + L_], 0.0, [bpp])
                    self.memset(ta[:, b0_: b0_ + 1], 0.0, [bta])
                    self.memset(tb_[:, b0_: b0_ + 1], 0.0, [btb])
                    self.memset(tb_[:, b0_ + L_ + 15: b0_ + L_ + 16], 0.0, [btb])
                wt = load_wi(0, ch)
                wo_p = load_wo(ch, 4 + ch)
                if ch == 0:
                    wo_p0 = wo_p

                def ppos(tok):
                    si = 0
                    for i, (c0, ncn, _) in enumerate(seqs):
                        if tok >= c0 * 128:
                            si = i
                    return tok + 16 * si + PADL

                for blk in range(nblk):
                    bk = self.bank()
                    proj_fm(wt, 0, blk, bk)
                    for piece in range(2):
                        t0 = blk * 512 + piece * 256
                        p0 = ppos(t0)
                        self.act(pp[:, p0:p0 + 256], self.ps[bk][:, piece * 256:(piece + 1) * 256], AF.Identity,
                                 [self.psb[bk]], [bpp])
                for (c0, ncn, _) in seqs:
                    L = ncn * 128
                    base = ppos(c0 * 128) - PADL
                    LL = L + 16
                    for half in range(2):
                        gi = ch * 2 + half
                        w = POOL_WINDOWS[gi]
                        ps_ = slice(half * 64, half * 64 + 64)
                        cur, curb = ta, bta
                        self.tt(ta[ps_, base + 1: base + LL], pp[ps_, base: base + LL - 1], pp[ps_, base + 1: base + LL],
                                ALU.add, [bpp], [bta])
                        other, otherb = tb_, btb
                        sh = 1
                        ww = 2
                        while ww < w:
                            self.tt(other[ps_, base + sh: base + LL - sh], cur[ps_, base: base + LL - 2 * sh],
                                    cur[ps_, base + 2 * sh: base + LL], ALU.add, [curb], [otherb])
                            cur, curb, other, otherb = other, otherb, cur, curb
                            sh *= 2
                            ww *= 2
                        e0 = gi * 16
                        self.tt(cur[ps_, base + PADL: base + PADL + 8], cur[ps_, base + PADL: base + PADL + 8],
                                self.cpc("edge", e0, 8)[ps_, :], ALU.mult, [curb, bcp], [curb])
                        self.tt(cur[ps_, base + PADL + L - 8: base + PADL + L], cur[ps_, base + PADL + L - 8: base + PADL + L],
                                self.cpc("edge", e0 + 8, 8)[ps_, :], ALU.mult, [curb, bcp], [curb])
                        self.stt(pooled[ps_, c0 * 128: c0 * 128 + L], cur[ps_, base + PADL: base + PADL + L], 1.0 / w,
                                 pp[ps_, base + PADL: base + PADL + L], ALU.mult, ALU.subtract,
                                 [curb, bpp], [self.sb("pooled")])
                for blk in range(nblk):
                    bk = self.bank()
                    self.mm(self.ps[bk][:], self.poolw_b[:, l * 256 + ch * 128: l * 256 + (ch + 1) * 128],
                            pooled[:, blk * 512:(blk + 1) * 512], True, True,
                            [self.b("poolw_b"), self.sb("pooled")], [self.psb[bk]])
                    self.act(cat[:, blk * 512:(blk + 1) * 512], self.ps[bk][:], AF.Identity, [self.psb[bk], bcp],
                             [self.sb("pcat", ch, blk)], scale=self.cpc(f"pool_scale{l}", ch, 1))
                    if ch == 1:
                        wout_partial([(wo_p0, 4, pcat[0][:, blk * 512:(blk + 1) * 512], self.sb("pcat", 0, blk)),
                                      (wo_p, 5, pcat[1][:, blk * 512:(blk + 1) * 512], self.sb("pcat", 1, blk))], blk)
            self.new_phase(KEEP)
            nsq = len(seqs)
            ZP = T + 30 * nsq
            if (2 * ZP) % 4:
                ZP += 1
            ZB = PH
            DG = ZB + 4 * ZP
            AC = DG + 2 * 31 * 256
            AB = AC + 2 * 2 * 2048
            CC = AB + 4 * 1024
            assert CC + 4 * T <= WI, (CC + 4 * T, WI)
            zb_ = [self.scr_bf(ZB + c * 2 * ZP, ZP) for c in range(2)]
            cat = [self.scr_bf(CC + c * 2 * T, T) for c in range(2)]

            def zpos(tok):
                si = 0
                for i, (c0, ncn, _) in enumerate(seqs):
                    if tok >= c0 * 128:
                        si = i
                return tok + 30 * si + 15

            for c in range(2):
                self.memset(zb_[c], 0.0, [self.sb("z", c)])
                for kk in range(31):
                    dgm = self.scr_bf(DG + (c * 31 + kk) * 256, 128)
                    self.ts(dgm, self.ident_b[:], self.cpc(f"conv_dw{l}", c * 31 + kk, 1), None, ALU.mult, None,
                            [self.b("ident_b"), bcp], [self.sb("dg", c)])
                wa = load_wi(0, 18 + c)
                wg = load_wi(1, 20 + c)
                for blk in range(nblk):
                    ba = self.bank()
                    bg = self.bank()
                    proj_fm(wa, 0, blk, ba)
                    proj_fm(wg, 1, blk, bg)
                    t, tb = self.gettmp()
                    self.act(t[:], self.ps[bg][:], AF.Sigmoid, [self.psb[bg]], [tb])
                    for piece in range(2):
                        p0 = zpos(blk * 512 + piece * 256)
                        self.tt(zb_[c][:, p0:p0 + 256], self.ps[ba][:, piece * 256:(piece + 1) * 256],
                                t[:, piece * 256:(piece + 1) * 256], ALU.mult, [self.psb[ba], tb], [self.sb("z", c)])
            for blk in range(nblk):
                par = blk % 2
                accs = []
                for c in range(2):
                    bk = self.bank()
                    if nsq == 1:
                        pieces = [(0, 512, zpos(blk * 512) - 15)]
                    else:
                        pieces = [(0, 256, zpos(blk * 512) - 15), (256, 256, zpos(blk * 512 + 256) - 15)]
                    for (co, n, zs) in pieces:
                        for kk in range(31):
                            dgm = self.scr_bf(DG + (c * 31 + kk) * 256, 128)
                            self.mm(self.ps[bk][:, co:co + n], dgm, zb_[c][:, zs + kk: zs + kk + n], kk == 0, kk == 30,
                                    [self.sb("dg", c), self.sb("z", c)], [self.psb[bk]])
                    a32 = self.scr_f32(AC + (par * 2 + c) * 2048, 512)
                    a32b = self.sb("a32", par, c)
                    ab = self.scr_bf(AB + (c * 2) * 1024, 512)
                    sq = self.scr_bf(AB + (c * 2 + 1) * 1024, 512)
                    abb = self.sb("ab", c)
                    cb = self.cpc(f"conv_b{l}", c, 1)
                    self.act(a32, self.ps[bk][:], AF.Identity, [self.psb[bk], bcp], [a32b], bias=cb)
                    self.act(ab, self.ps[bk][:], AF.Identity, [self.psb[bk], bcp], [abb], bias=cb)
                    self.act(sq, self.ps[bk][:], AF.Square, [self.psb[bk], bcp], [abb], bias=cb)
                    accs.append((a32, a32b, ab, sq, abb))
                bm_, bv_ = 6, 7
                for c in range(2):
                    self.mm(self.ps[bm_][:], self.ones256_b[:], accs[c][2], c == 0, c == 1,
                            [self.b("ones_b"), accs[c][4]], [self.psb[bm_]])
                for c in range(2):
                    self.mm(self.ps[bv_][:], self.ones256_b[:], accs[c][3], c == 0, c == 1,
                            [self.b("ones_b"), accs[c][4]], [self.psb[bv_]])
                mu, mub = self.gettmp()
                self.act(mu[:], self.ps[bm_][:], AF.Identity, [self.psb[bm_]], [mub])
                var, varb = self.gettmp()
                self.act(var[:], self.ps[bm_][:], AF.Square, [self.psb[bm_]], [varb])
                self.tt(var[:], self.ps[bv_][:], var[:], ALU.subtract, [self.psb[bv_], varb], [varb])
                self.rsqrt(var[:], var[:], [varb], varb, clamp=True)
                for c in range(2):
                    d, db = self.gettmp()
                    self.tt(d[:], accs[c][0], mu[:], ALU.subtract, [accs[c][1], mub], [db])
                    self.tt(d[:], d[:], var[:], ALU.mult, [db, varb], [db])
                    self.act(cat[c][:, blk * 512:(blk + 1) * 512], d[:], AF.Silu, [db, bcp], [self.sb("ccat", c, blk)],
                             scale=self.cpc(f"ln_g{l}", c, 1), bias=self.cpc(f"ln_b{l}", c, 1))
            wo_c = [load_wo(6 + c, 4 + c) for c in range(2)]
            for blk in range(nblk):
                wout_partial([(wo_c[c], 4 + c, cat[c][:, blk * 512:(blk + 1) * 512], self.sb("ccat", c, blk))
                              for c in range(2)], blk)

        nch = T // 128
        QF, QB, KF, KB = PH, PH + 2 * T, PH + 4 * T, PH + 6 * T
        KFT, KBT = PH + 8 * T, PH + 10 * T
        SF, SB = PH + 12 * T, PH + 14 * T
        VT = PH + 16 * T
        RF = PH + 18 * T
        GT = RF + 1024
        ST = GT + 2048
        CAT = ST + 2048
        OB = CAT + 2048
        assert OB + 2048 <= WI, (OB, WI)
        for h in range(4):
            self.new_phase(KEEP)
            qf = self.scr_bf(QF, T)
            qb = self.scr_bf(QB, T)
            kf = self.scr_bf(KF, T)
            kb = self.scr_bf(KB, T)
            kfT = self.scr_bf(KFT, T)
            kbT = self.scr_bf(KBT, T)
            sf = self.scr_bf(SF, T)
            sbk = self.scr_bf(SB, T)
            vT = self.scr_bf(VT, T)
            Rst = self.scr_f32(RF, 256)
            gt = self.scr_f32(GT, 512)
            bgt = self.sb("gt")
            lgf = self.lg[:, l * 8 + h: l * 8 + h + 1]
            lgb = self.lg[:, l * 8 + 4 + h: l * 8 + 4 + h + 1]
            nlgf = self.nlg[:, l * 8 + h: l * 8 + h + 1]
            nlgb = self.nlg[:, l * 8 + 4 + h: l * 8 + 4 + h + 1]
            blg = self.b("lg")
            self.act(gt[:, 0:128], self.cpc("pos1"), AF.Exp, [bcp, blg], [bgt], scale=lgf)
            self.act(gt[:, 128:256], self.cpc("posr"), AF.Exp, [bcp, blg], [bgt], scale=lgb)
            self.act(gt[:, 256:257], self.cpc("pos1c"), AF.Exp, [bcp, blg], [bgt], scale=nlgf)
            self.act(gt[:, 257:258], self.cpc("posrc"), AF.Exp, [bcp, blg], [bgt], scale=nlgb)
            ksc = 128.0 ** -0.5
            self.ts(gt[:, 256:258], gt[:, 256:258], ksc, None, ALU.mult, None, [bgt], [bgt])
            wkcol = (gt[:, 256:257], gt[:, 257:258])
            wq = load_wi(0, 2 + h)
            wk = load_wi(1, 6 + h)
            wv = load_wi(2, 10 + h)
            for blk in range(nblk):
                if rope:
                    rs_ = self.slot("rope")
                    rt = self.ropes[rs_]
                    rtb = self.b("rope", rs_)
                    self.dma("sp", rt[:, :].rearrange("p (a t) -> p a t", a=2),
                             self.rope_d[:, :, blk * 512:(blk + 1) * 512], f"rope{rs_}", [], [rtb])
                items = []
                for which, wt, slot, outs in (("q", wq, 0, ((qf, 0), (qb, 128))), ("k", wk, 1, ())):
                    if not full and which == "q":
                        continue
                    bk = self.bank()
                    proj_fm(wt, slot, blk, bk)
                    if rope:
                        q32, q32b = self.gettmp()
                        self.act(q32[:], self.ps[bk][:], AF.Identity, [self.psb[bk]], [q32b])
                        items.append((which, outs, q32, q32b))
                    else:
                        items.append((which, outs, self.ps[bk], self.psb[bk]))
                for (which, outs, q32, q32b) in items:
                    if rope:
                        bp = self.bank()
                        self.mm(self.ps[bp][:], self.cpc("perm"), q32[:], True, True, [bcp, q32b], [self.psb[bp]])
                        t2, t2b = self.gettmp()
                        self.tt(t2[:], self.ps[bp][:], rt[:, 512:1024], ALU.mult, [self.psb[bp], rtb], [t2b])
                        self.tt(q32[:], q32[:], rt[:, 0:512], ALU.mult, [q32b, rtb], [q32b])
                        if which == "k":
                            self.tt(kf[:, blk * 512:(blk + 1) * 512], q32[:], t2[:], ALU.add, [q32b, t2b],
                                    [self.sb("k", 256, blk)])
                            continue
                        self.tt(q32[:], q32[:], t2[:], ALU.add, [q32b, t2b], [q32b])
                    elif which == "k":
                        self.cpy(kf[:, blk * 512:(blk + 1) * 512], q32[:], [q32b], [self.sb("k", 256, blk)])
                        continue
                    src, srcb = q32[:], q32b
                    for (dst, go) in outs:
                        o3 = dst[:, blk * 512:(blk + 1) * 512].rearrange("p (c i) -> p c i", c=4)
                        i3 = src.rearrange("p (c i) -> p c i", c=4)
                        g3 = gt[:, go:go + 128].unsqueeze(1).to_broadcast([128, 4, 128])
                        self.tt(o3, i3, g3, ALU.mult, [srcb, bgt], [self.sb(which, go, blk)])
            for cgp in range(nch // 4):
                bk = self.bank()
                for cc in range(4):
                    n = cgp * 4 + cc
                    blk = n // 4
                    for k in range(KC):
                        self.mm(self.ps[bk][:, cc * 128:(cc + 1) * 128], self.scr_bf(HT + k * (2 * T) + n * 256, 128), wv[:, k * 128:(k + 1) * 128],
                                k == 0, k == KC - 1, [self.sb("h", k, blk), self.sb("wi", 2)], [self.psb[bk]])
                self.act(vT[:, cgp * 512:(cgp + 1) * 512], self.ps[bk][:], AF.Identity, [self.psb[bk]], [self.sb("vT", cgp)])
            for cgp in range(nch // 4):
                bk = self.bank()
                pst = self.ps[bk][:, :].bitcast(BF16)
                for cc in range(4):
                    n = cgp * 4 + cc
                    self.tr(pst[:, cc * 128:(cc + 1) * 128], kf[:, n * 128:(n + 1) * 128], self.ident_b[:],
                            [self.sb("k", 256, n // 4), self.b("ident_b")], [self.psb[bk]])
                self.act(kfT[:, cgp * 512:(cgp + 1) * 512], pst[:, 0:512], AF.Identity, [self.psb[bk], bgt],
                         [self.sb("kfT", cgp)], scale=wkcol[0])
                self.act(kbT[:, cgp * 512:(cgp + 1) * 512], pst[:, 0:512], AF.Identity, [self.psb[bk], bgt],
                         [self.sb("kbT", cgp)], scale=wkcol[1])
            if full:
                wg = load_wi(3, 14 + h)
                for blk in range(nblk):
                    bg = self.bank()
                    proj_fm(wg, 3, blk, bg)
                    self.act(self.sgall[:, blk * 512:(blk + 1) * 512], self.ps[bg][:], AF.Silu, [self.psb[bg]],
                             [self.b("sg", blk)])
            for (c0, ncn, bi_) in seqs:
                for idx in range(ncn):
                    for d_ in range(2):
                        kT, nm = (kfT, "kfT") if d_ == 0 else (kbT, "kbT")
                        sdst = sf if d_ == 0 else sbk
                        g128c = self.g128[:, l * 8 + d_ * 4 + h: l * 8 + d_ * 4 + h + 1]
                        R = Rst[:, d_ * 128:(d_ + 1) * 128]
                        Rb = self.sb("R", d_)
                        st_off = (((l * 2 + bi_) * 2 + d_) * 4 + h) * 128
                        S0 = self.states[:, st_off: st_off + 128]
                        S0b = self.b("state", l, bi_, d_, h)
                        n = c0 + idx if d_ == 0 else c0 + ncn - 1 - idx
                        if full:
                            if idx == 0:
                                if rope:
                                    self.cpy(sdst[:, n * 128:(n + 1) * 128], S0, [S0b], [self.sb("S", d_, n)])
                                else:
                                    self.memset(sdst[:, n * 128:(n + 1) * 128], 0.0, [self.sb("S", d_, n)])
                            else:
                                self.act(sdst[:, n * 128:(n + 1) * 128], R, AF.Identity, [Rb, blg], [self.sb("S", d_, n)],
                                         scale=g128c)
                        last = idx == ncn - 1
                        if last and rope:
                            continue
                        bk = self.bank()
                        self.mm(self.ps[bk][:, 0:128], kT[:, n * 128:(n + 1) * 128], vT[:, n * 128:(n + 1) * 128], True, True,
                                [self.sb(nm, n // 4), self.sb("vT", n // 4)], [self.psb[bk]])
                        if idx == 0:
                            if rope:
                                self.tt(R, self.ps[bk][:, 0:128], S0, ALU.add, [self.psb[bk], S0b], [Rb])
                            else:
                                self.cpy(R, self.ps[bk][:, 0:128], [self.psb[bk]], [Rb])
                        else:
                            self.stt(R, R, g128c, self.ps[bk][:, 0:128], ALU.mult, ALU.add, [Rb, blg, self.psb[bk]], [Rb])
                        if last and not rope:
                            self.act(S0, R, AF.Identity, [Rb, blg], [S0b], scale=g128c)
            if not full:
                continue
            odd = h % 2 == 1
            if odd:
                wo_prev = load_wo(2 + h - 1, 4)
                wo_cur = load_wo(2 + h, 5)

            def stageA1(blk):
                sts = []
                for d_ in range(2):
                    kk_, kgo = kf, 256
                    qq_, qgo = (qf, 0) if d_ == 0 else (qb, 128)
                    bk = self.bank()
                    for cc in range(4):
                        n = blk * 4 + cc
                        self.mm(self.ps[bk][:, cc * 128:(cc + 1) * 128], kk_[:, n * 128:(n + 1) * 128],
                                qq_[:, n * 128:(n + 1) * 128], True, True,
                                [self.sb("k", kgo, blk), self.sb("q", qgo, blk)], [self.psb[bk]])
                    sT = self.scr_bf(ST + d_ * 1024, 512)
                    mk = self.cpc("maskf" if d_ == 0 else "maskb").unsqueeze(1).to_broadcast([128, 4, 128])
                    self.stt(sT.rearrange("p (c i) -> p c i", c=4), self.ps[bk][:, :].rearrange("p (c i) -> p c i", c=4),
                             wkcol[d_], mk, ALU.mult, ALU.mult, [self.psb[bk], bcp, bgt], [self.sb("sT", d_)])
                    sts.append(sT)
                bo = 4 + blk % 2
                for cc in range(4):
                    n = blk * 4 + cc
                    oc = self.ps[bo][:, cc * 128:(cc + 1) * 128]
                    vch = vT[:, n * 128:(n + 1) * 128]
                    self.mm(oc, vch, sts[0][:, cc * 128:(cc + 1) * 128], True, False,
                            [self.sb("vT", n // 4), self.sb("sT", 0)], [self.psb[bo]])
                    self.mm(oc, vch, sts[1][:, cc * 128:(cc + 1) * 128], False, False,
                            [self.sb("vT", n // 4), self.sb("sT", 1)], [self.psb[bo]])
                    self.mm(oc, sf[:, n * 128:(n + 1) * 128], qf[:, n * 128:(n + 1) * 128], False, False,
                            [self.sb("S", 0, n), self.sb("q", 0, blk)], [self.psb[bo]])
                    self.mm(oc, sbk[:, n * 128:(n + 1) * 128], qb[:, n * 128:(n + 1) * 128], False, True,
                            [self.sb("S", 1, n), self.sb("q", 128, blk)], [self.psb[bo]])
                return bo

            def stageA2(blk, bo):
                par = blk % 2
                rb = self.b("rope", par)
                o32 = self.ropes[par][:, 0:512]
                mu = self.ropes[par][:, 512:1024]
                var, varb = self.rstd[par], self.rstdb[par]
                self.act(o32, self.ps[bo][:], AF.Identity, [self.psb[bo]], [rb])
                ob = self.scr_bf(OB, 512)
                osq = self.scr_bf(OB + 1024, 512)
                obb = self.sb("ob")
                self.act(ob, self.ps[bo][:], AF.Identity, [self.psb[bo]], [obb])
                self.act(osq, self.ps[bo][:], AF.Square, [self.psb[bo]], [obb])
                self.mm(self.ps[6][:], self.ones128_b[:], ob, True, True, [self.b("ones_b"), obb], [self.psb[6]])
                self.mm(self.ps[7][:], self.ones128_b[:], osq, True, True, [self.b("ones_b"), obb], [self.psb[7]])
                self.act(mu, self.ps[6][:], AF.Identity, [self.psb[6]], [rb])
                self.act(var[:], self.ps[6][:], AF.Square, [self.psb[6]], [varb])
                self.tt(var[:], self.ps[7][:], var[:], ALU.subtract, [self.psb[7], varb], [varb])
                self.rsqrt(var[:], var[:], [varb], varb, clamp=True)

            def stageB(blk):
                par = blk % 2
                rb = self.b("rope", par)
                o32 = self.ropes[par][:, 0:512]
                mu = self.ropes[par][:, 512:1024]
                var, varb = self.rstd[par], self.rstdb[par]
                self.tt(o32, o32, mu, ALU.subtract, [rb], [rb])
                self.tt(o32, o32, var[:], ALU.mult, [rb, varb], [rb])
                if odd:
                    cat, catb = self.scr_bf(CAT + par * 1024, 512), self.sb("hcat", par)
                else:
                    cat, catb = self.hprev[:, blk * 512:(blk + 1) * 512], self.b("hprev", blk)
                self.stt(cat, o32, self.cpc(f"gn_g{l}", h, 1), self.sgall[:, blk * 512:(blk + 1) * 512], ALU.mult, ALU.mult,
                         [rb, bcp, self.b("sg", blk)], [catb])

            def stageC(blk):
                if not odd:
                    return
                par = blk % 2
                wout_partial([(wo_prev, 4, self.hprev[:, blk * 512:(blk + 1) * 512], self.b("hprev", blk)),
                              (wo_cur, 5, self.scr_bf(CAT + par * 1024, 512), self.sb("hcat", par))], blk)

            self.bank_set = [0, 1, 2, 3]
            bo_ = stageA1(0)
            stageA2(0, bo_)
            for blk in range(nblk):
                if blk + 1 < nblk:
                    bo_ = stageA1(blk + 1)
                stageB(blk)
                stageC(blk)
                if blk + 1 < nblk:
                    stageA2(blk + 1, bo_)
            self.bank_set = [0, 1, 2, 3, 4, 5]
        self.new_phase()

    def final_norm(self, nblk):
        for blk in range(nblk):
            rs, rsb = self.rms_rstd(blk)
            for k in range(KC):
                xs = self.xv(k, blk * 512, 512)
                self.stt(xs, xs, self.cpc("final_g", k, 1), rs[:], ALU.mult, ALU.mult,
                         [self.xb(k, blk), self.b("cp"), rsb], [self.xb(k, blk)])

    def build(self):
        stop = self.dbg_stop
        self.prologue()
        ctx_seqs = [(0, 2, 0), (2, 2, 1)]
        self.load_seg(self.ctx_d, 512)
        stage = 0
        done = False
        for l in range(NL):
            last = l == NL - 1
            self.ffn(l, 0, 1, 0)
            stage += 1
            if stop == ("c", stage):
                done = True
                break
            self.mixer(l, 1, 0, ctx_seqs, rope=False, full=not last, bidx=None)
            stage += 1
            if stop == ("c", stage):
                done = True
                break
            if not last:
                self.ffn(l, 1, 1, 0)
                stage += 1
                if stop == ("c", stage):
                    done = True
                    break
        if stop is not None and stop[0] == "c":
            self.store_seg(self.dbgy_d, 512)
        else:
            for bi in range(2):
                self.load_seg(self.x_d[bi], SEQ)
                stage = 0
                done = False
                for l in range(NL):
                    self.ffn(l, 0, 4, 1 + bi)
                    stage += 1
                    if stop == ("x", stage):
                        done = True
                        break
                    self.mixer(l, 4, 1 + bi, [(0, 16, bi)], rope=True, full=True, bidx=bi)
                    stage += 1
                    if stop == ("x", stage):
                        done = True
                        break
                    self.ffn(l, 1, 4, 1 + bi)
                    stage += 1
                    if stop == ("x", stage):
                        done = True
                        break
                if not done:
                    self.final_norm(4)
                self.store_seg(self.out_d[bi], SEQ)
        outs = [self.b("outd", i) for i in range(8)]
        self.P.op("sp", lambda h: h.nop(), reads=outs)
        self.P.emit()
        return self.nc


_CACHE = {}


def kernel(x, c, ctx, c_ctx, w_mod, b_mod, norm_g, ffn_w1, ffn_w3, ffn_w2, w_in, w_out,
           pool_w, pool_scale, ret_decay_fwd, ret_decay_bwd, ret_gn_g, conv_dw, conv_b,
           conv_ln_g, conv_ln_b, final_g, _dbg_stop=None, _cores=None):
    inp = dict(x=x, c=c, ctx=ctx, c_ctx=c_ctx, w_mod=w_mod, b_mod=b_mod, norm_g=norm_g, ffn_w1=ffn_w1,
               ffn_w3=ffn_w3, ffn_w2=ffn_w2, w_in=w_in, w_out=w_out, pool_w=pool_w, pool_scale=pool_scale,
               ret_decay_fwd=ret_decay_fwd, ret_decay_bwd=ret_decay_bwd, ret_gn_g=ret_gn_g, conv_dw=conv_dw,
               conv_b=conv_b, conv_ln_g=conv_ln_g, conv_ln_b=conv_ln_b, final_g=final_g)
    inp = {k: np.asarray(v) for k, v in inp.items()}
    cores = list(range(NCORES)) if _cores is None else _cores
    W = relayout_weights(inp)
    rope = build_rope()
    poolw = build_poolw(inp)
    k = K(dbg_stop=_dbg_stop)
    nc = k.build()
    in_maps = []
    xs = np.asarray(inp["x"], np.float32)
    cs = np.asarray(inp["ctx"], np.float32)
    for core in cores:
        m = dict(W)
        m["x"] = np.ascontiguousarray(xs[2 * core: 2 * core + 2])
        m["ctx"] = np.ascontiguousarray(cs[2 * core: 2 * core + 2].reshape(2 * CTX, D))
        m["cp"], m["cp2"] = build_cp(inp, core)
        m["rope"] = rope
        m["poolw"] = poolw
        in_maps.append(m)
    res = run_bass_kernel_spmd(nc, in_maps, core_ids=list(range(len(cores))))
    if _dbg_stop is not None:
        return res.results
    out = np.concatenate([np.asarray(r["out"], np.float32) for r in res.results], axis=0)
    return out
```

```python
import numpy as np
import concourse.bass as bass
import concourse.mybir as mybir
from concourse.bass_utils import run_bass_kernel_spmd

F32 = mybir.dt.float32
BF16 = mybir.dt.bfloat16
ALU = mybir.AluOpType
AF = mybir.ActivationFunctionType

D = 1024
KC = 8
DFF = 2816
FC = 22
SEQ = 2048
CTX = 256
NL = 2
EPS = 1e-6
NCORES = 8


class Buf:
    __slots__ = ("name", "last_w", "readers")

    def __init__(self, name, inherit=None):
        self.name = name
        self.last_w = None
        self.readers = dict(inherit) if inherit else {}


class Tok:
    __slots__ = ("key", "ord", "clock", "op")

    def __init__(self, key, ord_, clock, op):
        self.key = key
        self.ord = ord_
        self.clock = clock
        self.op = op


class Op:
    __slots__ = ("eng", "fn", "waits", "tok", "signal", "semval", "dma_sem")


class Prog:
    ENGS = ("pe", "act", "dve", "pool", "sp")

    def __init__(self, nc):
        self.nc = nc
        self.h = {"pe": nc.tensor, "act": nc.scalar, "dve": nc.vector,
                  "pool": nc.gpsimd, "sp": nc.sync}
        self.ops = []
        self.nops = {e: 0 for e in self.ENGS}
        self.seen = {e: {} for e in self.ENGS}
        self.dma_count = {}
        self.sems = {}

    def op(self, eng, fn, reads=(), writes=(), dma=None):
        seen = self.seen[eng]
        deps = {}

        def add(t, raw):
            if t is None:
                return
            if t.key == ("e", eng) and eng in ("pe", "sp"):
                return
            k = t.key
            if k not in deps or deps[k].ord < t.ord:
                deps[k] = t

        for b in reads:
            add(b.last_w, True)
        for b in writes:
            add(b.last_w, True)
            for t in b.readers.values():
                add(t, False)
        o = Op()
        o.eng = eng
        o.fn = fn
        o.waits = []
        o.signal = False
        o.semval = None
        o.dma_sem = dma
        for k, t in deps.items():
            if seen.get(k, 0) >= t.ord:
                continue
            o.waits.append(t)
            if t.op is not None:
                t.op.signal = True
            seen[k] = t.ord
            for kk, vv in t.clock.items():
                if seen.get(kk, 0) < vv:
                    seen[kk] = vv
        if dma is None:
            self.nops[eng] += 1
            tok = Tok(("e", eng), self.nops[eng], dict(seen), o)
        else:
            self.dma_count[dma] = self.dma_count.get(dma, 0) + 16
            tok = Tok(("d", dma), self.dma_count[dma], dict(seen), None)
        o.tok = tok
        self.ops.append(o)
        for b in writes:
            b.last_w = tok
            b.readers = {}
        for b in reads:
            if b in writes:
                continue
            b.readers[tok.key] = tok
        return tok

    def _sem(self, name):
        if name not in self.sems:
            self.sems[name] = self.nc.alloc_semaphore(name)
        return self.sems[name]

    def emit(self):
        cnt = {e: 0 for e in self.ENGS}
        for o in self.ops:
            h = self.h[o.eng]
            for t in o.waits:
                if t.key[0] == "e":
                    h.wait_ge(self._sem("s_" + t.key[1]), t.op.semval)
                else:
                    h.wait_ge(self._sem("d_" + t.key[1]), t.ord)
            ins = o.fn(h)
            if o.dma_sem is not None:
                ins.then_inc(self._sem("d_" + o.dma_sem), 16)
            elif o.signal:
                cnt[o.eng] += 1
                o.semval = cnt[o.eng]
                ins.then_inc(self._sem("s_" + o.eng), 1)


def _cp_layout():
    off = {}
    c = 0

    def add(name, n):
        nonlocal c
        off[name] = (c, n)
        c += n

    add("ident", 128)
    add("perm", 128)
    add("maskf", 128)
    add("maskb", 128)
    add("pos1", 128)
    add("posr", 128)
    add("pos1c", 1)
    add("posrc", 1)
    add("eps", 1)
    add("one", 1)
    add("edge", 4 * 16)
    add("final_g", 8)
    for l in range(NL):
        add(f"pool_scale{l}", 2)
        add(f"gn_g{l}", 4)
        add(f"conv_b{l}", 2)
        add(f"ln_g{l}", 2)
        add(f"ln_b{l}", 2)
        add(f"conv_dw{l}", 62)
        add(f"dec{l}", 8)
    return off, c


def _cp2_layout():
    off = {}
    c = 0
    for name, n in (("ones1024", 128), ("ones128", 128), ("ones256", 128), ("cvec", 24),
                    ("normg0", 72), ("bmod0", 216), ("normg1", 72), ("bmod1", 216)):
        off[name] = (c, n)
        c += n
    return off, c


CP_OFF, CP_N = _cp_layout()
CP2_OFF, CP2_N = _cp2_layout()
POOL_WINDOWS = (2, 4, 8, 16)


def _fm(v):
    return np.ascontiguousarray(np.asarray(v, np.float32).reshape(-1, 128).T)


def build_cp(inp, core):
    cp = np.zeros((128, CP_N), np.float32)
    cp2 = np.zeros((128, CP2_N), np.float32)

    def put(name, arr):
        if name in CP2_OFF:
            o, n = CP2_OFF[name]
            cp2[:, o:o + n] = np.asarray(arr, np.float32).reshape(128, n)
            return
        o, n = CP_OFF[name]
        arr = np.asarray(arr, np.float32).reshape(128, n)
        cp[:, o:o + n] = arr

    put("ident", np.eye(128, dtype=np.float32))
    perm = np.zeros((128, 128), np.float32)
    for m in range(128):
        partner = m + 32 if (m % 64) < 32 else m - 32
        perm[partner, m] = 1.0
    put("perm", perm)
    put("ones1024", np.full((128, 128), 1.0 / 1024, np.float32))
    put("ones128", np.full((128, 128), 1.0 / 128, np.float32))
    put("ones256", np.full((128, 128), 1.0 / 256, np.float32))
    jj = np.arange(128)[:, None]
    ii = np.arange(128)[None, :]
    put("maskf", (ii >= jj).astype(np.float32))
    put("maskb", (jj > ii).astype(np.float32))
    put("pos1", np.tile(np.arange(1, 129, dtype=np.float32)[None, :], (128, 1)))
    put("posr", np.tile((128 - np.arange(128, dtype=np.float32))[None, :], (128, 1)))
    put("pos1c", np.arange(1, 129, dtype=np.float32)[:, None])
    put("posrc", (128 - np.arange(128, dtype=np.float32))[:, None])
    put("eps", np.full((128, 1), EPS, np.float32))
    put("one", np.ones((128, 1), np.float32))
    edge = np.zeros((4, 16), np.float32)
    for gi, w in enumerate(POOL_WINDOWS):
        for t in range(8):
            cl = min(t + w // 2, w) if True else w
            cl = (t + w // 2) - max(t - w // 2, 0)
            edge[gi, t] = w / cl
            d = 8 - t
            cr = min(d, w // 2) + w // 2
            edge[gi, 8 + t] = w / cr
    put("edge", np.tile(edge.reshape(1, 64), (128, 1)))
    b0 = 2 * core
    cv = np.stack([np.asarray(inp["c_ctx"], np.float32),
                   np.asarray(inp["c"][b0], np.float32),
                   np.asarray(inp["c"][b0 + 1], np.float32)], axis=0)
    put("cvec", cv.reshape(3, 8, 128).transpose(2, 1, 0).reshape(128, 24))
    put("final_g", _fm(inp["final_g"]))
    for l in range(NL):
        ng = np.asarray(inp["norm_g"][l], np.float32).reshape(3, 8, 128).transpose(2, 0, 1)
        put(f"normg{l}", np.repeat(ng[:, :, :, None], 3, axis=3).reshape(128, 72))
        bm = _fm(inp["b_mod"][l])
        put(f"bmod{l}", np.repeat(bm[:, :, None], 3, axis=2).reshape(128, 216))
        put(f"pool_scale{l}", _fm(inp["pool_scale"][l]))
        put(f"gn_g{l}", _fm(inp["ret_gn_g"][l]))
        put(f"conv_b{l}", _fm(inp["conv_b"][l]))
        put(f"ln_g{l}", _fm(inp["conv_ln_g"][l]))
        put(f"ln_b{l}", _fm(inp["conv_ln_b"][l]))
        dw = np.asarray(inp["conv_dw"][l], np.float32)
        put(f"conv_dw{l}", dw.reshape(31, 2, 128).transpose(2, 1, 0).reshape(128, 62))
        dec = np.concatenate([np.asarray(inp["ret_decay_fwd"][l], np.float32),
                              np.asarray(inp["ret_decay_bwd"][l], np.float32)])
        put(f"dec{l}", np.tile(dec[None, :], (128, 1)))
    return cp, cp2


def build_poolw(inp):
    out = np.zeros((128, NL, 2, 128), np.float32)
    for l in range(NL):
        pw = np.asarray(inp["pool_w"][l], np.float32)
        for ch in range(2):
            out[0:64, l, ch, 0:64] = pw[2 * ch]
            out[64:128, l, ch, 64:128] = pw[2 * ch + 1]
    return out.reshape(128, NL * 256)


def build_rope():
    n_freq = 32
    inv = (10000.0 ** (-np.arange(n_freq, dtype=np.float32) / n_freq)).astype(np.float32)
    t = np.arange(SEQ)
    row = (t // 64).astype(np.float32)
    col = (t % 64).astype(np.float32)
    tab = np.zeros((128, 2, SEQ), np.float32)
    for f in range(128):
        pos = row if f < 64 else col
        ang = (pos * inv[f % 32]).astype(np.float32)
        tab[f, 0] = np.cos(ang)
        s = np.sin(ang)
        tab[f, 1] = -s if (f % 64) < 32 else s
    return tab


def relayout_weights(inp):
    w = {}
    w1 = np.asarray(inp["ffn_w1"], np.float32).reshape(4, 8, 128, 11, 256)
    w["w1"] = np.ascontiguousarray(w1.transpose(0, 3, 2, 1, 4)).reshape(44, 128, 2048)
    w3 = np.asarray(inp["ffn_w3"], np.float32).reshape(4, 8, 128, 11, 256)
    w["w3"] = np.ascontiguousarray(w3.transpose(0, 3, 2, 1, 4)).reshape(44, 128, 2048)
    w2 = np.asarray(inp["ffn_w2"], np.float32).reshape(4, 22, 128, 8, 128)
    w["w2"] = np.ascontiguousarray(w2.transpose(0, 3, 2, 1, 4)).reshape(32, 128, 2816)
    wi = np.asarray(inp["w_in"], np.float32).reshape(2, 8, 128, 22, 128)
    w["win"] = np.ascontiguousarray(wi.transpose(0, 3, 2, 1, 4)).reshape(44, 128, 1024)
    w["wout"] = np.ascontiguousarray(np.asarray(inp["w_out"], np.float32).reshape(16, 128, 1024))
    wm = np.asarray(inp["w_mod"], np.float32).reshape(2, 8, 128, 18, 512)
    w["wmod"] = np.ascontiguousarray(wm.transpose(0, 3, 2, 1, 4)).reshape(36, 128, 4096)
    return w


class K:
    def __init__(self, dbg_stop=None):
        self.dbg_stop = dbg_stop
        nc = bass.Bass("TRN2", target_bir_lowering=False)
        self.nc = nc
        self.P = Prog(nc)
        dt = nc.dram_tensor
        self.x_d = dt("x", [2, SEQ, D], F32, kind="ExternalInput").ap()
        self.ctx_d = dt("ctx", [2 * CTX, D], F32, kind="ExternalInput").ap()
        self.cp_d = dt("cp", [128, CP_N], F32, kind="ExternalInput").ap()
        self.cp2_d = dt("cp2", [128, CP2_N], F32, kind="ExternalInput").ap()
        self.rope_d = dt("rope", [128, 2, SEQ], F32, kind="ExternalInput").ap()
        self.poolw_d = dt("poolw", [128, NL * 256], F32, kind="ExternalInput").ap()
        self.w1_d = dt("w1", [44, 128, 2048], F32, kind="ExternalInput").ap()
        self.w3_d = dt("w3", [44, 128, 2048], F32, kind="ExternalInput").ap()
        self.w2_d = dt("w2", [32, 128, 2816], F32, kind="ExternalInput").ap()
        self.win_d = dt("win", [44, 128, 1024], F32, kind="ExternalInput").ap()
        self.wout_d = dt("wout", [16, 128, 1024], F32, kind="ExternalInput").ap()
        self.wmod_d = dt("wmod", [36, 128, 4096], F32, kind="ExternalInput").ap()
        self.out_d = dt("out", [2, SEQ, D], F32, kind="ExternalOutput").ap()
        if dbg_stop is not None:
            self.dbgy_d = dt("dbgy", [2 * CTX, D], F32, kind="ExternalOutput").ap()

        A = nc.alloc_sbuf_tensor
        self.xT = A("xT", [128, KC * SEQ], F32)
        self.cp = A("cp_s", [128, CP_N], F32)
        self.mod = A("mod", [128, NL * 216], F32)
        self.gs = A("gs", [128, NL * 72], F32)
        self.hg = A("hg", [128, NL * 72], F32)
        self.sc = A("sc", [128, 24], F32)
        self.sc_b = A("sc_b", [128, 24], BF16)
        self.lg = A("lg", [128, NL * 8], F32)
        self.nlg = A("nlg", [128, NL * 8], F32)
        self.g128 = A("g128", [128, NL * 8], F32)
        self.ones_b = A("ones_b", [128, 128], BF16)
        self.ident_b = A("ident_b", [128, 128], BF16)
        self.ones128_b = A("ones128_b", [128, 128], BF16)
        self.ones256_b = A("ones256_b", [128, 128], BF16)
        self.poolw_b = A("poolw_b", [128, NL * 256], BF16)
        self.states = A("states", [128, NL * 2 * 2 * 4 * 128], BF16)
        self.sgall = A("sgall", [128, SEQ], BF16)
        self.hprev = A("hprev", [128, SEQ], BF16)
        self.NTMP = 7
        self.tmp = [A(f"tmp{i}", [128, 512], F32) for i in range(self.NTMP)]
        self.sqb = [A(f"sqb{i}", [128, 512], BF16) for i in range(2)]
        self.rstd = [A(f"rstd{i}", [128, 512], F32) for i in range(2)]
        self.rstdb = [Buf(f"rstd{i}") for i in range(2)]
        self.rstd_i = 0
        self.ropes = [A(f"rope{i}", [128, 2 * 512], F32) for i in range(2)]
        self.SCR_BYTES = 89088 + 2048
        self.scr = A("scr", [128, self.SCR_BYTES // 4], F32)
        self.ps = [nc.alloc_psum_tensor(f"ps{i}", [128, 512], F32) for i in range(8)]
        self.psb = [Buf(f"ps{i}") for i in range(8)]
        self.bank_i = 0
        self.bank_set = [0, 1, 2, 3, 4, 5]
        self.tmp_i = 0
        self.tmpb = [Buf(f"tmp{i}") for i in range(self.NTMP)]
        self.sqb_i = 0
        self.sqbb = [Buf(f"sqb{i}") for i in range(2)]
        self.bufs = {}
        self.scr_bufs = {}
        self.inherit = {}
        self.slot_ctr = {}

    def b(self, *key):
        if key not in self.bufs:
            self.bufs[key] = Buf(str(key))
        return self.bufs[key]

    def sb(self, *key):
        if key not in self.scr_bufs:
            self.scr_bufs[key] = Buf(str(key), self.inherit)
        return self.scr_bufs[key]

    def new_phase(self, keep=()):
        keepd = {}
        for key, bf in self.scr_bufs.items():
            if key[0] in keep:
                keepd[key] = bf
                continue
            toks = list(bf.readers.values())
            if bf.last_w is not None:
                toks.append(bf.last_w)
            for t in toks:
                if t.key not in self.inherit or self.inherit[t.key].ord < t.ord:
                    self.inherit[t.key] = t
        self.scr_bufs = keepd

    def scr_f32(self, off_bytes, n):
        assert off_bytes % 4 == 0 and off_bytes + 4 * n <= self.SCR_BYTES, (off_bytes, n)
        return self.scr[:, off_bytes // 4: off_bytes // 4 + n]

    def scr_bf(self, off_bytes, n):
        assert off_bytes % 4 == 0 and n % 2 == 0 and off_bytes + 2 * n <= self.SCR_BYTES, (off_bytes, n)
        return self.scr[:, off_bytes // 4: off_bytes // 4 + n // 2].bitcast(BF16)

    def bank(self):
        bs = self.bank_set
        self.bank_i = (self.bank_i + 1) % len(bs)
        return bs[self.bank_i]

    def gettmp(self):
        i = self.tmp_i
        self.tmp_i = (i + 1) % self.NTMP
        return self.tmp[i], self.tmpb[i]

    def getsq(self):
        i = self.sqb_i
        self.sqb_i = (i + 1) % 2
        return self.sqb[i], self.sqbb[i]

    def cpc(self, name, a=0, n=None):
        o, nn = CP_OFF[name]
        if n is None:
            n = nn - a
        return self.cp[:, o + a: o + a + n]

    def mm(self, out, lhsT, rhs, start, stop, R, W):
        self.P.op("pe", lambda h: h.matmul(out, lhsT, rhs, start=start, stop=stop), reads=R, writes=W)

    def tr(self, out, in_, ident, R, W):
        self.P.op("pe", lambda h: h.transpose(out, in_, ident), reads=R, writes=W)

    def act(self, out, in_, func, R, W, scale=1.0, bias=0.0):
        self.P.op("act", lambda h: h.activation(out=out, in_=in_, func=func, bias=bias, scale=scale),
                  reads=R, writes=W)

    def tt(self, out, in0, in1, op, R, W, eng="dve"):
        self.P.op(eng, lambda h: h.tensor_tensor(out=out, in0=in0, in1=in1, op=op), reads=R, writes=W)

    def stt(self, out, in0, scalar, in1, op0, op1, R, W, eng="dve"):
        self.P.op(eng, lambda h: h.scalar_tensor_tensor(out=out, in0=in0, scalar=scalar, in1=in1,
                                                        op0=op0, op1=op1), reads=R, writes=W)

    def ts(self, out, in0, s1, s2, op0, op1, R, W, eng="dve"):
        if s2 is None:
            self.P.op(eng, lambda h: h.tensor_scalar(out=out, in0=in0, scalar1=s1, scalar2=None, op0=op0),
                      reads=R, writes=W)
        else:
            self.P.op(eng, lambda h: h.tensor_scalar(out=out, in0=in0, scalar1=s1, scalar2=s2, op0=op0, op1=op1),
                      reads=R, writes=W)

    def cpy(self, out, in_, R, W, eng="dve"):
        self.P.op(eng, lambda h: h.tensor_copy(out=out, in_=in_), reads=R, writes=W)

    def rsqrt(self, out, in_, R, wb, clamp=False):
        if clamp:
            self.act(out, in_, AF.Relu, list(R), [wb])
            self.act(out, out, AF.Ln, [wb, self.b("cp")], [wb], bias=self.cpc("eps"))
        else:
            self.act(out, in_, AF.Ln, list(R) + [self.b("cp")], [wb], bias=self.cpc("eps"))
        self.act(out, out, AF.Exp, [wb], [wb], scale=-0.5)

    def recip(self, out, in_, R, W):
        self.P.op("dve", lambda h: h.reciprocal(out=out, in_=in_), reads=R, writes=W)

    def memset(self, ap, val, W, eng="dve"):
        self.P.op(eng, lambda h: h.memset(ap, val), writes=W)

    def dma(self, q, out, in_, sem, R, W):
        self.P.op(q, lambda h: h.dma_start(out=out, in_=in_), reads=R, writes=W, dma=sem)

    def xv(self, k, t0, n):
        return self.xT[:, k * SEQ + t0: k * SEQ + t0 + n]

    def xb(self, k, blk):
        return self.b("x", k, blk)

    def modcol(self, l, j, k, grp):
        c = l * 216 + (j * 8 + k) * 3 + grp
        return self.mod[:, c:c + 1]

    def gscol(self, l, jn, k, grp):
        c = l * 72 + (jn * 8 + k) * 3 + grp
        return self.gs[:, c:c + 1]

    def hgcol(self, l, jn, k, grp):
        c = l * 72 + (jn * 8 + k) * 3 + grp
        return self.hg[:, c:c + 1]

    def prologue(self):
        P = self.P
        bcp = self.b("cp")
        self.dma("sp", self.cp[:], self.cp_d, "cp", [], [bcp])
        cp2 = self.scr_f32(32 * 1024, CP2_N)
        bcp2 = self.sb("cp2")
        self.dma("sp", cp2, self.cp2_d, "cp2", [], [bcp2])

        def c2(name, a=0, n=None):
            o, nn = CP2_OFF[name]
            if n is None:
                n = nn - a
            return cp2[:, o + a: o + a + n]
        self.cpy(self.ones_b[:], c2("ones1024"), [bcp2], [self.b("ones_b")])
        self.cpy(self.ident_b[:], self.cpc("ident"), [bcp], [self.b("ident_b")])
        self.cpy(self.ones128_b[:], c2("ones128"), [bcp2], [self.b("ones_b")])
        self.cpy(self.ones256_b[:], c2("ones256"), [bcp2], [self.b("ones_b")])
        self.dma("pool", self.poolw_b[:], self.poolw_d, "poolw", [], [self.b("poolw_b")])
        bsc = self.b("sc")
        self.act(self.sc[:], c2("cvec"), AF.Silu, [bcp2], [bsc])
        self.cpy(self.sc_b[:], self.sc[:], [bsc], [bsc])
        blg = self.b("lg")
        for l in range(NL):
            sl = slice(l * 8, (l + 1) * 8)
            self.act(self.nlg[:, sl], self.cpc(f"dec{l}"), AF.Exp, [bcp], [blg], scale=-1.0)
            self.act(self.nlg[:, sl], self.nlg[:, sl], AF.Ln, [blg, bcp], [blg], bias=self.cpc("one"))
            self.ts(self.lg[:, sl], self.nlg[:, sl], -1.0, None, ALU.mult, None, [blg], [blg])
            self.act(self.g128[:, sl], self.lg[:, sl], AF.Exp, [blg], [blg], scale=128.0)
        for l in range(NL):
            bank = 6 + l
            for cg in range(18):
                s = (l * 18 + cg) % 4
                wt = self.scr_bf(s * 8192, 4096)
                wb = self.sb("wm", s)
                self.dma("pool", wt, self.wmod_d[l * 18 + cg], f"wm{s}", [], [wb])
                for jj in range(4):
                    j = cg * 4 + jj
                    for k in range(KC):
                        self.mm(self.ps[bank][:, j * 3:(j + 1) * 3],
                                wt[:, k * 512 + jj * 128: k * 512 + (jj + 1) * 128],
                                self.sc_b[:, k * 3:(k + 1) * 3], k == 0, k == KC - 1,
                                [wb, bsc], [self.psb[bank]])
            bm = self.b("mod", l)
            self.tt(self.mod[:, l * 216:(l + 1) * 216], self.ps[bank][:, 0:216], c2(f"bmod{l}"),
                    ALU.add, [self.psb[bank], bcp2], [bm])
            for jn in range(3):
                j = 3 * jn + 1
                self.stt(self.gs[:, l * 72 + jn * 24: l * 72 + (jn + 1) * 24],
                         self.mod[:, l * 216 + j * 24: l * 216 + (j + 1) * 24], 1.0,
                         c2(f"normg{l}", jn * 24, 24), ALU.add, ALU.mult, [bm, bcp2], [self.b("gs", l)])
                jg = 3 * jn + 2
                self.ts(self.hg[:, l * 72 + jn * 24: l * 72 + (jn + 1) * 24],
                        self.mod[:, l * 216 + jg * 24: l * 216 + (jg + 1) * 24],
                        1.0 if jn == 1 else 0.5, None, ALU.mult, None, [bm], [self.b("hg", l)])
        self.new_phase()

    def load_seg(self, src, ntok):
        bident = self.b("cp")
        for tt_ in range(ntok // 128):
            s = tt_ % 8
            st = self.scr_f32(s * 4096, 1024)
            stb = self.sb("xin", s)
            self.dma("sp", st, src[tt_ * 128:(tt_ + 1) * 128, :], f"xin{s}", [], [stb])
            blk = tt_ // 4
            for half in range(2):
                bk = self.bank()
                for kk in range(4):
                    k = half * 4 + kk
                    self.tr(self.ps[bk][:, kk * 128:(kk + 1) * 128], st[:, k * 128:(k + 1) * 128],
                            self.cpc("ident"), [stb, bident], [self.psb[bk]])
                out = self.xT[:, half * 4 * SEQ:(half * 4 + 4) * SEQ].rearrange("p (k t) -> p k t", k=4)[
                    :, :, tt_ * 128:(tt_ + 1) * 128]
                in_ = self.ps[bk][:, :].rearrange("p (k t) -> p k t", k=4)
                eng = "dve" if half == 0 else "act"
                W = [self.xb(half * 4 + kk, blk) for kk in range(4)]
                if eng == "dve":
                    self.cpy(out, in_, [self.psb[bk]], W)
                else:
                    self.act(out, in_, AF.Identity, [self.psb[bk]], W)
        self.new_phase()

    def store_seg(self, dst, ntok):
        bident = self.b("cp")
        for tt_ in range(ntok // 128):
            s = tt_ % 8
            st = self.scr_f32(s * 4096, 1024)
            stb = self.sb("xout", s)
            blk = tt_ // 4
            for half in range(2):
                bk = self.bank()
                for kk in range(4):
                    k = half * 4 + kk
                    self.tr(self.ps[bk][:, kk * 128:(kk + 1) * 128], self.xv(k, tt_ * 128, 128),
                            self.cpc("ident"), [self.xb(k, blk), bident], [self.psb[bk]])
                if half == 0:
                    self.cpy(st[:, 0:512], self.ps[bk][:, :], [self.psb[bk]], [stb])
                else:
                    self.act(st[:, 512:1024], self.ps[bk][:, :], AF.Identity, [self.psb[bk]], [stb])
            self.dma("sp", dst[tt_ * 128:(tt_ + 1) * 128, :], st, f"xout{s}", [stb], [self.b("outd", s)])
        self.new_phase()

    def rms_rstd(self, blk, sq_eng="act", defer=False):
        bk = 6 + (blk % 2)
        for k in range(KC):
            sq, sqb = self.getsq()
            xs = self.xv(k, blk * 512, 512)
            if sq_eng == "act":
                self.act(sq[:], xs, AF.Square, [self.xb(k, blk)], [sqb])
            else:
                self.tt(sq[:], xs, xs, ALU.mult, [self.xb(k, blk)], [sqb])
            self.mm(self.ps[bk][:], self.ones_b[:], sq[:], k == 0, k == KC - 1,
                    [sqb, self.b("ones_b")], [self.psb[bk]])
        ri = self.rstd_i
        self.rstd_i = 1 - ri
        rs, rsb = self.rstd[ri], self.rstdb[ri]

        def fin():
            self.rsqrt(rs[:], self.ps[bk][:], [self.psb[bk]], rsb)
        if defer:
            return rs, rsb, fin
        fin()
        return rs, rsb

    def norm_mod(self, blk, l, jn, grp, hT, hoff, hbuf, rs=None):
        rs, rsb = rs if rs is not None else self.rms_rstd(blk)
        for k in range(KC):
            t, tb = self.gettmp()
            self.tt(t[:], self.xv(k, blk * 512, 512), rs[:], ALU.mult, [self.xb(k, blk), rsb], [tb])
            self.act(hT(k, hoff), t[:], AF.Identity, [tb, self.b("gs", l), self.b("mod", l)], [hbuf(k)],
                     scale=self.gscol(l, jn, k, grp), bias=self.modcol(l, 3 * jn, k, grp))

    def ffn(self, l, f, nblk, grp):
        jn = 0 if f == 0 else 2
        lf = l * 2 + f
        HT = 0
        GT = 16 * 1024
        W1 = 60 * 1024
        W3 = 68 * 1024
        W2 = 76 * 1024
        for g0 in range(0, nblk, 2):
            blks = list(range(g0, min(g0 + 2, nblk)))
            hTf = lambda k, off: self.scr_bf(HT + k * 2048 + off * 2, 512)
            rs0 = self.rms_rstd(blks[0])
            if len(blks) > 1:
                rs1, rs1b, fin1 = self.rms_rstd(blks[1], sq_eng="dve", defer=True)
            self.norm_mod(blks[0], l, jn, grp, hTf, 0, lambda k: self.sb("h", k, 0), rs=rs0)
            if len(blks) > 1:
                fin1()
                self.norm_mod(blks[1], l, jn, grp, hTf, 512, lambda k: self.sb("h", k, 1), rs=(rs1, rs1b))
            for cg in range(11):
                s = self.slot("w13")
                w1t = self.scr_bf(W1 + s * 4096, 2048)
                w3t = self.scr_bf(W3 + s * 4096, 2048)
                self.dma("pool", w1t, self.w1_d[lf * 11 + cg], f"w1{s}", [], [self.sb("w1", s)])
                self.dma("pool", w3t, self.w3_d[lf * 11 + cg], f"w3{s}", [], [self.sb("w3", s)])
                for bi, blk in enumerate(blks):
                    for m in range(2):
                        pa = self.bank()
                        pb = self.bank()
                        for k in range(KC):
                            self.mm(self.ps[pa][:], w1t[:, k * 256 + m * 128: k * 256 + (m + 1) * 128],
                                    self.scr_bf(HT + k * 2048 + bi * 1024, 512), k == 0, k == KC - 1,
                                    [self.sb("w1", s), self.sb("h", k, bi)], [self.psb[pa]])
                        for k in range(KC):
                            self.mm(self.ps[pb][:], w3t[:, k * 256 + m * 128: k * 256 + (m + 1) * 128],
                                    self.scr_bf(HT + k * 2048 + bi * 1024, 512), k == 0, k == KC - 1,
                                    [self.sb("w3", s), self.sb("h", k, bi)], [self.psb[pb]])
                        t, tb = self.gettmp()
                        self.act(t[:], self.ps[pa][:], AF.Silu, [self.psb[pa]], [tb])
                        j = cg * 2 + m
                        self.tt(self.scr_bf(GT + j * 2048 + bi * 1024, 512), t[:], self.ps[pb][:], ALU.mult,
                                [tb, self.psb[pb]], [self.sb("g", j, bi)])
            for dg in range(KC):
                s = self.slot("w2")
                w2t = self.scr_bf(W2 + s * 5632, 2816)
                self.dma("pool", w2t, self.w2_d[lf * 8 + dg], f"w2{s}", [], [self.sb("w2", s)])
                for bi, blk in enumerate(blks):
                    pc = self.bank()
                    for j in range(FC):
                        self.mm(self.ps[pc][:], w2t[:, j * 128:(j + 1) * 128],
                                self.scr_bf(GT + j * 2048 + bi * 1024, 512), j == 0, j == FC - 1,
                                [self.sb("w2", s), self.sb("g", j, bi)], [self.psb[pc]])
                    xs = self.xv(dg, blk * 512, 512)
                    self.stt(xs, self.ps[pc][:], self.hgcol(l, jn, dg, grp), xs, ALU.mult, ALU.add,
                             [self.psb[pc], self.b("hg", l), self.xb(dg, blk)], [self.xb(dg, blk)])
        self.new_phase()

    def slot(self, name):
        v = self.slot_ctr.get(name, 0)
        self.slot_ctr[name] = v + 1
        return v % 2

    def mixer(self, l, nblk, grp, seqs, rope, full, bidx):
        T = nblk * 512
        HT = 0
        PH = 32 * 1024
        WI = 77 * 1024
        hT = lambda k, off: self.scr_bf(HT + k * (2 * T) + off * 2, 512)
        pend = {}
        for blk in range(min(2, nblk)):
            pend[blk] = self.rms_rstd(blk, sq_eng="act" if blk % 2 == 0 else "dve", defer=True)
        for blk in range(nblk):
            rs_, rsb_, fin_ = pend.pop(blk)
            fin_()
            self.norm_mod(blk, l, 1, grp, hT, blk * 512, lambda k, blk=blk: self.sb("h", k, blk), rs=(rs_, rsb_))
            if blk + 2 < nblk:
                pend[blk + 2] = self.rms_rstd(blk + 2, sq_eng="act" if blk % 2 == 0 else "dve", defer=True)
        KEEP = ("h", "wi")

        def load_wi(slot, chunk):
            wt = self.scr_bf(WI + slot * 2048, 1024)
            self.dma("pool", wt, self.win_d[l * 22 + chunk], f"wi{slot}", [], [self.sb("wi", slot)])
            return wt

        def load_wo(chunk, slot=4):
            wt = self.scr_bf(WI + slot * 2048, 1024)
            self.dma("pool", wt, self.wout_d[l * 8 + chunk], f"wi{slot}", [], [self.sb("wi", slot)])
            return wt

        def proj_fm(wt, slot, blk, bk):
            for k in range(KC):
                self.mm(self.ps[bk][:], wt[:, k * 128:(k + 1) * 128], hT(k, blk * 512), k == 0, k == KC - 1,
                        [self.sb("wi", slot), self.sb("h", k, blk)], [self.psb[bk]])

        def wout_partial(terms, blk):
            for dg in range(KC):
                pc = self.bank()
                for ti, (wo, wslot, cat_ap, catb) in enumerate(terms):
                    self.mm(self.ps[pc][:], wo[:, dg * 128:(dg + 1) * 128], cat_ap, ti == 0, ti == len(terms) - 1,
                            [self.sb("wi", wslot), catb], [self.psb[pc]])
                xs = self.xv(dg, blk * 512, 512)
                self.stt(xs, self.ps[pc][:], self.hgcol(l, 1, dg, grp), xs, ALU.mult, ALU.add,
                         [self.psb[pc], self.b("hg", l), self.xb(dg, blk)], [self.xb(dg, blk)])

        bcp = self.b("cp")
        if full:
            for ch in range(2):
                self.new_phase(KEEP + ("pcat",))
                PADL = 8
                LP = T + 16 * len(seqs)
                pp = self.scr_f32(PH, LP)
                ta = self.scr_f32(PH + 4 * LP, LP)
                tb_ = self.scr_f32(PH + 8 * LP, LP)
                pooled = self.scr_bf(PH + 12 * LP, T)
                CATP = PH + 45 * 1024 - 4 * T
                assert PH + 12 * LP + 2 * T <= CATP
                pcat = [self.scr_bf(CATP + c_ * 2 * T, T) for c_ in range(2)]
                cat = pcat[ch]
                bpp, bta, btb = self.sb("pp"), self.sb("ta"), self.sb("tb")
                for si_, (c0_, ncn_, _) in enumerate(seqs):
                    b0_ = c0_ * 128 + 16 * si_
                    L_ = ncn_ * 128
                    self.memset(pp[:, b0_: b0_ + 8], 0.0, [bpp])
                    self.memset(pp[:, b0_ + 8 + L_: b0_ + 16 + L_], 0.0, [bpp])
                    self.memset(ta[:, b0_: b0_ + 1], 0.0, [bta])
                    self.memset(tb_[:, b0_: b0_ + 1], 0.0, [btb])
                    self.memset(tb_[:, b0_ + L_ + 15: b0_ + L_ + 16], 0.0, [btb])
                wt = load_wi(0, ch)
                wo_p = load_wo(ch, 4 + ch)
                if ch == 0:
                    wo_p0 = wo_p

                def ppos(tok):
                    si = 0
                    for i, (c0, ncn, _) in enumerate(seqs):
                        if tok >= c0 * 128:
                            si = i
                    return tok + 16 * si + PADL

                for blk in range(nblk):
                    bk = self.bank()
                    proj_fm(wt, 0, blk, bk)
                    for piece in range(2):
                        t0 = blk * 512 + piece * 256
                        p0 = ppos(t0)
                        self.act(pp[:, p0:p0 + 256], self.ps[bk][:, piece * 256:(piece + 1) * 256], AF.Identity,
                                 [self.psb[bk]], [bpp])
                for (c0, ncn, _) in seqs:
                    L = ncn * 128
                    base = ppos(c0 * 128) - PADL
                    LL = L + 16
                    wins = (POOL_WINDOWS[ch * 2], POOL_WINDOWS[ch * 2 + 1])
                    full = slice(0, 128)
                    hi_half = slice(64, 128)
                    self.tt(ta[full, base + 1: base + LL], pp[full, base: base + LL - 1], pp[full, base + 1: base + LL],
                            ALU.add, [bpp], [bta])
                    have = {2: (ta, bta)}
                    cur, curb, other, otherb = ta, bta, tb_, btb
                    sh = 1
                    ww = 2
                    while ww < max(wins):
                        part = full if min(wins) >= 2 * ww else hi_half
                        self.tt(other[part, base + sh: base + LL - sh], cur[part, base: base + LL - 2 * sh],
                                cur[part, base + 2 * sh: base + LL], ALU.add, [curb], [otherb])
                        cur, curb, other, otherb = other, otherb, cur, curb
                        sh *= 2
                        ww *= 2
                        have[ww] = (cur, curb)
                    for half in range(2):
                        gi = ch * 2 + half
                        w = POOL_WINDOWS[gi]
                        ps_ = slice(half * 64, half * 64 + 64)
                        fb, fbb = have[w]
                        e0 = gi * 16
                        self.tt(fb[ps_, base + PADL: base + PADL + 8], fb[ps_, base + PADL: base + PADL + 8],
                                self.cpc("edge", e0, 8)[ps_, :], ALU.mult, [fbb, bcp], [fbb])
                        self.tt(fb[ps_, base + PADL + L - 8: base + PADL + L], fb[ps_, base + PADL + L - 8: base + PADL + L],
                                self.cpc("edge", e0 + 8, 8)[ps_, :], ALU.mult, [fbb, bcp], [fbb])
                        self.stt(pooled[ps_, c0 * 128: c0 * 128 + L], fb[ps_, base + PADL: base + PADL + L], 1.0 / w,
                                 pp[ps_, base + PADL: base + PADL + L], ALU.mult, ALU.subtract,
                                 [fbb, bpp], [self.sb("pooled")])
                for blk in range(nblk):
                    bk = self.bank()
                    self.mm(self.ps[bk][:], self.poolw_b[:, l * 256 + ch * 128: l * 256 + (ch + 1) * 128],
                            pooled[:, blk * 512:(blk + 1) * 512], True, True,
                            [self.b("poolw_b"), self.sb("pooled")], [self.psb[bk]])
                    self.act(cat[:, blk * 512:(blk + 1) * 512], self.ps[bk][:], AF.Identity, [self.psb[bk], bcp],
                             [self.sb("pcat", ch, blk)], scale=self.cpc(f"pool_scale{l}", ch, 1))
                    if ch == 1:
                        wout_partial([(wo_p0, 4, pcat[0][:, blk * 512:(blk + 1) * 512], self.sb("pcat", 0, blk)),
                                      (wo_p, 5, pcat[1][:, blk * 512:(blk + 1) * 512], self.sb("pcat", 1, blk))], blk)
            self.new_phase(KEEP)
            nsq = len(seqs)
            ZP = T + 30 * nsq
            if (2 * ZP) % 4:
                ZP += 1
            ZB = PH
            DG = ZB + 4 * ZP
            AC = DG + 2 * 31 * 256
            AB = AC + 2 * 2 * 2048
            CC = AB + 4 * 1024
            assert CC + 4 * T <= WI, (CC + 4 * T, WI)
            zb_ = [self.scr_bf(ZB + c * 2 * ZP, ZP) for c in range(2)]
            cat = [self.scr_bf(CC + c * 2 * T, T) for c in range(2)]

            def zpos(tok):
                si = 0
                for i, (c0, ncn, _) in enumerate(seqs):
                    if tok >= c0 * 128:
                        si = i
                return tok + 30 * si + 15

            for c in range(2):
                self.memset(zb_[c], 0.0, [self.sb("z", c)])
                for kk in range(31):
                    dgm = self.scr_bf(DG + (c * 31 + kk) * 256, 128)
                    self.ts(dgm, self.ident_b[:], self.cpc(f"conv_dw{l}", c * 31 + kk, 1), None, ALU.mult, None,
                            [self.b("ident_b"), bcp], [self.sb("dg", c)])
                wa = load_wi(0, 18 + c)
                wg = load_wi(1, 20 + c)
                for blk in range(nblk):
                    ba = self.bank()
                    bg = self.bank()
                    proj_fm(wa, 0, blk, ba)
                    proj_fm(wg, 1, blk, bg)
                    t, tb = self.gettmp()
                    self.act(t[:], self.ps[bg][:], AF.Sigmoid, [self.psb[bg]], [tb])
                    for piece in range(2):
                        p0 = zpos(blk * 512 + piece * 256)
                        self.tt(zb_[c][:, p0:p0 + 256], self.ps[ba][:, piece * 256:(piece + 1) * 256],
                                t[:, piece * 256:(piece + 1) * 256], ALU.mult, [self.psb[ba], tb], [self.sb("z", c)])
            for blk in range(nblk):
                par = blk % 2
                accs = []
                for c in range(2):
                    bk = self.bank()
                    if nsq == 1:
                        pieces = [(0, 512, zpos(blk * 512) - 15)]
                    else:
                        pieces = [(0, 256, zpos(blk * 512) - 15), (256, 256, zpos(blk * 512 + 256) - 15)]
                    for (co, n, zs) in pieces:
                        for kk in range(31):
                            dgm = self.scr_bf(DG + (c * 31 + kk) * 256, 128)
                            self.mm(self.ps[bk][:, co:co + n], dgm, zb_[c][:, zs + kk: zs + kk + n], kk == 0, kk == 30,
                                    [self.sb("dg", c), self.sb("z", c)], [self.psb[bk]])
                    a32 = self.scr_f32(AC + (par * 2 + c) * 2048, 512)
                    a32b = self.sb("a32", par, c)
                    ab = self.scr_bf(AB + (c * 2) * 1024, 512)
                    sq = self.scr_bf(AB + (c * 2 + 1) * 1024, 512)
                    abb = self.sb("ab", c)
                    cb = self.cpc(f"conv_b{l}", c, 1)
                    self.act(a32, self.ps[bk][:], AF.Identity, [self.psb[bk], bcp], [a32b], bias=cb)
                    self.act(ab, self.ps[bk][:], AF.Identity, [self.psb[bk], bcp], [abb], bias=cb)
                    self.act(sq, self.ps[bk][:], AF.Square, [self.psb[bk], bcp], [abb], bias=cb)
                    accs.append((a32, a32b, ab, sq, abb))
                bm_, bv_ = 6, 7
                for c in range(2):
                    self.mm(self.ps[bm_][:], self.ones256_b[:], accs[c][2], c == 0, c == 1,
                            [self.b("ones_b"), accs[c][4]], [self.psb[bm_]])
                for c in range(2):
                    self.mm(self.ps[bv_][:], self.ones256_b[:], accs[c][3], c == 0, c == 1,
                            [self.b("ones_b"), accs[c][4]], [self.psb[bv_]])
                mu, mub = self.gettmp()
                self.act(mu[:], self.ps[bm_][:], AF.Identity, [self.psb[bm_]], [mub])
                var, varb = self.gettmp()
                self.act(var[:], self.ps[bm_][:], AF.Square, [self.psb[bm_]], [varb])
                self.tt(var[:], self.ps[bv_][:], var[:], ALU.subtract, [self.psb[bv_], varb], [varb])
                self.rsqrt(var[:], var[:], [varb], varb, clamp=True)
                for c in range(2):
                    d, db = self.gettmp()
                    self.tt(d[:], accs[c][0], mu[:], ALU.subtract, [accs[c][1], mub], [db])
                    self.tt(d[:], d[:], var[:], ALU.mult, [db, varb], [db])
                    self.act(cat[c][:, blk * 512:(blk + 1) * 512], d[:], AF.Silu, [db, bcp], [self.sb("ccat", c, blk)],
                             scale=self.cpc(f"ln_g{l}", c, 1), bias=self.cpc(f"ln_b{l}", c, 1))
            wo_c = [load_wo(6 + c, 4 + c) for c in range(2)]
            for blk in range(nblk):
                wout_partial([(wo_c[c], 4 + c, cat[c][:, blk * 512:(blk + 1) * 512], self.sb("ccat", c, blk))
                              for c in range(2)], blk)

        nch = T // 128
        QF, QB, KF, KB = PH, PH + 2 * T, PH + 4 * T, PH + 6 * T
        KFT, KBT = PH + 8 * T, PH + 10 * T
        SF, SB = PH + 12 * T, PH + 14 * T
        VT = PH + 16 * T
        RF = PH + 18 * T
        GT = RF + 1024
        ST = GT + 2048
        CAT = ST + 2048
        OB = CAT + 2048
        assert OB + 2048 <= WI, (OB, WI)
        for h in range(4):
            self.new_phase(KEEP)
            qf = self.scr_bf(QF, T)
            qb = self.scr_bf(QB, T)
            kf = self.scr_bf(KF, T)
            kb = self.scr_bf(KB, T)
            kfT = self.scr_bf(KFT, T)
            kbT = self.scr_bf(KBT, T)
            sf = self.scr_bf(SF, T)
            sbk = self.scr_bf(SB, T)
            vT = self.scr_bf(VT, T)
            Rst = self.scr_f32(RF, 256)
            gt = self.scr_f32(GT, 512)
            bgt = self.sb("gt")
            lgf = self.lg[:, l * 8 + h: l * 8 + h + 1]
            lgb = self.lg[:, l * 8 + 4 + h: l * 8 + 4 + h + 1]
            nlgf = self.nlg[:, l * 8 + h: l * 8 + h + 1]
            nlgb = self.nlg[:, l * 8 + 4 + h: l * 8 + 4 + h + 1]
            blg = self.b("lg")
            self.act(gt[:, 0:128], self.cpc("pos1"), AF.Exp, [bcp, blg], [bgt], scale=lgf)
            self.act(gt[:, 128:256], self.cpc("posr"), AF.Exp, [bcp, blg], [bgt], scale=lgb)
            self.act(gt[:, 256:257], self.cpc("pos1c"), AF.Exp, [bcp, blg], [bgt], scale=nlgf)
            self.act(gt[:, 257:258], self.cpc("posrc"), AF.Exp, [bcp, blg], [bgt], scale=nlgb)
            ksc = 128.0 ** -0.5
            self.ts(gt[:, 256:258], gt[:, 256:258], ksc, None, ALU.mult, None, [bgt], [bgt])
            wkcol = (gt[:, 256:257], gt[:, 257:258])
            wq = load_wi(0, 2 + h)
            wk = load_wi(1, 6 + h)
            wv = load_wi(2, 10 + h)
            for blk in range(nblk):
                if rope:
                    rs_ = self.slot("rope")
                    rt = self.ropes[rs_]
                    rtb = self.b("rope", rs_)
                    self.dma("sp", rt[:, :].rearrange("p (a t) -> p a t", a=2),
                             self.rope_d[:, :, blk * 512:(blk + 1) * 512], f"rope{rs_}", [], [rtb])
                items = []
                for which, wt, slot, outs in (("q", wq, 0, ((qf, 0), (qb, 128))), ("k", wk, 1, ())):
                    if not full and which == "q":
                        continue
                    bk = self.bank()
                    proj_fm(wt, slot, blk, bk)
                    if rope:
                        q32, q32b = self.gettmp()
                        self.act(q32[:], self.ps[bk][:], AF.Identity, [self.psb[bk]], [q32b])
                        items.append((which, outs, q32, q32b))
                    else:
                        items.append((which, outs, self.ps[bk], self.psb[bk]))
                for (which, outs, q32, q32b) in items:
                    if rope:
                        bp = self.bank()
                        self.mm(self.ps[bp][:], self.cpc("perm"), q32[:], True, True, [bcp, q32b], [self.psb[bp]])
                        t2, t2b = self.gettmp()
                        self.tt(t2[:], self.ps[bp][:], rt[:, 512:1024], ALU.mult, [self.psb[bp], rtb], [t2b])
                        self.tt(q32[:], q32[:], rt[:, 0:512], ALU.mult, [q32b, rtb], [q32b])
                        if which == "k":
                            self.tt(kf[:, blk * 512:(blk + 1) * 512], q32[:], t2[:], ALU.add, [q32b, t2b],
                                    [self.sb("k", 256, blk)])
                            continue
                        self.tt(q32[:], q32[:], t2[:], ALU.add, [q32b, t2b], [q32b])
                    elif which == "k":
                        self.cpy(kf[:, blk * 512:(blk + 1) * 512], q32[:], [q32b], [self.sb("k", 256, blk)])
                        continue
                    src, srcb = q32[:], q32b
                    for (dst, go) in outs:
                        o3 = dst[:, blk * 512:(blk + 1) * 512].rearrange("p (c i) -> p c i", c=4)
                        i3 = src.rearrange("p (c i) -> p c i", c=4)
                        g3 = gt[:, go:go + 128].unsqueeze(1).to_broadcast([128, 4, 128])
                        self.tt(o3, i3, g3, ALU.mult, [srcb, bgt], [self.sb(which, go, blk)])
            for cgp in range(nch // 4):
                bk = self.bank()
                for cc in range(4):
                    n = cgp * 4 + cc
                    blk = n // 4
                    for k in range(KC):
                        self.mm(self.ps[bk][:, cc * 128:(cc + 1) * 128], self.scr_bf(HT + k * (2 * T) + n * 256, 128), wv[:, k * 128:(k + 1) * 128],
                                k == 0, k == KC - 1, [self.sb("h", k, blk), self.sb("wi", 2)], [self.psb[bk]])
                self.act(vT[:, cgp * 512:(cgp + 1) * 512], self.ps[bk][:], AF.Identity, [self.psb[bk]], [self.sb("vT", cgp)])
            for cgp in range(nch // 4):
                bk = self.bank()
                pst = self.ps[bk][:, :].bitcast(BF16)
                for cc in range(4):
                    n = cgp * 4 + cc
                    self.tr(pst[:, cc * 128:(cc + 1) * 128], kf[:, n * 128:(n + 1) * 128], self.ident_b[:],
                            [self.sb("k", 256, n // 4), self.b("ident_b")], [self.psb[bk]])
                self.act(kfT[:, cgp * 512:(cgp + 1) * 512], pst[:, 0:512], AF.Identity, [self.psb[bk], bgt],
                         [self.sb("kfT", cgp)], scale=wkcol[0])
                self.act(kbT[:, cgp * 512:(cgp + 1) * 512], pst[:, 0:512], AF.Identity, [self.psb[bk], bgt],
                         [self.sb("kbT", cgp)], scale=wkcol[1])
            if full:
                wg = load_wi(3, 14 + h)
                for blk in range(nblk):
                    bg = self.bank()
                    proj_fm(wg, 3, blk, bg)
                    self.act(self.sgall[:, blk * 512:(blk + 1) * 512], self.ps[bg][:], AF.Silu, [self.psb[bg]],
                             [self.b("sg", blk)])
            for (c0, ncn, bi_) in seqs:
                for idx in range(ncn):
                    for d_ in range(2):
                        kT, nm = (kfT, "kfT") if d_ == 0 else (kbT, "kbT")
                        sdst = sf if d_ == 0 else sbk
                        g128c = self.g128[:, l * 8 + d_ * 4 + h: l * 8 + d_ * 4 + h + 1]
                        R = Rst[:, d_ * 128:(d_ + 1) * 128]
                        Rb = self.sb("R", d_)
                        st_off = (((l * 2 + bi_) * 2 + d_) * 4 + h) * 128
                        S0 = self.states[:, st_off: st_off + 128]
                        S0b = self.b("state", l, bi_, d_, h)
                        n = c0 + idx if d_ == 0 else c0 + ncn - 1 - idx
                        if full:
                            if idx == 0:
                                if rope:
                                    self.cpy(sdst[:, n * 128:(n + 1) * 128], S0, [S0b], [self.sb("S", d_, n)])
                                else:
                                    self.memset(sdst[:, n * 128:(n + 1) * 128], 0.0, [self.sb("S", d_, n)])
                            else:
                                self.act(sdst[:, n * 128:(n + 1) * 128], R, AF.Identity, [Rb, blg], [self.sb("S", d_, n)],
                                         scale=g128c)
                        last = idx == ncn - 1
                        if last and rope:
                            continue
                        bk = self.bank()
                        self.mm(self.ps[bk][:, 0:128], kT[:, n * 128:(n + 1) * 128], vT[:, n * 128:(n + 1) * 128], True, True,
                                [self.sb(nm, n // 4), self.sb("vT", n // 4)], [self.psb[bk]])
                        if idx == 0:
                            if rope:
                                self.tt(R, self.ps[bk][:, 0:128], S0, ALU.add, [self.psb[bk], S0b], [Rb])
                            else:
                                self.cpy(R, self.ps[bk][:, 0:128], [self.psb[bk]], [Rb])
                        else:
                            self.stt(R, R, g128c, self.ps[bk][:, 0:128], ALU.mult, ALU.add, [Rb, blg, self.psb[bk]], [Rb])
                        if last and not rope:
                            self.act(S0, R, AF.Identity, [Rb, blg], [S0b], scale=g128c)
            if not full:
                continue
            odd = h % 2 == 1
            if odd:
                wo_prev = load_wo(2 + h - 1, 4)
                wo_cur = load_wo(2 + h, 5)

            def stageA1(blk):
                sts = []
                for d_ in range(2):
                    kk_, kgo = kf, 256
                    qq_, qgo = (qf, 0) if d_ == 0 else (qb, 128)
                    bk = self.bank()
                    for cc in range(4):
                        n = blk * 4 + cc
                        self.mm(self.ps[bk][:, cc * 128:(cc + 1) * 128], kk_[:, n * 128:(n + 1) * 128],
                                qq_[:, n * 128:(n + 1) * 128], True, True,
                                [self.sb("k", kgo, blk), self.sb("q", qgo, blk)], [self.psb[bk]])
                    sT = self.scr_bf(ST + d_ * 1024, 512)
                    mk = self.cpc("maskf" if d_ == 0 else "maskb").unsqueeze(1).to_broadcast([128, 4, 128])
                    self.stt(sT.rearrange("p (c i) -> p c i", c=4), self.ps[bk][:, :].rearrange("p (c i) -> p c i", c=4),
                             wkcol[d_], mk, ALU.mult, ALU.mult, [self.psb[bk], bcp, bgt], [self.sb("sT", d_)])
                    sts.append(sT)
                bo = 4 + blk % 2
                for cc in range(4):
                    n = blk * 4 + cc
                    oc = self.ps[bo][:, cc * 128:(cc + 1) * 128]
                    vch = vT[:, n * 128:(n + 1) * 128]
                    self.mm(oc, vch, sts[0][:, cc * 128:(cc + 1) * 128], True, False,
                            [self.sb("vT", n // 4), self.sb("sT", 0)], [self.psb[bo]])
                    self.mm(oc, vch, sts[1][:, cc * 128:(cc + 1) * 128], False, False,
                            [self.sb("vT", n // 4), self.sb("sT", 1)], [self.psb[bo]])
                    self.mm(oc, sf[:, n * 128:(n + 1) * 128], qf[:, n * 128:(n + 1) * 128], False, False,
                            [self.sb("S", 0, n), self.sb("q", 0, blk)], [self.psb[bo]])
                    self.mm(oc, sbk[:, n * 128:(n + 1) * 128], qb[:, n * 128:(n + 1) * 128], False, True,
                            [self.sb("S", 1, n), self.sb("q", 128, blk)], [self.psb[bo]])
                return bo

            def stageA2(blk, bo):
                par = blk % 2
                rb = self.b("rope", par)
                o32 = self.ropes[par][:, 0:512]
                mu = self.ropes[par][:, 512:1024]
                var, varb = self.rstd[par], self.rstdb[par]
                self.act(o32, self.ps[bo][:], AF.Identity, [self.psb[bo]], [rb])
                ob = self.scr_bf(OB, 512)
                osq = self.scr_bf(OB + 1024, 512)
                obb = self.sb("ob")
                self.act(ob, self.ps[bo][:], AF.Identity, [self.psb[bo]], [obb])
                self.act(osq, self.ps[bo][:], AF.Square, [self.psb[bo]], [obb])
                self.mm(self.ps[6][:], self.ones128_b[:], ob, True, True, [self.b("ones_b"), obb], [self.psb[6]])
                self.mm(self.ps[7][:], self.ones128_b[:], osq, True, True, [self.b("ones_b"), obb], [self.psb[7]])
                self.act(mu, self.ps[6][:], AF.Identity, [self.psb[6]], [rb])
                self.act(var[:], self.ps[6][:], AF.Square, [self.psb[6]], [varb])
                self.tt(var[:], self.ps[7][:], var[:], ALU.subtract, [self.psb[7], varb], [varb])
                self.rsqrt(var[:], var[:], [varb], varb, clamp=True)

            def stageB(blk):
                par = blk % 2
                rb = self.b("rope", par)
                o32 = self.ropes[par][:, 0:512]
                mu = self.ropes[par][:, 512:1024]
                var, varb = self.rstd[par], self.rstdb[par]
                self.tt(o32, o32, mu, ALU.subtract, [rb], [rb])
                self.tt(o32, o32, var[:], ALU.mult, [rb, varb], [rb])
                if odd:
                    cat, catb = self.scr_bf(CAT + par * 1024, 512), self.sb("hcat", par)
                else:
                    cat, catb = self.hprev[:, blk * 512:(blk + 1) * 512], self.b("hprev", blk)
                self.stt(cat, o32, self.cpc(f"gn_g{l}", h, 1), self.sgall[:, blk * 512:(blk + 1) * 512], ALU.mult, ALU.mult,
                         [rb, bcp, self.b("sg", blk)], [catb])

            def stageC(blk):
                if not odd:
                    return
                par = blk % 2
                wout_partial([(wo_prev, 4, self.hprev[:, blk * 512:(blk + 1) * 512], self.b("hprev", blk)),
                              (wo_cur, 5, self.scr_bf(CAT + par * 1024, 512), self.sb("hcat", par))], blk)

            self.bank_set = [0, 1, 2, 3]
            bo_ = stageA1(0)
            stageA2(0, bo_)
            for blk in range(nblk):
                if blk + 1 < nblk:
                    bo_ = stageA1(blk + 1)
                stageB(blk)
                stageC(blk)
                if blk + 1 < nblk:
                    stageA2(blk + 1, bo_)
            self.bank_set = [0, 1, 2, 3, 4, 5]
        self.new_phase()

    def final_norm(self, nblk):
        for blk in range(nblk):
            rs, rsb = self.rms_rstd(blk)
            for k in range(KC):
                xs = self.xv(k, blk * 512, 512)
                self.stt(xs, xs, self.cpc("final_g", k, 1), rs[:], ALU.mult, ALU.mult,
                         [self.xb(k, blk), self.b("cp"), rsb], [self.xb(k, blk)])

    def build(self):
        stop = self.dbg_stop
        self.prologue()
        ctx_seqs = [(0, 2, 0), (2, 2, 1)]
        self.load_seg(self.ctx_d, 512)
        stage = 0
        done = False
        for l in range(NL):
            last = l == NL - 1
            self.ffn(l, 0, 1, 0)
            stage += 1
            if stop == ("c", stage):
                done = True
                break
            self.mixer(l, 1, 0, ctx_seqs, rope=False, full=not last, bidx=None)
            stage += 1
            if stop == ("c", stage):
                done = True
                break
            if not last:
                self.ffn(l, 1, 1, 0)
                stage += 1
                if stop == ("c", stage):
                    done = True
                    break
        if stop is not None and stop[0] == "c":
            self.store_seg(self.dbgy_d, 512)
        else:
            for bi in range(2):
                self.load_seg(self.x_d[bi], SEQ)
                stage = 0
                done = False
                for l in range(NL):
                    self.ffn(l, 0, 4, 1 + bi)
                    stage += 1
                    if stop == ("x", stage):
                        done = True
                        break
                    self.mixer(l, 4, 1 + bi, [(0, 16, bi)], rope=True, full=True, bidx=bi)
                    stage += 1
                    if stop == ("x", stage):
                        done = True
                        break
                    self.ffn(l, 1, 4, 1 + bi)
                    stage += 1
                    if stop == ("x", stage):
                        done = True
                        break
                if not done:
                    self.final_norm(4)
                self.store_seg(self.out_d[bi], SEQ)
        outs = [self.b("outd", i) for i in range(8)]
        self.P.op("sp", lambda h: h.nop(), reads=outs)
        self.P.emit()
        return self.nc


_CACHE = {}


def kernel(x, c, ctx, c_ctx, w_mod, b_mod, norm_g, ffn_w1, ffn_w3, ffn_w2, w_in, w_out,
           pool_w, pool_scale, ret_decay_fwd, ret_decay_bwd, ret_gn_g, conv_dw, conv_b,
           conv_ln_g, conv_ln_b, final_g, _dbg_stop=None, _cores=None):
    inp = dict(x=x, c=c, ctx=ctx, c_ctx=c_ctx, w_mod=w_mod, b_mod=b_mod, norm_g=norm_g, ffn_w1=ffn_w1,
               ffn_w3=ffn_w3, ffn_w2=ffn_w2, w_in=w_in, w_out=w_out, pool_w=pool_w, pool_scale=pool_scale,
               ret_decay_fwd=ret_decay_fwd, ret_decay_bwd=ret_decay_bwd, ret_gn_g=ret_gn_g, conv_dw=conv_dw,
               conv_b=conv_b, conv_ln_g=conv_ln_g, conv_ln_b=conv_ln_b, final_g=final_g)
    inp = {k: np.asarray(v) for k, v in inp.items()}
    cores = list(range(NCORES)) if _cores is None else _cores
    W = relayout_weights(inp)
    rope = build_rope()
    poolw = build_poolw(inp)
    k = K(dbg_stop=_dbg_stop)
    nc = k.build()
    in_maps = []
    xs = np.asarray(inp["x"], np.float32)
    cs = np.asarray(inp["ctx"], np.float32)
    for core in cores:
        m = dict(W)
        m["x"] = np.ascontiguousarray(xs[2 * core: 2 * core + 2])
        m["ctx"] = np.ascontiguousarray(cs[2 * core: 2 * core + 2].reshape(2 * CTX, D))
        m["cp"], m["cp2"] = build_cp(inp, core)
        m["rope"] = rope
        m["poolw"] = poolw
        in_maps.append(m)
    res = run_bass_kernel_spmd(nc, in_maps, core_ids=list(range(len(cores))))
    if _dbg_stop is not None:
        return res.results
    out = np.concatenate([np.asarray(r["out"], np.float32) for r in res.results], axis=0)
    return out
```

```python
import numpy as np
import concourse.bass as bass
import concourse.mybir as mybir
from concourse.bass_utils import run_bass_kernel_spmd

F32 = mybir.dt.float32
BF16 = mybir.dt.bfloat16
ALU = mybir.AluOpType
AF = mybir.ActivationFunctionType

D = 1024
KC = 8
DFF = 2816
FC = 22
SEQ = 2048
CTX = 256
NL = 2
EPS = 1e-6
NCORES = 8


class Buf:
    __slots__ = ("name", "last_w", "readers")

    def __init__(self, name, inherit=None):
        self.name = name
        self.last_w = None
        self.readers = dict(inherit) if inherit else {}


class Tok:
    __slots__ = ("key", "ord", "clock", "op")

    def __init__(self, key, ord_, clock, op):
        self.key = key
        self.ord = ord_
        self.clock = clock
        self.op = op


class Op:
    __slots__ = ("eng", "fn", "waits", "tok", "signal", "semval", "dma_sem")


class Prog:
    ENGS = ("pe", "act", "dve", "pool", "sp")

    def __init__(self, nc):
        self.nc = nc
        self.h = {"pe": nc.tensor, "act": nc.scalar, "dve": nc.vector,
                  "pool": nc.gpsimd, "sp": nc.sync}
        self.ops = []
        self.nops = {e: 0 for e in self.ENGS}
        self.seen = {e: {} for e in self.ENGS}
        self.dma_count = {}
        self.sems = {}

    def op(self, eng, fn, reads=(), writes=(), dma=None):
        seen = self.seen[eng]
        deps = {}

        def add(t, raw):
            if t is None:
                return
            if t.key == ("e", eng) and eng in ("pe", "sp"):
                return
            k = t.key
            if k not in deps or deps[k].ord < t.ord:
                deps[k] = t

        for b in reads:
            add(b.last_w, True)
        for b in writes:
            add(b.last_w, True)
            for t in b.readers.values():
                add(t, False)
        o = Op()
        o.eng = eng
        o.fn = fn
        o.waits = []
        o.signal = False
        o.semval = None
        o.dma_sem = dma
        for k, t in deps.items():
            if seen.get(k, 0) >= t.ord:
                continue
            o.waits.append(t)
            if t.op is not None:
                t.op.signal = True
            seen[k] = t.ord
            for kk, vv in t.clock.items():
                if seen.get(kk, 0) < vv:
                    seen[kk] = vv
        if dma is None:
            self.nops[eng] += 1
            tok = Tok(("e", eng), self.nops[eng], dict(seen), o)
        else:
            self.dma_count[dma] = self.dma_count.get(dma, 0) + 16
            tok = Tok(("d", dma), self.dma_count[dma], dict(seen), None)
        o.tok = tok
        self.ops.append(o)
        for b in writes:
            b.last_w = tok
            b.readers = {}
        for b in reads:
            if b in writes:
                continue
            b.readers[tok.key] = tok
        return tok

    def _sem(self, name):
        if name not in self.sems:
            self.sems[name] = self.nc.alloc_semaphore(name)
        return self.sems[name]

    def emit(self):
        cnt = {e: 0 for e in self.ENGS}
        for o in self.ops:
            h = self.h[o.eng]
            for t in o.waits:
                if t.key[0] == "e":
                    h.wait_ge(self._sem("s_" + t.key[1]), t.op.semval)
                else:
                    h.wait_ge(self._sem("d_" + t.key[1]), t.ord)
            ins = o.fn(h)
            if o.dma_sem is not None:
                ins.then_inc(self._sem("d_" + o.dma_sem), 16)
            elif o.signal:
                cnt[o.eng] += 1
                o.semval = cnt[o.eng]
                ins.then_inc(self._sem("s_" + o.eng), 1)


def _cp_layout():
    off = {}
    c = 0

    def add(name, n):
        nonlocal c
        off[name] = (c, n)
        c += n

    add("ident", 128)
    add("perm", 128)
    add("maskf", 128)
    add("maskb", 128)
    add("pos1", 128)
    add("posr", 128)
    add("pos1c", 1)
    add("posrc", 1)
    add("eps", 1)
    add("one", 1)
    add("edge", 4 * 16)
    add("final_g", 8)
    for l in range(NL):
        add(f"pool_scale{l}", 2)
        add(f"gn_g{l}", 4)
        add(f"conv_b{l}", 2)
        add(f"ln_g{l}", 2)
        add(f"ln_b{l}", 2)
        add(f"conv_dw{l}", 62)
        add(f"dec{l}", 8)
    return off, c


def _cp2_layout():
    off = {}
    c = 0
    for name, n in (("ones1024", 128), ("ones128", 128), ("ones256", 128), ("cvec", 24),
                    ("normg0", 72), ("bmod0", 216), ("normg1", 72), ("bmod1", 216)):
        off[name] = (c, n)
        c += n
    return off, c


CP_OFF, CP_N = _cp_layout()
CP2_OFF, CP2_N = _cp2_layout()
POOL_WINDOWS = (2, 4, 8, 16)


def _fm(v):
    return np.ascontiguousarray(np.asarray(v, np.float32).reshape(-1, 128).T)


def build_cp(inp, core):
    cp = np.zeros((128, CP_N), np.float32)
    cp2 = np.zeros((128, CP2_N), np.float32)

    def put(name, arr):
        if name in CP2_OFF:
            o, n = CP2_OFF[name]
            cp2[:, o:o + n] = np.asarray(arr, np.float32).reshape(128, n)
            return
        o, n = CP_OFF[name]
        arr = np.asarray(arr, np.float32).reshape(128, n)
        cp[:, o:o + n] = arr

    put("ident", np.eye(128, dtype=np.float32))
    perm = np.zeros((128, 128), np.float32)
    for m in range(128):
        partner = m + 32 if (m % 64) < 32 else m - 32
        perm[partner, m] = 1.0
    put("perm", perm)
    put("ones1024", np.full((128, 128), 1.0 / 1024, np.float32))
    put("ones128", np.full((128, 128), 1.0 / 128, np.float32))
    put("ones256", np.full((128, 128), 1.0 / 256, np.float32))
    jj = np.arange(128)[:, None]
    ii = np.arange(128)[None, :]
    put("maskf", (ii >= jj).astype(np.float32))
    put("maskb", (jj > ii).astype(np.float32))
    put("pos1", np.tile(np.arange(1, 129, dtype=np.float32)[None, :], (128, 1)))
    put("posr", np.tile((128 - np.arange(128, dtype=np.float32))[None, :], (128, 1)))
    put("pos1c", np.arange(1, 129, dtype=np.float32)[:, None])
    put("posrc", (128 - np.arange(128, dtype=np.float32))[:, None])
    put("eps", np.full((128, 1), EPS, np.float32))
    put("one", np.ones((128, 1), np.float32))
    edge = np.zeros((4, 16), np.float32)
    for gi, w in enumerate(POOL_WINDOWS):
        for t in range(8):
            cl = min(t + w // 2, w) if True else w
            cl = (t + w // 2) - max(t - w // 2, 0)
            edge[gi, t] = w / cl
            d = 8 - t
            cr = min(d, w // 2) + w // 2
            edge[gi, 8 + t] = w / cr
    put("edge", np.tile(edge.reshape(1, 64), (128, 1)))
    b0 = 2 * core
    cv = np.stack([np.asarray(inp["c_ctx"], np.float32),
                   np.asarray(inp["c"][b0], np.float32),
                   np.asarray(inp["c"][b0 + 1], np.float32)], axis=0)
    put("cvec", cv.reshape(3, 8, 128).transpose(2, 1, 0).reshape(128, 24))
    put("final_g", _fm(inp["final_g"]))
    for l in range(NL):
        ng = np.asarray(inp["norm_g"][l], np.float32).reshape(3, 8, 128).transpose(2, 0, 1)
        put(f"normg{l}", np.repeat(ng[:, :, :, None], 3, axis=3).reshape(128, 72))
        bm = _fm(inp["b_mod"][l])
        put(f"bmod{l}", np.repeat(bm[:, :, None], 3, axis=2).reshape(128, 216))
        put(f"pool_scale{l}", _fm(inp["pool_scale"][l]))
        put(f"gn_g{l}", _fm(inp["ret_gn_g"][l]))
        put(f"conv_b{l}", _fm(inp["conv_b"][l]))
        put(f"ln_g{l}", _fm(inp["conv_ln_g"][l]))
        put(f"ln_b{l}", _fm(inp["conv_ln_b"][l]))
        dw = np.asarray(inp["conv_dw"][l], np.float32)
        put(f"conv_dw{l}", dw.reshape(31, 2, 128).transpose(2, 1, 0).reshape(128, 62))
        dec = np.concatenate([np.asarray(inp["ret_decay_fwd"][l], np.float32),
                              np.asarray(inp["ret_decay_bwd"][l], np.float32)])
        put(f"dec{l}", np.tile(dec[None, :], (128, 1)))
    return cp, cp2


def build_poolw(inp):
    out = np.zeros((128, NL, 2, 128), np.float32)
    for l in range(NL):
        pw = np.asarray(inp["pool_w"][l], np.float32)
        for ch in range(2):
            out[0:64, l, ch, 0:64] = pw[2 * ch]
            out[64:128, l, ch, 64:128] = pw[2 * ch + 1]
    return out.reshape(128, NL * 256)


def build_rope():
    n_freq = 32
    inv = (10000.0 ** (-np.arange(n_freq, dtype=np.float32) / n_freq)).astype(np.float32)
    t = np.arange(SEQ)
    row = (t // 64).astype(np.float32)
    col = (t % 64).astype(np.float32)
    tab = np.zeros((128, 2, SEQ), np.float32)
    for f in range(128):
        pos = row if f < 64 else col
        ang = (pos * inv[f % 32]).astype(np.float32)
        tab[f, 0] = np.cos(ang)
        s = np.sin(ang)
        tab[f, 1] = -s if (f % 64) < 32 else s
    return tab


def relayout_weights(inp):
    w = {}
    w1 = np.asarray(inp["ffn_w1"], np.float32).reshape(4, 8, 128, 11, 256)
    w["w1"] = np.ascontiguousarray(w1.transpose(0, 3, 2, 1, 4)).reshape(44, 128, 2048)
    w3 = np.asarray(inp["ffn_w3"], np.float32).reshape(4, 8, 128, 11, 256)
    w["w3"] = np.ascontiguousarray(w3.transpose(0, 3, 2, 1, 4)).reshape(44, 128, 2048)
    w2 = np.asarray(inp["ffn_w2"], np.float32).reshape(4, 22, 128, 8, 128)
    w["w2"] = np.ascontiguousarray(w2.transpose(0, 3, 2, 1, 4)).reshape(32, 128, 2816)
    wi = np.asarray(inp["w_in"], np.float32).reshape(2, 8, 128, 22, 128)
    w["win"] = np.ascontiguousarray(wi.transpose(0, 3, 2, 1, 4)).reshape(44, 128, 1024)
    w["wout"] = np.ascontiguousarray(np.asarray(inp["w_out"], np.float32).reshape(16, 128, 1024))
    wm = np.asarray(inp["w_mod"], np.float32).reshape(2, 8, 128, 18, 512)
    w["wmod"] = np.ascontiguousarray(wm.transpose(0, 3, 2, 1, 4)).reshape(36, 128, 4096)
    return w


class K:
    def __init__(self, dbg_stop=None):
        self.dbg_stop = dbg_stop
        nc = bass.Bass("TRN2", target_bir_lowering=False)
        self.nc = nc
        self.P = Prog(nc)
        dt = nc.dram_tensor
        self.x_d = dt("x", [2, SEQ, D], F32, kind="ExternalInput").ap()
        self.ctx_d = dt("ctx", [2 * CTX, D], F32, kind="ExternalInput").ap()
        self.cp_d = dt("cp", [128, CP_N], F32, kind="ExternalInput").ap()
        self.cp2_d = dt("cp2", [128, CP2_N], F32, kind="ExternalInput").ap()
        self.rope_d = dt("rope", [128, 2, SEQ], F32, kind="ExternalInput").ap()
        self.poolw_d = dt("poolw", [128, NL * 256], F32, kind="ExternalInput").ap()
        self.w1_d = dt("w1", [44, 128, 2048], F32, kind="ExternalInput").ap()
        self.w3_d = dt("w3", [44, 128, 2048], F32, kind="ExternalInput").ap()
        self.w2_d = dt("w2", [32, 128, 2816], F32, kind="ExternalInput").ap()
        self.win_d = dt("win", [44, 128, 1024], F32, kind="ExternalInput").ap()
        self.wout_d = dt("wout", [16, 128, 1024], F32, kind="ExternalInput").ap()
        self.wmod_d = dt("wmod", [36, 128, 4096], F32, kind="ExternalInput").ap()
        self.out_d = dt("out", [2, SEQ, D], F32, kind="ExternalOutput").ap()
        if dbg_stop is not None:
            self.dbgy_d = dt("dbgy", [2 * CTX, D], F32, kind="ExternalOutput").ap()

        A = nc.alloc_sbuf_tensor
        self.xT = A("xT", [128, KC * SEQ], F32)
        self.cp = A("cp_s", [128, CP_N], F32)
        self.mod = A("mod", [128, NL * 216], F32)
        self.gs = A("gs", [128, NL * 72], F32)
        self.hg = A("hg", [128, NL * 72], F32)
        self.sc = A("sc", [128, 24], F32)
        self.sc_b = A("sc_b", [128, 24], BF16)
        self.lg = A("lg", [128, NL * 8], F32)
        self.nlg = A("nlg", [128, NL * 8], F32)
        self.g128 = A("g128", [128, NL * 8], F32)
        self.ones_b = A("ones_b", [128, 128], BF16)
        self.ident_b = A("ident_b", [128, 128], BF16)
        self.ones128_b = A("ones128_b", [128, 128], BF16)
        self.ones256_b = A("ones256_b", [128, 128], BF16)
        self.cen_b = A("cen_b", [128, 128], BF16)
        self.poolw_b = A("poolw_b", [128, NL * 256], BF16)
        self.states = A("states", [128, NL * 2 * 2 * 4 * 128], BF16)
        self.sgall = A("sgall", [128, SEQ], BF16)
        self.hprev = A("hprev", [128, SEQ], BF16)
        self.NTMP = 7
        self.tmp = [A(f"tmp{i}", [128, 512], F32) for i in range(self.NTMP)]
        self.sqb = [A(f"sqb{i}", [128, 512], BF16) for i in range(2)]
        self.rstd = [A(f"rstd{i}", [128, 512], F32) for i in range(2)]
        self.rstdb = [Buf(f"rstd{i}") for i in range(2)]
        self.rstd_i = 0
        self.ropes = [A(f"rope{i}", [128, 2 * 512], F32) for i in range(2)]
        self.SCR_BYTES = 89088 + 2048
        self.scr = A("scr", [128, self.SCR_BYTES // 4], F32)
        self.ps = [nc.alloc_psum_tensor(f"ps{i}", [128, 512], F32) for i in range(8)]
        self.psb = [Buf(f"ps{i}") for i in range(8)]
        self.bank_i = 0
        self.bank_set = [0, 1, 2, 3, 4, 5]
        self.tmp_i = 0
        self.tmpb = [Buf(f"tmp{i}") for i in range(self.NTMP)]
        self.sqb_i = 0
        self.sqbb = [Buf(f"sqb{i}") for i in range(2)]
        self.bufs = {}
        self.scr_bufs = {}
        self.inherit = {}
        self.slot_ctr = {}

    def b(self, *key):
        if key not in self.bufs:
            self.bufs[key] = Buf(str(key))
        return self.bufs[key]

    def sb(self, *key):
        if key not in self.scr_bufs:
            self.scr_bufs[key] = Buf(str(key), self.inherit)
        return self.scr_bufs[key]

    def new_phase(self, keep=()):
        keepd = {}
        for key, bf in self.scr_bufs.items():
            if key[0] in keep:
                keepd[key] = bf
                continue
            toks = list(bf.readers.values())
            if bf.last_w is not None:
                toks.append(bf.last_w)
            for t in toks:
                if t.key not in self.inherit or self.inherit[t.key].ord < t.ord:
                    self.inherit[t.key] = t
        self.scr_bufs = keepd

    def scr_f32(self, off_bytes, n):
        assert off_bytes % 4 == 0 and off_bytes + 4 * n <= self.SCR_BYTES, (off_bytes, n)
        return self.scr[:, off_bytes // 4: off_bytes // 4 + n]

    def scr_bf(self, off_bytes, n):
        assert off_bytes % 4 == 0 and n % 2 == 0 and off_bytes + 2 * n <= self.SCR_BYTES, (off_bytes, n)
        return self.scr[:, off_bytes // 4: off_bytes // 4 + n // 2].bitcast(BF16)

    def bank(self):
        bs = self.bank_set
        self.bank_i = (self.bank_i + 1) % len(bs)
        return bs[self.bank_i]

    def gettmp(self):
        i = self.tmp_i
        self.tmp_i = (i + 1) % self.NTMP
        return self.tmp[i], self.tmpb[i]

    def getsq(self):
        i = self.sqb_i
        self.sqb_i = (i + 1) % 2
        return self.sqb[i], self.sqbb[i]

    def cpc(self, name, a=0, n=None):
        o, nn = CP_OFF[name]
        if n is None:
            n = nn - a
        return self.cp[:, o + a: o + a + n]

    def mm(self, out, lhsT, rhs, start, stop, R, W):
        self.P.op("pe", lambda h: h.matmul(out, lhsT, rhs, start=start, stop=stop), reads=R, writes=W)

    def tr(self, out, in_, ident, R, W):
        self.P.op("pe", lambda h: h.transpose(out, in_, ident), reads=R, writes=W)

    def act(self, out, in_, func, R, W, scale=1.0, bias=0.0):
        self.P.op("act", lambda h: h.activation(out=out, in_=in_, func=func, bias=bias, scale=scale),
                  reads=R, writes=W)

    def tt(self, out, in0, in1, op, R, W, eng="dve"):
        self.P.op(eng, lambda h: h.tensor_tensor(out=out, in0=in0, in1=in1, op=op), reads=R, writes=W)

    def stt(self, out, in0, scalar, in1, op0, op1, R, W, eng="dve"):
        self.P.op(eng, lambda h: h.scalar_tensor_tensor(out=out, in0=in0, scalar=scalar, in1=in1,
                                                        op0=op0, op1=op1), reads=R, writes=W)

    def ts(self, out, in0, s1, s2, op0, op1, R, W, eng="dve"):
        if s2 is None:
            self.P.op(eng, lambda h: h.tensor_scalar(out=out, in0=in0, scalar1=s1, scalar2=None, op0=op0),
                      reads=R, writes=W)
        else:
            self.P.op(eng, lambda h: h.tensor_scalar(out=out, in0=in0, scalar1=s1, scalar2=s2, op0=op0, op1=op1),
                      reads=R, writes=W)

    def cpy(self, out, in_, R, W, eng="dve"):
        self.P.op(eng, lambda h: h.tensor_copy(out=out, in_=in_), reads=R, writes=W)

    def rsqrt(self, out, in_, R, wb, clamp=False):
        if clamp:
            self.act(out, in_, AF.Relu, list(R), [wb])
            self.act(out, out, AF.Ln, [wb, self.b("cp")], [wb], bias=self.cpc("eps"))
        else:
            self.act(out, in_, AF.Ln, list(R) + [self.b("cp")], [wb], bias=self.cpc("eps"))
        self.act(out, out, AF.Exp, [wb], [wb], scale=-0.5)

    def recip(self, out, in_, R, W):
        self.P.op("dve", lambda h: h.reciprocal(out=out, in_=in_), reads=R, writes=W)

    def memset(self, ap, val, W, eng="dve"):
        self.P.op(eng, lambda h: h.memset(ap, val), writes=W)

    def dma(self, q, out, in_, sem, R, W):
        self.P.op(q, lambda h: h.dma_start(out=out, in_=in_), reads=R, writes=W, dma=sem)

    def xv(self, k, t0, n):
        return self.xT[:, k * SEQ + t0: k * SEQ + t0 + n]

    def xb(self, k, blk):
        return self.b("x", k, blk)

    def modcol(self, l, j, k, grp):
        c = l * 216 + (j * 8 + k) * 3 + grp
        return self.mod[:, c:c + 1]

    def gscol(self, l, jn, k, grp):
        c = l * 72 + (jn * 8 + k) * 3 + grp
        return self.gs[:, c:c + 1]

    def hgcol(self, l, jn, k, grp):
        c = l * 72 + (jn * 8 + k) * 3 + grp
        return self.hg[:, c:c + 1]

    def prologue(self):
        P = self.P
        bcp = self.b("cp")
        self.dma("sp", self.cp[:], self.cp_d, "cp", [], [bcp])
        cp2 = self.scr_f32(32 * 1024, CP2_N)
        bcp2 = self.sb("cp2")
        self.dma("sp", cp2, self.cp2_d, "cp2", [], [bcp2])

        def c2(name, a=0, n=None):
            o, nn = CP2_OFF[name]
            if n is None:
                n = nn - a
            return cp2[:, o + a: o + a + n]
        self.cpy(self.ones_b[:], c2("ones1024"), [bcp2], [self.b("ones_b")])
        self.cpy(self.ident_b[:], self.cpc("ident"), [bcp], [self.b("ident_b")])
        self.cpy(self.ones128_b[:], c2("ones128"), [bcp2], [self.b("ones_b")])
        self.cpy(self.ones256_b[:], c2("ones256"), [bcp2], [self.b("ones_b")])
        self.ts(self.cen_b[:], self.cpc("ident"), -1.0 / 128, None, ALU.add, None, [bcp], [self.b("ones_b")])
        self.dma("pool", self.poolw_b[:], self.poolw_d, "poolw", [], [self.b("poolw_b")])
        bsc = self.b("sc")
        self.act(self.sc[:], c2("cvec"), AF.Silu, [bcp2], [bsc])
        self.cpy(self.sc_b[:], self.sc[:], [bsc], [bsc])
        blg = self.b("lg")
        for l in range(NL):
            sl = slice(l * 8, (l + 1) * 8)
            self.act(self.nlg[:, sl], self.cpc(f"dec{l}"), AF.Exp, [bcp], [blg], scale=-1.0)
            self.act(self.nlg[:, sl], self.nlg[:, sl], AF.Ln, [blg, bcp], [blg], bias=self.cpc("one"))
            self.ts(self.lg[:, sl], self.nlg[:, sl], -1.0, None, ALU.mult, None, [blg], [blg])
            self.act(self.g128[:, sl], self.lg[:, sl], AF.Exp, [blg], [blg], scale=128.0)
        for l in range(NL):
            bank = 6 + l
            for cg in range(18):
                s = (l * 18 + cg) % 4
                wt = self.scr_bf(s * 8192, 4096)
                wb = self.sb("wm", s)
                self.dma("pool", wt, self.wmod_d[l * 18 + cg], f"wm{s}", [], [wb])
                for jj in range(4):
                    j = cg * 4 + jj
                    for k in range(KC):
                        self.mm(self.ps[bank][:, j * 3:(j + 1) * 3],
                                wt[:, k * 512 + jj * 128: k * 512 + (jj + 1) * 128],
                                self.sc_b[:, k * 3:(k + 1) * 3], k == 0, k == KC - 1,
                                [wb, bsc], [self.psb[bank]])
            bm = self.b("mod", l)
            self.tt(self.mod[:, l * 216:(l + 1) * 216], self.ps[bank][:, 0:216], c2(f"bmod{l}"),
                    ALU.add, [self.psb[bank], bcp2], [bm])
            for jn in range(3):
                j = 3 * jn + 1
                self.stt(self.gs[:, l * 72 + jn * 24: l * 72 + (jn + 1) * 24],
                         self.mod[:, l * 216 + j * 24: l * 216 + (j + 1) * 24], 1.0,
                         c2(f"normg{l}", jn * 24, 24), ALU.add, ALU.mult, [bm, bcp2], [self.b("gs", l)])
                jg = 3 * jn + 2
                self.ts(self.hg[:, l * 72 + jn * 24: l * 72 + (jn + 1) * 24],
                        self.mod[:, l * 216 + jg * 24: l * 216 + (jg + 1) * 24],
                        1.0 if jn == 1 else 0.5, None, ALU.mult, None, [bm], [self.b("hg", l)])
        self.new_phase()

    def load_seg(self, src, ntok):
        bident = self.b("cp")
        for tt_ in range(ntok // 128):
            s = tt_ % 8
            st = self.scr_f32(s * 4096, 1024)
            stb = self.sb("xin", s)
            self.dma("sp", st, src[tt_ * 128:(tt_ + 1) * 128, :], f"xin{s}", [], [stb])
            blk = tt_ // 4
            for half in range(2):
                bk = self.bank()
                for kk in range(4):
                    k = half * 4 + kk
                    self.tr(self.ps[bk][:, kk * 128:(kk + 1) * 128], st[:, k * 128:(k + 1) * 128],
                            self.cpc("ident"), [stb, bident], [self.psb[bk]])
                out = self.xT[:, half * 4 * SEQ:(half * 4 + 4) * SEQ].rearrange("p (k t) -> p k t", k=4)[
                    :, :, tt_ * 128:(tt_ + 1) * 128]
                in_ = self.ps[bk][:, :].rearrange("p (k t) -> p k t", k=4)
                eng = "dve" if half == 0 else "act"
                W = [self.xb(half * 4 + kk, blk) for kk in range(4)]
                if eng == "dve":
                    self.cpy(out, in_, [self.psb[bk]], W)
                else:
                    self.act(out, in_, AF.Identity, [self.psb[bk]], W)
        self.new_phase()

    def store_seg(self, dst, ntok):
        bident = self.b("cp")
        for tt_ in range(ntok // 128):
            s = tt_ % 8
            st = self.scr_f32(s * 4096, 1024)
            stb = self.sb("xout", s)
            blk = tt_ // 4
            for half in range(2):
                bk = self.bank()
                for kk in range(4):
                    k = half * 4 + kk
                    self.tr(self.ps[bk][:, kk * 128:(kk + 1) * 128], self.xv(k, tt_ * 128, 128),
                            self.cpc("ident"), [self.xb(k, blk), bident], [self.psb[bk]])
                if half == 0:
                    self.cpy(st[:, 0:512], self.ps[bk][:, :], [self.psb[bk]], [stb])
                else:
                    self.act(st[:, 512:1024], self.ps[bk][:, :], AF.Identity, [self.psb[bk]], [stb])
            self.dma("sp", dst[tt_ * 128:(tt_ + 1) * 128, :], st, f"xout{s}", [stb], [self.b("outd", s)])
        self.new_phase()

    def rms_rstd(self, blk, sq_eng="act", defer=False):
        bk = 6 + (blk % 2)
        for k in range(KC):
            sq, sqb = self.getsq()
            xs = self.xv(k, blk * 512, 512)
            if sq_eng == "act":
                self.act(sq[:], xs, AF.Square, [self.xb(k, blk)], [sqb])
            else:
                self.tt(sq[:], xs, xs, ALU.mult, [self.xb(k, blk)], [sqb])
            self.mm(self.ps[bk][:], self.ones_b[:], sq[:], k == 0, k == KC - 1,
                    [sqb, self.b("ones_b")], [self.psb[bk]])
        ri = self.rstd_i
        self.rstd_i = 1 - ri
        rs, rsb = self.rstd[ri], self.rstdb[ri]

        def fin():
            self.rsqrt(rs[:], self.ps[bk][:], [self.psb[bk]], rsb)
        if defer:
            return rs, rsb, fin
        fin()
        return rs, rsb

    def norm_mod(self, blk, l, jn, grp, hT, hoff, hbuf, rs=None):
        rs, rsb = rs if rs is not None else self.rms_rstd(blk)
        for k in range(KC):
            t, tb = self.gettmp()
            self.tt(t[:], self.xv(k, blk * 512, 512), rs[:], ALU.mult, [self.xb(k, blk), rsb], [tb])
            self.act(hT(k, hoff), t[:], AF.Identity, [tb, self.b("gs", l), self.b("mod", l)], [hbuf(k)],
                     scale=self.gscol(l, jn, k, grp), bias=self.modcol(l, 3 * jn, k, grp))

    def ffn(self, l, f, nblk, grp):
        jn = 0 if f == 0 else 2
        lf = l * 2 + f
        HT = 0
        GT = 16 * 1024
        W1 = 60 * 1024
        W3 = 68 * 1024
        W2 = 76 * 1024
        for g0 in range(0, nblk, 2):
            blks = list(range(g0, min(g0 + 2, nblk)))
            hTf = lambda k, off: self.scr_bf(HT + k * 2048 + off * 2, 512)
            rs0 = self.rms_rstd(blks[0])
            if len(blks) > 1:
                rs1, rs1b, fin1 = self.rms_rstd(blks[1], sq_eng="dve", defer=True)
            self.norm_mod(blks[0], l, jn, grp, hTf, 0, lambda k: self.sb("h", k, 0), rs=rs0)
            if len(blks) > 1:
                fin1()
                self.norm_mod(blks[1], l, jn, grp, hTf, 512, lambda k: self.sb("h", k, 1), rs=(rs1, rs1b))
            for cg in range(11):
                s = self.slot("w13")
                w1t = self.scr_bf(W1 + s * 4096, 2048)
                w3t = self.scr_bf(W3 + s * 4096, 2048)
                self.dma("pool", w1t, self.w1_d[lf * 11 + cg], f"w1{s}", [], [self.sb("w1", s)])
                self.dma("pool", w3t, self.w3_d[lf * 11 + cg], f"w3{s}", [], [self.sb("w3", s)])
                for bi, blk in enumerate(blks):
                    for m in range(2):
                        pa = self.bank()
                        pb = self.bank()
                        for k in range(KC):
                            self.mm(self.ps[pa][:], w1t[:, k * 256 + m * 128: k * 256 + (m + 1) * 128],
                                    self.scr_bf(HT + k * 2048 + bi * 1024, 512), k == 0, k == KC - 1,
                                    [self.sb("w1", s), self.sb("h", k, bi)], [self.psb[pa]])
                        for k in range(KC):
                            self.mm(self.ps[pb][:], w3t[:, k * 256 + m * 128: k * 256 + (m + 1) * 128],
                                    self.scr_bf(HT + k * 2048 + bi * 1024, 512), k == 0, k == KC - 1,
                                    [self.sb("w3", s), self.sb("h", k, bi)], [self.psb[pb]])
                        t, tb = self.gettmp()
                        self.act(t[:], self.ps[pa][:], AF.Silu, [self.psb[pa]], [tb])
                        j = cg * 2 + m
                        self.tt(self.scr_bf(GT + j * 2048 + bi * 1024, 512), t[:], self.ps[pb][:], ALU.mult,
                                [tb, self.psb[pb]], [self.sb("g", j, bi)])
            for dg in range(KC):
                s = self.slot("w2")
                w2t = self.scr_bf(W2 + s * 5632, 2816)
                self.dma("pool", w2t, self.w2_d[lf * 8 + dg], f"w2{s}", [], [self.sb("w2", s)])
                for bi, blk in enumerate(blks):
                    pc = self.bank()
                    for j in range(FC):
                        self.mm(self.ps[pc][:], w2t[:, j * 128:(j + 1) * 128],
                                self.scr_bf(GT + j * 2048 + bi * 1024, 512), j == 0, j == FC - 1,
                                [self.sb("w2", s), self.sb("g", j, bi)], [self.psb[pc]])
                    xs = self.xv(dg, blk * 512, 512)
                    self.stt(xs, self.ps[pc][:], self.hgcol(l, jn, dg, grp), xs, ALU.mult, ALU.add,
                             [self.psb[pc], self.b("hg", l), self.xb(dg, blk)], [self.xb(dg, blk)])
        self.new_phase()

    def slot(self, name):
        v = self.slot_ctr.get(name, 0)
        self.slot_ctr[name] = v + 1
        return v % 2

    def mixer(self, l, nblk, grp, seqs, rope, full, bidx):
        T = nblk * 512
        HT = 0
        PH = 32 * 1024
        WI = 77 * 1024
        hT = lambda k, off: self.scr_bf(HT + k * (2 * T) + off * 2, 512)
        pend = {}
        for blk in range(min(2, nblk)):
            pend[blk] = self.rms_rstd(blk, sq_eng="act" if blk % 2 == 0 else "dve", defer=True)
        for blk in range(nblk):
            rs_, rsb_, fin_ = pend.pop(blk)
            fin_()
            self.norm_mod(blk, l, 1, grp, hT, blk * 512, lambda k, blk=blk: self.sb("h", k, blk), rs=(rs_, rsb_))
            if blk + 2 < nblk:
                pend[blk + 2] = self.rms_rstd(blk + 2, sq_eng="act" if blk % 2 == 0 else "dve", defer=True)
        KEEP = ("h", "wi")

        def load_wi(slot, chunk):
            wt = self.scr_bf(WI + slot * 2048, 1024)
            self.dma("pool", wt, self.win_d[l * 22 + chunk], f"wi{slot}", [], [self.sb("wi", slot)])
            return wt

        def load_wo(chunk, slot=4):
            wt = self.scr_bf(WI + slot * 2048, 1024)
            self.dma("pool", wt, self.wout_d[l * 8 + chunk], f"wi{slot}", [], [self.sb("wi", slot)])
            return wt

        def proj_fm(wt, slot, blk, bk):
            for k in range(KC):
                self.mm(self.ps[bk][:], wt[:, k * 128:(k + 1) * 128], hT(k, blk * 512), k == 0, k == KC - 1,
                        [self.sb("wi", slot), self.sb("h", k, blk)], [self.psb[bk]])

        def wout_partial(terms, blk):
            for dg in range(KC):
                pc = self.bank()
                for ti, (wo, wslot, cat_ap, catb) in enumerate(terms):
                    self.mm(self.ps[pc][:], wo[:, dg * 128:(dg + 1) * 128], cat_ap, ti == 0, ti == len(terms) - 1,
                            [self.sb("wi", wslot), catb], [self.psb[pc]])
                xs = self.xv(dg, blk * 512, 512)
                self.stt(xs, self.ps[pc][:], self.hgcol(l, 1, dg, grp), xs, ALU.mult, ALU.add,
                         [self.psb[pc], self.b("hg", l), self.xb(dg, blk)], [self.xb(dg, blk)])

        bcp = self.b("cp")
        if full:
            for ch in range(2):
                self.new_phase(KEEP + ("pcat",))
                PADL = 8
                LP = T + 16 * len(seqs)
                pp = self.scr_f32(PH, LP)
                ta = self.scr_f32(PH + 4 * LP, LP)
                tb_ = self.scr_f32(PH + 8 * LP, LP)
                pooled = self.scr_bf(PH + 12 * LP, T)
                CATP = PH + 45 * 1024 - 4 * T
                assert PH + 12 * LP + 2 * T <= CATP
                pcat = [self.scr_bf(CATP + c_ * 2 * T, T) for c_ in range(2)]
                cat = pcat[ch]
                bpp, bta, btb = self.sb("pp"), self.sb("ta"), self.sb("tb")
                for si_, (c0_, ncn_, _) in enumerate(seqs):
                    b0_ = c0_ * 128 + 16 * si_
                    L_ = ncn_ * 128
                    self.memset(pp[:, b0_: b0_ + 8], 0.0, [bpp])
                    self.memset(pp[:, b0_ + 8 + L_: b0_ + 16 + L_], 0.0, [bpp])
                    self.memset(ta[:, b0_: b0_ + 1], 0.0, [bta])
                    self.memset(tb_[:, b0_: b0_ + 1], 0.0, [btb])
                    self.memset(tb_[:, b0_ + L_ + 15: b0_ + L_ + 16], 0.0, [btb])
                wt = load_wi(0, ch)
                wo_p = load_wo(ch, 4 + ch)
                if ch == 0:
                    wo_p0 = wo_p

                def ppos(tok):
                    si = 0
                    for i, (c0, ncn, _) in enumerate(seqs):
                        if tok >= c0 * 128:
                            si = i
                    return tok + 16 * si + PADL

                for blk in range(nblk):
                    bk = self.bank()
                    proj_fm(wt, 0, blk, bk)
                    for piece in range(2):
                        t0 = blk * 512 + piece * 256
                        p0 = ppos(t0)
                        self.act(pp[:, p0:p0 + 256], self.ps[bk][:, piece * 256:(piece + 1) * 256], AF.Identity,
                                 [self.psb[bk]], [bpp])
                for (c0, ncn, _) in seqs:
                    L = ncn * 128
                    base = ppos(c0 * 128) - PADL
                    LL = L + 16
                    wins = (POOL_WINDOWS[ch * 2], POOL_WINDOWS[ch * 2 + 1])
                    full = slice(0, 128)
                    hi_half = slice(64, 128)
                    self.tt(ta[full, base + 1: base + LL], pp[full, base: base + LL - 1], pp[full, base + 1: base + LL],
                            ALU.add, [bpp], [bta])
                    have = {2: (ta, bta)}
                    cur, curb, other, otherb = ta, bta, tb_, btb
                    sh = 1
                    ww = 2
                    while ww < max(wins):
                        part = full if min(wins) >= 2 * ww else hi_half
                        self.tt(other[part, base + sh: base + LL - sh], cur[part, base: base + LL - 2 * sh],
                                cur[part, base + 2 * sh: base + LL], ALU.add, [curb], [otherb])
                        cur, curb, other, otherb = other, otherb, cur, curb
                        sh *= 2
                        ww *= 2
                        have[ww] = (cur, curb)
                    for half in range(2):
                        gi = ch * 2 + half
                        w = POOL_WINDOWS[gi]
                        ps_ = slice(half * 64, half * 64 + 64)
                        fb, fbb = have[w]
                        e0 = gi * 16
                        self.tt(fb[ps_, base + PADL: base + PADL + 8], fb[ps_, base + PADL: base + PADL + 8],
                                self.cpc("edge", e0, 8)[ps_, :], ALU.mult, [fbb, bcp], [fbb])
                        self.tt(fb[ps_, base + PADL + L - 8: base + PADL + L], fb[ps_, base + PADL + L - 8: base + PADL + L],
                                self.cpc("edge", e0 + 8, 8)[ps_, :], ALU.mult, [fbb, bcp], [fbb])
                        self.stt(pooled[ps_, c0 * 128: c0 * 128 + L], fb[ps_, base + PADL: base + PADL + L], 1.0 / w,
                                 pp[ps_, base + PADL: base + PADL + L], ALU.mult, ALU.subtract,
                                 [fbb, bpp], [self.sb("pooled")])
                for blk in range(nblk):
                    bk = self.bank()
                    self.mm(self.ps[bk][:], self.poolw_b[:, l * 256 + ch * 128: l * 256 + (ch + 1) * 128],
                            pooled[:, blk * 512:(blk + 1) * 512], True, True,
                            [self.b("poolw_b"), self.sb("pooled")], [self.psb[bk]])
                    self.act(cat[:, blk * 512:(blk + 1) * 512], self.ps[bk][:], AF.Identity, [self.psb[bk], bcp],
                             [self.sb("pcat", ch, blk)], scale=self.cpc(f"pool_scale{l}", ch, 1))
                    if ch == 1:
                        wout_partial([(wo_p0, 4, pcat[0][:, blk * 512:(blk + 1) * 512], self.sb("pcat", 0, blk)),
                                      (wo_p, 5, pcat[1][:, blk * 512:(blk + 1) * 512], self.sb("pcat", 1, blk))], blk)
            self.new_phase(KEEP)
            nsq = len(seqs)
            ZP = T + 30 * nsq
            if (2 * ZP) % 4:
                ZP += 1
            ZB = PH
            DG = ZB + 4 * ZP
            AC = DG + 2 * 31 * 256
            AB = AC + 2 * 2 * 2048
            CC = AB + 4 * 1024
            assert CC + 4 * T <= WI, (CC + 4 * T, WI)
            zb_ = [self.scr_bf(ZB + c * 2 * ZP, ZP) for c in range(2)]
            cat = [self.scr_bf(CC + c * 2 * T, T) for c in range(2)]

            def zpos(tok):
                si = 0
                for i, (c0, ncn, _) in enumerate(seqs):
                    if tok >= c0 * 128:
                        si = i
                return tok + 30 * si + 15

            for c in range(2):
                self.memset(zb_[c], 0.0, [self.sb("z", c)])
                for kk in range(31):
                    dgm = self.scr_bf(DG + (c * 31 + kk) * 256, 128)
                    self.ts(dgm, self.ident_b[:], self.cpc(f"conv_dw{l}", c * 31 + kk, 1), None, ALU.mult, None,
                            [self.b("ident_b"), bcp], [self.sb("dg", c)])
                wa = load_wi(0, 18 + c)
                wg = load_wi(1, 20 + c)
                for blk in range(nblk):
                    ba = self.bank()
                    bg = self.bank()
                    proj_fm(wa, 0, blk, ba)
                    proj_fm(wg, 1, blk, bg)
                    t, tb = self.gettmp()
                    self.act(t[:], self.ps[bg][:], AF.Sigmoid, [self.psb[bg]], [tb])
                    for piece in range(2):
                        p0 = zpos(blk * 512 + piece * 256)
                        self.tt(zb_[c][:, p0:p0 + 256], self.ps[ba][:, piece * 256:(piece + 1) * 256],
                                t[:, piece * 256:(piece + 1) * 256], ALU.mult, [self.psb[ba], tb], [self.sb("z", c)])
            for blk in range(nblk):
                par = blk % 2
                accs = []
                for c in range(2):
                    bk = self.bank()
                    if nsq == 1:
                        pieces = [(0, 512, zpos(blk * 512) - 15)]
                    else:
                        pieces = [(0, 256, zpos(blk * 512) - 15), (256, 256, zpos(blk * 512 + 256) - 15)]
                    for (co, n, zs) in pieces:
                        for kk in range(31):
                            dgm = self.scr_bf(DG + (c * 31 + kk) * 256, 128)
                            self.mm(self.ps[bk][:, co:co + n], dgm, zb_[c][:, zs + kk: zs + kk + n], kk == 0, kk == 30,
                                    [self.sb("dg", c), self.sb("z", c)], [self.psb[bk]])
                    a32 = self.scr_f32(AC + (par * 2 + c) * 2048, 512)
                    a32b = self.sb("a32", par, c)
                    ab = self.scr_bf(AB + (c * 2) * 1024, 512)
                    sq = self.scr_bf(AB + (c * 2 + 1) * 1024, 512)
                    abb = self.sb("ab", c)
                    cb = self.cpc(f"conv_b{l}", c, 1)
                    self.act(a32, self.ps[bk][:], AF.Identity, [self.psb[bk], bcp], [a32b], bias=cb)
                    self.act(ab, self.ps[bk][:], AF.Identity, [self.psb[bk], bcp], [abb], bias=cb)
                    self.act(sq, self.ps[bk][:], AF.Square, [self.psb[bk], bcp], [abb], bias=cb)
                    accs.append((a32, a32b, ab, sq, abb))
                bm_, bv_ = 6, 7
                for c in range(2):
                    self.mm(self.ps[bm_][:], self.ones256_b[:], accs[c][2], c == 0, c == 1,
                            [self.b("ones_b"), accs[c][4]], [self.psb[bm_]])
                for c in range(2):
                    self.mm(self.ps[bv_][:], self.ones256_b[:], accs[c][3], c == 0, c == 1,
                            [self.b("ones_b"), accs[c][4]], [self.psb[bv_]])
                mu, mub = self.gettmp()
                self.act(mu[:], self.ps[bm_][:], AF.Identity, [self.psb[bm_]], [mub])
                var, varb = self.gettmp()
                self.act(var[:], self.ps[bm_][:], AF.Square, [self.psb[bm_]], [varb])
                self.tt(var[:], self.ps[bv_][:], var[:], ALU.subtract, [self.psb[bv_], varb], [varb])
                self.rsqrt(var[:], var[:], [varb], varb, clamp=True)
                for c in range(2):
                    d, db = self.gettmp()
                    self.tt(d[:], accs[c][0], mu[:], ALU.subtract, [accs[c][1], mub], [db])
                    self.tt(d[:], d[:], var[:], ALU.mult, [db, varb], [db])
                    self.act(cat[c][:, blk * 512:(blk + 1) * 512], d[:], AF.Silu, [db, bcp], [self.sb("ccat", c, blk)],
                             scale=self.cpc(f"ln_g{l}", c, 1), bias=self.cpc(f"ln_b{l}", c, 1))
            wo_c = [load_wo(6 + c, 4 + c) for c in range(2)]
            for blk in range(nblk):
                wout_partial([(wo_c[c], 4 + c, cat[c][:, blk * 512:(blk + 1) * 512], self.sb("ccat", c, blk))
                              for c in range(2)], blk)

        nch = T // 128
        QF, QB, KF, KB = PH, PH + 2 * T, PH + 4 * T, PH + 6 * T
        KFT, KBT = PH + 8 * T, PH + 10 * T
        SF, SB = PH + 12 * T, PH + 14 * T
        VT = PH + 16 * T
        RF = PH + 18 * T
        GT = RF + 1024
        ST = GT + 2048
        CAT = ST + 2048
        OB = CAT + 2048
        assert OB + 2048 <= WI, (OB, WI)
        for h in range(4):
            self.new_phase(KEEP)
            qf = self.scr_bf(QF, T)
            qb = self.scr_bf(QB, T)
            kf = self.scr_bf(KF, T)
            kb = self.scr_bf(KB, T)
            kfT = self.scr_bf(KFT, T)
            kbT = self.scr_bf(KBT, T)
            sf = self.scr_bf(SF, T)
            sbk = self.scr_bf(SB, T)
            vT = self.scr_bf(VT, T)
            Rst = self.scr_f32(RF, 256)
            gt = self.scr_f32(GT, 512)
            bgt = self.sb("gt")
            lgf = self.lg[:, l * 8 + h: l * 8 + h + 1]
            lgb = self.lg[:, l * 8 + 4 + h: l * 8 + 4 + h + 1]
            nlgf = self.nlg[:, l * 8 + h: l * 8 + h + 1]
            nlgb = self.nlg[:, l * 8 + 4 + h: l * 8 + 4 + h + 1]
            blg = self.b("lg")
            self.act(gt[:, 0:128], self.cpc("pos1"), AF.Exp, [bcp, blg], [bgt], scale=lgf)
            self.act(gt[:, 128:256], self.cpc("posr"), AF.Exp, [bcp, blg], [bgt], scale=lgb)
            self.act(gt[:, 256:257], self.cpc("pos1c"), AF.Exp, [bcp, blg], [bgt], scale=nlgf)
            self.act(gt[:, 257:258], self.cpc("posrc"), AF.Exp, [bcp, blg], [bgt], scale=nlgb)
            ksc = 128.0 ** -0.5
            self.ts(gt[:, 256:258], gt[:, 256:258], ksc, None, ALU.mult, None, [bgt], [bgt])
            wkcol = (gt[:, 256:257], gt[:, 257:258])
            wq = load_wi(0, 2 + h)
            wk = load_wi(1, 6 + h)
            wv = load_wi(2, 10 + h)
            for blk in range(nblk):
                if rope:
                    rs_ = self.slot("rope")
                    rt = self.ropes[rs_]
                    rtb = self.b("rope", rs_)
                    self.dma("sp", rt[:, :].rearrange("p (a t) -> p a t", a=2),
                             self.rope_d[:, :, blk * 512:(blk + 1) * 512], f"rope{rs_}", [], [rtb])
                items = []
                for which, wt, slot, outs in (("q", wq, 0, ((qf, 0), (qb, 128))), ("k", wk, 1, ())):
                    if not full and which == "q":
                        continue
                    bk = self.bank()
                    proj_fm(wt, slot, blk, bk)
                    if rope:
                        q32, q32b = self.gettmp()
                        self.act(q32[:], self.ps[bk][:], AF.Identity, [self.psb[bk]], [q32b])
                        items.append((which, outs, q32, q32b))
                    else:
                        items.append((which, outs, self.ps[bk], self.psb[bk]))
                for (which, outs, q32, q32b) in items:
                    if rope:
                        bp = self.bank()
                        self.mm(self.ps[bp][:], self.cpc("perm"), q32[:], True, True, [bcp, q32b], [self.psb[bp]])
                        t2, t2b = self.gettmp()
                        self.tt(t2[:], self.ps[bp][:], rt[:, 512:1024], ALU.mult, [self.psb[bp], rtb], [t2b])
                        self.tt(q32[:], q32[:], rt[:, 0:512], ALU.mult, [q32b, rtb], [q32b])
                        if which == "k":
                            self.tt(kf[:, blk * 512:(blk + 1) * 512], q32[:], t2[:], ALU.add, [q32b, t2b],
                                    [self.sb("k", 256, blk)])
                            continue
                        self.tt(q32[:], q32[:], t2[:], ALU.add, [q32b, t2b], [q32b])
                    elif which == "k":
                        self.cpy(kf[:, blk * 512:(blk + 1) * 512], q32[:], [q32b], [self.sb("k", 256, blk)])
                        continue
                    src, srcb = q32[:], q32b
                    for (dst, go) in outs:
                        o3 = dst[:, blk * 512:(blk + 1) * 512].rearrange("p (c i) -> p c i", c=4)
                        i3 = src.rearrange("p (c i) -> p c i", c=4)
                        g3 = gt[:, go:go + 128].unsqueeze(1).to_broadcast([128, 4, 128])
                        self.tt(o3, i3, g3, ALU.mult, [srcb, bgt], [self.sb(which, go, blk)])
            for cgp in range(nch // 4):
                bk = self.bank()
                for cc in range(4):
                    n = cgp * 4 + cc
                    blk = n // 4
                    for k in range(KC):
                        self.mm(self.ps[bk][:, cc * 128:(cc + 1) * 128], self.scr_bf(HT + k * (2 * T) + n * 256, 128), wv[:, k * 128:(k + 1) * 128],
                                k == 0, k == KC - 1, [self.sb("h", k, blk), self.sb("wi", 2)], [self.psb[bk]])
                self.act(vT[:, cgp * 512:(cgp + 1) * 512], self.ps[bk][:], AF.Identity, [self.psb[bk]], [self.sb("vT", cgp)])
            for cgp in range(nch // 4):
                bk = self.bank()
                pst = self.ps[bk][:, :].bitcast(BF16)
                for cc in range(4):
                    n = cgp * 4 + cc
                    self.tr(pst[:, cc * 128:(cc + 1) * 128], kf[:, n * 128:(n + 1) * 128], self.ident_b[:],
                            [self.sb("k", 256, n // 4), self.b("ident_b")], [self.psb[bk]])
                self.act(kfT[:, cgp * 512:(cgp + 1) * 512], pst[:, 0:512], AF.Identity, [self.psb[bk], bgt],
                         [self.sb("kfT", cgp)], scale=wkcol[0])
                self.act(kbT[:, cgp * 512:(cgp + 1) * 512], pst[:, 0:512], AF.Identity, [self.psb[bk], bgt],
                         [self.sb("kbT", cgp)], scale=wkcol[1])
            if full:
                wg = load_wi(3, 14 + h)
                for blk in range(nblk):
                    bg = self.bank()
                    proj_fm(wg, 3, blk, bg)
                    self.act(self.sgall[:, blk * 512:(blk + 1) * 512], self.ps[bg][:], AF.Silu, [self.psb[bg]],
                             [self.b("sg", blk)])
            for (c0, ncn, bi_) in seqs:
                for idx in range(ncn):
                    for d_ in range(2):
                        kT, nm = (kfT, "kfT") if d_ == 0 else (kbT, "kbT")
                        sdst = sf if d_ == 0 else sbk
                        g128c = self.g128[:, l * 8 + d_ * 4 + h: l * 8 + d_ * 4 + h + 1]
                        R = Rst[:, d_ * 128:(d_ + 1) * 128]
                        Rb = self.sb("R", d_)
                        st_off = (((l * 2 + bi_) * 2 + d_) * 4 + h) * 128
                        S0 = self.states[:, st_off: st_off + 128]
                        S0b = self.b("state", l, bi_, d_, h)
                        n = c0 + idx if d_ == 0 else c0 + ncn - 1 - idx
                        if full:
                            if idx == 0:
                                if rope:
                                    self.cpy(sdst[:, n * 128:(n + 1) * 128], S0, [S0b], [self.sb("S", d_, n)])
                                else:
                                    self.memset(sdst[:, n * 128:(n + 1) * 128], 0.0, [self.sb("S", d_, n)])
                            else:
                                self.act(sdst[:, n * 128:(n + 1) * 128], R, AF.Identity, [Rb, blg], [self.sb("S", d_, n)],
                                         scale=g128c)
                        last = idx == ncn - 1
                        if last and rope:
                            continue
                        bk = self.bank()
                        self.mm(self.ps[bk][:, 0:128], kT[:, n * 128:(n + 1) * 128], vT[:, n * 128:(n + 1) * 128], True, True,
                                [self.sb(nm, n // 4), self.sb("vT", n // 4)], [self.psb[bk]])
                        if idx == 0:
                            if rope:
                                self.tt(R, self.ps[bk][:, 0:128], S0, ALU.add, [self.psb[bk], S0b], [Rb])
                            else:
                                self.cpy(R, self.ps[bk][:, 0:128], [self.psb[bk]], [Rb])
                        else:
                            self.stt(R, R, g128c, self.ps[bk][:, 0:128], ALU.mult, ALU.add, [Rb, blg, self.psb[bk]], [Rb])
                        if last and not rope:
                            self.act(S0, R, AF.Identity, [Rb, blg], [S0b], scale=g128c)
            if not full:
                continue
            odd = h % 2 == 1
            if odd:
                wo_prev = load_wo(2 + h - 1, 4)
                wo_cur = load_wo(2 + h, 5)

            def stageA1(blk):
                sts = []
                for d_ in range(2):
                    kk_, kgo = kf, 256
                    qq_, qgo = (qf, 0) if d_ == 0 else (qb, 128)
                    bk = self.bank()
                    for cc in range(4):
                        n = blk * 4 + cc
                        self.mm(self.ps[bk][:, cc * 128:(cc + 1) * 128], kk_[:, n * 128:(n + 1) * 128],
                                qq_[:, n * 128:(n + 1) * 128], True, True,
                                [self.sb("k", kgo, blk), self.sb("q", qgo, blk)], [self.psb[bk]])
                    sT = self.scr_bf(ST + d_ * 1024, 512)
                    mk = self.cpc("maskf" if d_ == 0 else "maskb").unsqueeze(1).to_broadcast([128, 4, 128])
                    self.stt(sT.rearrange("p (c i) -> p c i", c=4), self.ps[bk][:, :].rearrange("p (c i) -> p c i", c=4),
                             wkcol[d_], mk, ALU.mult, ALU.mult, [self.psb[bk], bcp, bgt], [self.sb("sT", d_)])
                    sts.append(sT)
                bo = 4 + blk % 2
                for cc in range(4):
                    n = blk * 4 + cc
                    oc = self.ps[bo][:, cc * 128:(cc + 1) * 128]
                    vch = vT[:, n * 128:(n + 1) * 128]
                    self.mm(oc, vch, sts[0][:, cc * 128:(cc + 1) * 128], True, False,
                            [self.sb("vT", n // 4), self.sb("sT", 0)], [self.psb[bo]])
                    self.mm(oc, vch, sts[1][:, cc * 128:(cc + 1) * 128], False, False,
                            [self.sb("vT", n // 4), self.sb("sT", 1)], [self.psb[bo]])
                    self.mm(oc, sf[:, n * 128:(n + 1) * 128], qf[:, n * 128:(n + 1) * 128], False, False,
                            [self.sb("S", 0, n), self.sb("q", 0, blk)], [self.psb[bo]])
                    self.mm(oc, sbk[:, n * 128:(n + 1) * 128], qb[:, n * 128:(n + 1) * 128], False, True,
                            [self.sb("S", 1, n), self.sb("q", 128, blk)], [self.psb[bo]])
                return bo

            def stageA2(blk, bo):
                par = blk % 2
                rb = self.b("rope", par)
                o32 = self.ropes[par][:, 0:512]
                mu = self.ropes[par][:, 512:1024]
                var, varb = self.rstd[par], self.rstdb[par]
                ob = self.scr_bf(OB, 512)
                osq = self.scr_bf(OB + 1024, 512)
                obb = self.sb("ob")
                pc_ = 6 + par
                self.act(ob, self.ps[bo][:], AF.Identity, [self.psb[bo]], [obb])
                self.mm(self.ps[pc_][:], self.cen_b[:], ob, True, True, [self.b("ones_b"), obb], [self.psb[pc_]])
                self.act(osq, self.ps[pc_][:], AF.Square, [self.psb[pc_]], [obb])
                bv = self.bank()
                self.mm(self.ps[bv][:], self.ones128_b[:], osq, True, True, [self.b("ones_b"), obb], [self.psb[bv]])
                self.rsqrt(var[:], self.ps[bv][:], [self.psb[bv]], varb)

            def stageB(blk):
                par = blk % 2
                rb = self.b("rope", par)
                o32 = self.ropes[par][:, 0:512]
                mu = self.ropes[par][:, 512:1024]
                var, varb = self.rstd[par], self.rstdb[par]
                pc_ = 6 + par
                self.tt(o32, self.ps[pc_][:], var[:], ALU.mult, [self.psb[pc_], varb], [rb])
                if odd:
                    cat, catb = self.scr_bf(CAT + par * 1024, 512), self.sb("hcat", par)
                else:
                    cat, catb = self.hprev[:, blk * 512:(blk + 1) * 512], self.b("hprev", blk)
                self.stt(cat, o32, self.cpc(f"gn_g{l}", h, 1), self.sgall[:, blk * 512:(blk + 1) * 512], ALU.mult, ALU.mult,
                         [rb, bcp, self.b("sg", blk)], [catb])

            def stageC(blk):
                if not odd:
                    return
                par = blk % 2
                wout_partial([(wo_prev, 4, self.hprev[:, blk * 512:(blk + 1) * 512], self.b("hprev", blk)),
                              (wo_cur, 5, self.scr_bf(CAT + par * 1024, 512), self.sb("hcat", par))], blk)

            self.bank_set = [0, 1, 2, 3]
            bo_ = stageA1(0)
            stageA2(0, bo_)
            for blk in range(nblk):
                if blk + 1 < nblk:
                    bo_ = stageA1(blk + 1)
                stageB(blk)
                stageC(blk)
                if blk + 1 < nblk:
                    stageA2(blk + 1, bo_)
            self.bank_set = [0, 1, 2, 3, 4, 5]
        self.new_phase()

    def final_norm(self, nblk):
        for blk in range(nblk):
            rs, rsb = self.rms_rstd(blk, sq_eng="act" if blk % 2 == 0 else "dve")
            for k in range(KC):
                xs = self.xv(k, blk * 512, 512)
                self.stt(xs, xs, self.cpc("final_g", k, 1), rs[:], ALU.mult, ALU.mult,
                         [self.xb(k, blk), self.b("cp"), rsb], [self.xb(k, blk)])

    def build(self):
        stop = self.dbg_stop
        self.prologue()
        ctx_seqs = [(0, 2, 0), (2, 2, 1)]
        self.load_seg(self.ctx_d, 512)
        stage = 0
        done = False
        for l in range(NL):
            last = l == NL - 1
            self.ffn(l, 0, 1, 0)
            stage += 1
            if stop == ("c", stage):
                done = True
                break
            self.mixer(l, 1, 0, ctx_seqs, rope=False, full=not last, bidx=None)
            stage += 1
            if stop == ("c", stage):
                done = True
                break
            if not last:
                self.ffn(l, 1, 1, 0)
                stage += 1
                if stop == ("c", stage):
                    done = True
                    break
        if stop is not None and stop[0] == "c":
            self.store_seg(self.dbgy_d, 512)
        else:
            for bi in range(2):
                self.load_seg(self.x_d[bi], SEQ)
                stage = 0
                done = False
                for l in range(NL):
                    self.ffn(l, 0, 4, 1 + bi)
                    stage += 1
                    if stop == ("x", stage):
                        done = True
                        break
                    self.mixer(l, 4, 1 + bi, [(0, 16, bi)], rope=True, full=True, bidx=bi)
                    stage += 1
                    if stop == ("x", stage):
                        done = True
                        break
                    self.ffn(l, 1, 4, 1 + bi)
                    stage += 1
                    if stop == ("x", stage):
                        done = True
                        break
                if not done:
                    self.final_norm(4)
                self.store_seg(self.out_d[bi], SEQ)
        outs = [self.b("outd", i) for i in range(8)]
        self.P.op("sp", lambda h: h.nop(), reads=outs)
        self.P.emit()
        return self.nc


_CACHE = {}


def kernel(x, c, ctx, c_ctx, w_mod, b_mod, norm_g, ffn_w1, ffn_w3, ffn_w2, w_in, w_out,
           pool_w, pool_scale, ret_decay_fwd, ret_decay_bwd, ret_gn_g, conv_dw, conv_b,
           conv_ln_g, conv_ln_b, final_g, _dbg_stop=None, _cores=None):
    inp = dict(x=x, c=c, ctx=ctx, c_ctx=c_ctx, w_mod=w_mod, b_mod=b_mod, norm_g=norm_g, ffn_w1=ffn_w1,
               ffn_w3=ffn_w3, ffn_w2=ffn_w2, w_in=w_in, w_out=w_out, pool_w=pool_w, pool_scale=pool_scale,
               ret_decay_fwd=ret_decay_fwd, ret_decay_bwd=ret_decay_bwd, ret_gn_g=ret_gn_g, conv_dw=conv_dw,
               conv_b=conv_b, conv_ln_g=conv_ln_g, conv_ln_b=conv_ln_b, final_g=final_g)
    inp = {k: np.asarray(v) for k, v in inp.items()}
    cores = list(range(NCORES)) if _cores is None else _cores
    W = relayout_weights(inp)
    rope = build_rope()
    poolw = build_poolw(inp)
    k = K(dbg_stop=_dbg_stop)
    nc = k.build()
    in_maps = []
    xs = np.asarray(inp["x"], np.float32)
    cs = np.asarray(inp["ctx"], np.float32)
    for core in cores:
        m = dict(W)
        m["x"] = np.ascontiguousarray(xs[2 * core: 2 * core + 2])
        m["ctx"] = np.ascontiguousarray(cs[2 * core: 2 * core + 2].reshape(2 * CTX, D))
        m["cp"], m["cp2"] = build_cp(inp, core)
        m["rope"] = rope
        m["poolw"] = poolw
        in_maps.append(m)
    res = run_bass_kernel_spmd(nc, in_maps, core_ids=list(range(len(cores))))
    if _dbg_stop is not None:
        return res.results
    out = np.concatenate([np.asarray(r["out"], np.float32) for r in res.results], axis=0)
    return out
```

```python
import numpy as np
import concourse.bass as bass
import concourse.mybir as mybir
from concourse.bass_utils import run_bass_kernel_spmd

F32 = mybir.dt.float32
BF16 = mybir.dt.bfloat16
ALU = mybir.AluOpType
AF = mybir.ActivationFunctionType

D = 1024
KC = 8
DFF = 2816
FC = 22
SEQ = 2048
CTX = 256
NL = 2
EPS = 1e-6
NCORES = 8


class Buf:
    __slots__ = ("name", "last_w", "readers")

    def __init__(self, name, inherit=None):
        self.name = name
        self.last_w = None
        self.readers = dict(inherit) if inherit else {}


class Tok:
    __slots__ = ("key", "ord", "clock", "op")

    def __init__(self, key, ord_, clock, op):
        self.key = key
        self.ord = ord_
        self.clock = clock
        self.op = op


class Op:
    __slots__ = ("eng", "fn", "waits", "tok", "signal", "semval", "dma_sem")


class Prog:
    ENGS = ("pe", "act", "dve", "pool", "sp")

    def __init__(self, nc):
        self.nc = nc
        self.h = {"pe": nc.tensor, "act": nc.scalar, "dve": nc.vector,
                  "pool": nc.gpsimd, "sp": nc.sync}
        self.ops = []
        self.nops = {e: 0 for e in self.ENGS}
        self.seen = {e: {} for e in self.ENGS}
        self.dma_count = {}
        self.sems = {}

    def op(self, eng, fn, reads=(), writes=(), dma=None):
        seen = self.seen[eng]
        deps = {}

        def add(t, raw):
            if t is None:
                return
            if t.key == ("e", eng) and eng in ("pe", "sp"):
                return
            k = t.key
            if k not in deps or deps[k].ord < t.ord:
                deps[k] = t

        for b in reads:
            add(b.last_w, True)
        for b in writes:
            add(b.last_w, True)
            for t in b.readers.values():
                add(t, False)
        o = Op()
        o.eng = eng
        o.fn = fn
        o.waits = []
        o.signal = False
        o.semval = None
        o.dma_sem = dma
        for k, t in deps.items():
            if seen.get(k, 0) >= t.ord:
                continue
            o.waits.append(t)
            if t.op is not None:
                t.op.signal = True
            seen[k] = t.ord
            for kk, vv in t.clock.items():
                if seen.get(kk, 0) < vv:
                    seen[kk] = vv
        if dma is None:
            self.nops[eng] += 1
            tok = Tok(("e", eng), self.nops[eng], dict(seen), o)
        else:
            self.dma_count[dma] = self.dma_count.get(dma, 0) + 16
            tok = Tok(("d", dma), self.dma_count[dma], dict(seen), None)
        o.tok = tok
        self.ops.append(o)
        for b in writes:
            b.last_w = tok
            b.readers = {}
        for b in reads:
            if b in writes:
                continue
            b.readers[tok.key] = tok
        return tok

    def _sem(self, name):
        if name not in self.sems:
            self.sems[name] = self.nc.alloc_semaphore(name)
        return self.sems[name]

    def emit(self):
        cnt = {e: 0 for e in self.ENGS}
        for o in self.ops:
            h = self.h[o.eng]
            for t in o.waits:
                if t.key[0] == "e":
                    h.wait_ge(self._sem("s_" + t.key[1]), t.op.semval)
                else:
                    h.wait_ge(self._sem("d_" + t.key[1]), t.ord)
            ins = o.fn(h)
            if o.dma_sem is not None:
                ins.then_inc(self._sem("d_" + o.dma_sem), 16)
            elif o.signal:
                cnt[o.eng] += 1
                o.semval = cnt[o.eng]
                ins.then_inc(self._sem("s_" + o.eng), 1)


def _cp_layout():
    off = {}
    c = 0

    def add(name, n):
        nonlocal c
        off[name] = (c, n)
        c += n

    add("ident", 128)
    add("perm", 128)
    add("maskf", 128)
    add("maskb", 128)
    add("pos1", 128)
    add("posr", 128)
    add("pos1c", 1)
    add("posrc", 1)
    add("eps", 1)
    add("one", 1)
    add("edge", 4 * 16)
    add("final_g", 8)
    for l in range(NL):
        add(f"pool_scale{l}", 2)
        add(f"gn_g{l}", 4)
        add(f"conv_b{l}", 2)
        add(f"ln_g{l}", 2)
        add(f"ln_b{l}", 2)
        add(f"conv_dw{l}", 62)
        add(f"dec{l}", 8)
    return off, c


def _cp2_layout():
    off = {}
    c = 0
    for name, n in (("ones1024", 128), ("ones128", 128), ("ones256", 128), ("cvec", 24),
                    ("normg0", 72), ("bmod0", 216), ("normg1", 72), ("bmod1", 216)):
        off[name] = (c, n)
        c += n
    return off, c


CP_OFF, CP_N = _cp_layout()
CP2_OFF, CP2_N = _cp2_layout()
POOL_WINDOWS = (2, 4, 8, 16)


def _fm(v):
    return np.ascontiguousarray(np.asarray(v, np.float32).reshape(-1, 128).T)


def build_cp(inp, core):
    cp = np.zeros((128, CP_N), np.float32)
    cp2 = np.zeros((128, CP2_N), np.float32)

    def put(name, arr):
        if name in CP2_OFF:
            o, n = CP2_OFF[name]
            cp2[:, o:o + n] = np.asarray(arr, np.float32).reshape(128, n)
            return
        o, n = CP_OFF[name]
        arr = np.asarray(arr, np.float32).reshape(128, n)
        cp[:, o:o + n] = arr

    put("ident", np.eye(128, dtype=np.float32))
    perm = np.zeros((128, 128), np.float32)
    for m in range(128):
        partner = m + 32 if (m % 64) < 32 else m - 32
        perm[partner, m] = 1.0
    put("perm", perm)
    put("ones1024", np.full((128, 128), 1.0 / 1024, np.float32))
    put("ones128", np.full((128, 128), 1.0 / 128, np.float32))
    put("ones256", np.full((128, 128), 1.0 / 256, np.float32))
    jj = np.arange(128)[:, None]
    ii = np.arange(128)[None, :]
    put("maskf", (ii >= jj).astype(np.float32))
    put("maskb", (jj > ii).astype(np.float32))
    put("pos1", np.tile(np.arange(1, 129, dtype=np.float32)[None, :], (128, 1)))
    put("posr", np.tile((128 - np.arange(128, dtype=np.float32))[None, :], (128, 1)))
    put("pos1c", np.arange(1, 129, dtype=np.float32)[:, None])
    put("posrc", (128 - np.arange(128, dtype=np.float32))[:, None])
    put("eps", np.full((128, 1), EPS, np.float32))
    put("one", np.ones((128, 1), np.float32))
    edge = np.zeros((4, 16), np.float32)
    for gi, w in enumerate(POOL_WINDOWS):
        for t in range(8):
            cl = min(t + w // 2, w) if True else w
            cl = (t + w // 2) - max(t - w // 2, 0)
            edge[gi, t] = w / cl
            d = 8 - t
            cr = min(d, w // 2) + w // 2
            edge[gi, 8 + t] = w / cr
    put("edge", np.tile(edge.reshape(1, 64), (128, 1)))
    b0 = 2 * core
    cv = np.stack([np.asarray(inp["c_ctx"], np.float32),
                   np.asarray(inp["c"][b0], np.float32),
                   np.asarray(inp["c"][b0 + 1], np.float32)], axis=0)
    put("cvec", cv.reshape(3, 8, 128).transpose(2, 1, 0).reshape(128, 24))
    put("final_g", _fm(inp["final_g"]))
    for l in range(NL):
        ng = np.asarray(inp["norm_g"][l], np.float32).reshape(3, 8, 128).transpose(2, 0, 1)
        put(f"normg{l}", np.repeat(ng[:, :, :, None], 3, axis=3).reshape(128, 72))
        bm = _fm(inp["b_mod"][l])
        put(f"bmod{l}", np.repeat(bm[:, :, None], 3, axis=2).reshape(128, 216))
        put(f"pool_scale{l}", _fm(inp["pool_scale"][l]))
        put(f"gn_g{l}", _fm(inp["ret_gn_g"][l]))
        put(f"conv_b{l}", _fm(inp["conv_b"][l]))
        put(f"ln_g{l}", _fm(inp["conv_ln_g"][l]))
        put(f"ln_b{l}", _fm(inp["conv_ln_b"][l]))
        dw = np.asarray(inp["conv_dw"][l], np.float32)
        put(f"conv_dw{l}", dw.reshape(31, 2, 128).transpose(2, 1, 0).reshape(128, 62))
        dec = np.concatenate([np.asarray(inp["ret_decay_fwd"][l], np.float32),
                              np.asarray(inp["ret_decay_bwd"][l], np.float32)])
        put(f"dec{l}", np.tile(dec[None, :], (128, 1)))
    return cp, cp2


def build_poolw(inp):
    out = np.zeros((128, NL, 2, 128), np.float32)
    for l in range(NL):
        pw = np.asarray(inp["pool_w"][l], np.float32)
        for ch in range(2):
            out[0:64, l, ch, 0:64] = pw[2 * ch]
            out[64:128, l, ch, 64:128] = pw[2 * ch + 1]
    return out.reshape(128, NL * 256)


def build_rope():
    n_freq = 32
    inv = (10000.0 ** (-np.arange(n_freq, dtype=np.float32) / n_freq)).astype(np.float32)
    t = np.arange(SEQ)
    row = (t // 64).astype(np.float32)
    col = (t % 64).astype(np.float32)
    tab = np.zeros((128, 2, SEQ), np.float32)
    for f in range(128):
        pos = row if f < 64 else col
        ang = (pos * inv[f % 32]).astype(np.float32)
        tab[f, 0] = np.cos(ang)
        s = np.sin(ang)
        tab[f, 1] = -s if (f % 64) < 32 else s
    return tab


def relayout_weights(inp):
    w = {}
    w1 = np.asarray(inp["ffn_w1"], np.float32).reshape(4, 8, 128, 11, 256)
    w["w1"] = np.ascontiguousarray(w1.transpose(0, 3, 2, 1, 4)).reshape(44, 128, 2048)
    w3 = np.asarray(inp["ffn_w3"], np.float32).reshape(4, 8, 128, 11, 256)
    w["w3"] = np.ascontiguousarray(w3.transpose(0, 3, 2, 1, 4)).reshape(44, 128, 2048)
    w2 = np.asarray(inp["ffn_w2"], np.float32).reshape(4, 22, 128, 8, 128)
    w["w2"] = np.ascontiguousarray(w2.transpose(0, 3, 2, 1, 4)).reshape(32, 128, 2816)
    wi = np.asarray(inp["w_in"], np.float32).reshape(2, 8, 128, 22, 128)
    w["win"] = np.ascontiguousarray(wi.transpose(0, 3, 2, 1, 4)).reshape(44, 128, 1024)
    w["wout"] = np.ascontiguousarray(np.asarray(inp["w_out"], np.float32).reshape(16, 128, 1024))
    wm = np.asarray(inp["w_mod"], np.float32).reshape(2, 8, 128, 18, 512)
    w["wmod"] = np.ascontiguousarray(wm.transpose(0, 3, 2, 1, 4)).reshape(36, 128, 4096)
    return w


class K:
    def __init__(self, dbg_stop=None):
        self.dbg_stop = dbg_stop
        nc = bass.Bass("TRN2", target_bir_lowering=False)
        self.nc = nc
        self.P = Prog(nc)
        dt = nc.dram_tensor
        self.x_d = dt("x", [2, SEQ, D], F32, kind="ExternalInput").ap()
        self.ctx_d = dt("ctx", [2 * CTX, D], F32, kind="ExternalInput").ap()
        self.cp_d = dt("cp", [128, CP_N], F32, kind="ExternalInput").ap()
        self.cp2_d = dt("cp2", [128, CP2_N], F32, kind="ExternalInput").ap()
        self.rope_d = dt("rope", [128, 2, SEQ], F32, kind="ExternalInput").ap()
        self.poolw_d = dt("poolw", [128, NL * 256], F32, kind="ExternalInput").ap()
        self.w1_d = dt("w1", [44, 128, 2048], F32, kind="ExternalInput").ap()
        self.w3_d = dt("w3", [44, 128, 2048], F32, kind="ExternalInput").ap()
        self.w2_d = dt("w2", [32, 128, 2816], F32, kind="ExternalInput").ap()
        self.win_d = dt("win", [44, 128, 1024], F32, kind="ExternalInput").ap()
        self.wout_d = dt("wout", [16, 128, 1024], F32, kind="ExternalInput").ap()
        self.wmod_d = dt("wmod", [36, 128, 4096], F32, kind="ExternalInput").ap()
        self.out_d = dt("out", [2, SEQ, D], F32, kind="ExternalOutput").ap()
        if dbg_stop is not None:
            self.dbgy_d = dt("dbgy", [2 * CTX, D], F32, kind="ExternalOutput").ap()

        A = nc.alloc_sbuf_tensor
        self.xT = A("xT", [128, KC * SEQ], F32)
        self.cp = A("cp_s", [128, CP_N], F32)
        self.mod = A("mod", [128, NL * 216], F32)
        self.gs = A("gs", [128, NL * 72], F32)
        self.hg = A("hg", [128, NL * 72], F32)
        self.sc = A("sc", [128, 24], F32)
        self.sc_b = A("sc_b", [128, 24], BF16)
        self.lg = A("lg", [128, NL * 8], F32)
        self.nlg = A("nlg", [128, NL * 8], F32)
        self.g128 = A("g128", [128, NL * 8], F32)
        self.ones_b = A("ones_b", [128, 128], BF16)
        self.ident_b = A("ident_b", [128, 128], BF16)
        self.ones128_b = A("ones128_b", [128, 128], BF16)
        self.ones256_b = A("ones256_b", [128, 128], BF16)
        self.cen_b = A("cen_b", [128, 128], BF16)
        self.cen256_b = A("cen256_b", [128, 128], BF16)
        self.neg256_b = A("neg256_b", [128, 128], BF16)
        self.poolw_b = A("poolw_b", [128, NL * 256], BF16)
        self.states = A("states", [128, NL * 2 * 2 * 4 * 128], BF16)
        self.sgall = A("sgall", [128, SEQ], BF16)
        self.hprev = A("hprev", [128, SEQ], BF16)
        self.NTMP = 7
        self.tmp = [A(f"tmp{i}", [128, 512], F32) for i in range(self.NTMP)]
        self.sqb = [A(f"sqb{i}", [128, 512], BF16) for i in range(2)]
        self.rstd = [A(f"rstd{i}", [128, 512], F32) for i in range(2)]
        self.rstdb = [Buf(f"rstd{i}") for i in range(2)]
        self.rstd_i = 0
        self.ropes = [A(f"rope{i}", [128, 2 * 512], F32) for i in range(2)]
        self.SCR_BYTES = 89088 + 2048
        self.scr = A("scr", [128, self.SCR_BYTES // 4], F32)
        self.ps = [nc.alloc_psum_tensor(f"ps{i}", [128, 512], F32) for i in range(8)]
        self.psb = [Buf(f"ps{i}") for i in range(8)]
        self.bank_i = 0
        self.bank_set = [0, 1, 2, 3, 4, 5]
        self.tmp_i = 0
        self.tmpb = [Buf(f"tmp{i}") for i in range(self.NTMP)]
        self.sqb_i = 0
        self.sqbb = [Buf(f"sqb{i}") for i in range(2)]
        self.bufs = {}
        self.scr_bufs = {}
        self.inherit = {}
        self.slot_ctr = {}

    def b(self, *key):
        if key not in self.bufs:
            self.bufs[key] = Buf(str(key))
        return self.bufs[key]

    def sb(self, *key):
        if key not in self.scr_bufs:
            self.scr_bufs[key] = Buf(str(key), self.inherit)
        return self.scr_bufs[key]

    def new_phase(self, keep=()):
        keepd = {}
        for key, bf in self.scr_bufs.items():
            if key[0] in keep:
                keepd[key] = bf
                continue
            toks = list(bf.readers.values())
            if bf.last_w is not None:
                toks.append(bf.last_w)
            for t in toks:
                if t.key not in self.inherit or self.inherit[t.key].ord < t.ord:
                    self.inherit[t.key] = t
        self.scr_bufs = keepd

    def scr_f32(self, off_bytes, n):
        assert off_bytes % 4 == 0 and off_bytes + 4 * n <= self.SCR_BYTES, (off_bytes, n)
        return self.scr[:, off_bytes // 4: off_bytes // 4 + n]

    def scr_bf(self, off_bytes, n):
        assert off_bytes % 4 == 0 and n % 2 == 0 and off_bytes + 2 * n <= self.SCR_BYTES, (off_bytes, n)
        return self.scr[:, off_bytes // 4: off_bytes // 4 + n // 2].bitcast(BF16)

    def bank(self):
        bs = self.bank_set
        self.bank_i = (self.bank_i + 1) % len(bs)
        return bs[self.bank_i]

    def gettmp(self):
        i = self.tmp_i
        self.tmp_i = (i + 1) % self.NTMP
        return self.tmp[i], self.tmpb[i]

    def getsq(self):
        i = self.sqb_i
        self.sqb_i = (i + 1) % 2
        return self.sqb[i], self.sqbb[i]

    def cpc(self, name, a=0, n=None):
        o, nn = CP_OFF[name]
        if n is None:
            n = nn - a
        return self.cp[:, o + a: o + a + n]

    def mm(self, out, lhsT, rhs, start, stop, R, W):
        self.P.op("pe", lambda h: h.matmul(out, lhsT, rhs, start=start, stop=stop), reads=R, writes=W)

    def tr(self, out, in_, ident, R, W):
        self.P.op("pe", lambda h: h.transpose(out, in_, ident), reads=R, writes=W)

    def act(self, out, in_, func, R, W, scale=1.0, bias=0.0):
        self.P.op("act", lambda h: h.activation(out=out, in_=in_, func=func, bias=bias, scale=scale),
                  reads=R, writes=W)

    def tt(self, out, in0, in1, op, R, W, eng="dve"):
        self.P.op(eng, lambda h: h.tensor_tensor(out=out, in0=in0, in1=in1, op=op), reads=R, writes=W)

    def stt(self, out, in0, scalar, in1, op0, op1, R, W, eng="dve"):
        self.P.op(eng, lambda h: h.scalar_tensor_tensor(out=out, in0=in0, scalar=scalar, in1=in1,
                                                        op0=op0, op1=op1), reads=R, writes=W)

    def ts(self, out, in0, s1, s2, op0, op1, R, W, eng="dve"):
        if s2 is None:
            self.P.op(eng, lambda h: h.tensor_scalar(out=out, in0=in0, scalar1=s1, scalar2=None, op0=op0),
                      reads=R, writes=W)
        else:
            self.P.op(eng, lambda h: h.tensor_scalar(out=out, in0=in0, scalar1=s1, scalar2=s2, op0=op0, op1=op1),
                      reads=R, writes=W)

    def cpy(self, out, in_, R, W, eng="dve"):
        self.P.op(eng, lambda h: h.tensor_copy(out=out, in_=in_), reads=R, writes=W)

    def rsqrt(self, out, in_, R, wb, clamp=False):
        if clamp:
            self.act(out, in_, AF.Relu, list(R), [wb])
            self.act(out, out, AF.Ln, [wb, self.b("cp")], [wb], bias=self.cpc("eps"))
        else:
            self.act(out, in_, AF.Ln, list(R) + [self.b("cp")], [wb], bias=self.cpc("eps"))
        self.act(out, out, AF.Exp, [wb], [wb], scale=-0.5)

    def recip(self, out, in_, R, W):
        self.P.op("dve", lambda h: h.reciprocal(out=out, in_=in_), reads=R, writes=W)

    def memset(self, ap, val, W, eng="dve"):
        self.P.op(eng, lambda h: h.memset(ap, val), writes=W)

    def dma(self, q, out, in_, sem, R, W):
        self.P.op(q, lambda h: h.dma_start(out=out, in_=in_), reads=R, writes=W, dma=sem)

    def xv(self, k, t0, n):
        return self.xT[:, k * SEQ + t0: k * SEQ + t0 + n]

    def xb(self, k, blk):
        return self.b("x", k, blk)

    def modcol(self, l, j, k, grp):
        c = l * 216 + (j * 8 + k) * 3 + grp
        return self.mod[:, c:c + 1]

    def gscol(self, l, jn, k, grp):
        c = l * 72 + (jn * 8 + k) * 3 + grp
        return self.gs[:, c:c + 1]

    def hgcol(self, l, jn, k, grp):
        c = l * 72 + (jn * 8 + k) * 3 + grp
        return self.hg[:, c:c + 1]

    def prologue(self):
        P = self.P
        bcp = self.b("cp")
        self.dma("sp", self.cp[:], self.cp_d, "cp", [], [bcp])
        cp2 = self.scr_f32(32 * 1024, CP2_N)
        bcp2 = self.sb("cp2")
        self.dma("sp", cp2, self.cp2_d, "cp2", [], [bcp2])

        def c2(name, a=0, n=None):
            o, nn = CP2_OFF[name]
            if n is None:
                n = nn - a
            return cp2[:, o + a: o + a + n]
        self.cpy(self.ones_b[:], c2("ones1024"), [bcp2], [self.b("ones_b")])
        self.cpy(self.ident_b[:], self.cpc("ident"), [bcp], [self.b("ident_b")])
        self.cpy(self.ones128_b[:], c2("ones128"), [bcp2], [self.b("ones_b")])
        self.cpy(self.ones256_b[:], c2("ones256"), [bcp2], [self.b("ones_b")])
        self.ts(self.cen_b[:], self.cpc("ident"), -1.0 / 128, None, ALU.add, None, [bcp], [self.b("ones_b")])
        self.ts(self.cen256_b[:], self.cpc("ident"), -1.0 / 256, None, ALU.add, None, [bcp], [self.b("ones_b")])
        self.ts(self.neg256_b[:], c2("ones256"), -1.0, None, ALU.mult, None, [bcp2], [self.b("ones_b")])
        self.dma("pool", self.poolw_b[:], self.poolw_d, "poolw", [], [self.b("poolw_b")])
        bsc = self.b("sc")
        self.act(self.sc[:], c2("cvec"), AF.Silu, [bcp2], [bsc])
        self.cpy(self.sc_b[:], self.sc[:], [bsc], [bsc])
        blg = self.b("lg")
        for l in range(NL):
            sl = slice(l * 8, (l + 1) * 8)
            self.act(self.nlg[:, sl], self.cpc(f"dec{l}"), AF.Exp, [bcp], [blg], scale=-1.0)
            self.act(self.nlg[:, sl], self.nlg[:, sl], AF.Ln, [blg, bcp], [blg], bias=self.cpc("one"))
            self.ts(self.lg[:, sl], self.nlg[:, sl], -1.0, None, ALU.mult, None, [blg], [blg])
            self.act(self.g128[:, sl], self.lg[:, sl], AF.Exp, [blg], [blg], scale=128.0)
        for l in range(NL):
            bank = 6 + l
            for cg in range(18):
                s = (l * 18 + cg) % 4
                wt = self.scr_bf(s * 8192, 4096)
                wb = self.sb("wm", s)
                self.dma("pool", wt, self.wmod_d[l * 18 + cg], f"wm{s}", [], [wb])
                for jj in range(4):
                    j = cg * 4 + jj
                    for k in range(KC):
                        self.mm(self.ps[bank][:, j * 3:(j + 1) * 3],
                                wt[:, k * 512 + jj * 128: k * 512 + (jj + 1) * 128],
                                self.sc_b[:, k * 3:(k + 1) * 3], k == 0, k == KC - 1,
                                [wb, bsc], [self.psb[bank]])
            bm = self.b("mod", l)
            self.tt(self.mod[:, l * 216:(l + 1) * 216], self.ps[bank][:, 0:216], c2(f"bmod{l}"),
                    ALU.add, [self.psb[bank], bcp2], [bm])
            for jn in range(3):
                j = 3 * jn + 1
                self.stt(self.gs[:, l * 72 + jn * 24: l * 72 + (jn + 1) * 24],
                         self.mod[:, l * 216 + j * 24: l * 216 + (j + 1) * 24], 1.0,
                         c2(f"normg{l}", jn * 24, 24), ALU.add, ALU.mult, [bm, bcp2], [self.b("gs", l)])
                jg = 3 * jn + 2
                self.ts(self.hg[:, l * 72 + jn * 24: l * 72 + (jn + 1) * 24],
                        self.mod[:, l * 216 + jg * 24: l * 216 + (jg + 1) * 24],
                        1.0 if jn == 1 else 0.5, None, ALU.mult, None, [bm], [self.b("hg", l)])
        self.new_phase()

    def load_seg(self, src, ntok):
        bident = self.b("cp")
        for tt_ in range(ntok // 128):
            s = tt_ % 8
            st = self.scr_f32(s * 4096, 1024)
            stb = self.sb("xin", s)
            self.dma("sp", st, src[tt_ * 128:(tt_ + 1) * 128, :], f"xin{s}", [], [stb])
            blk = tt_ // 4
            for half in range(2):
                bk = self.bank()
                for kk in range(4):
                    k = half * 4 + kk
                    self.tr(self.ps[bk][:, kk * 128:(kk + 1) * 128], st[:, k * 128:(k + 1) * 128],
                            self.cpc("ident"), [stb, bident], [self.psb[bk]])
                out = self.xT[:, half * 4 * SEQ:(half * 4 + 4) * SEQ].rearrange("p (k t) -> p k t", k=4)[
                    :, :, tt_ * 128:(tt_ + 1) * 128]
                in_ = self.ps[bk][:, :].rearrange("p (k t) -> p k t", k=4)
                eng = "dve" if half == 0 else "act"
                W = [self.xb(half * 4 + kk, blk) for kk in range(4)]
                if eng == "dve":
                    self.cpy(out, in_, [self.psb[bk]], W)
                else:
                    self.act(out, in_, AF.Identity, [self.psb[bk]], W)
        self.new_phase()

    def store_seg(self, dst, ntok):
        bident = self.b("cp")
        for tt_ in range(ntok // 128):
            s = tt_ % 8
            st = self.scr_f32(s * 4096, 1024)
            stb = self.sb("xout", s)
            blk = tt_ // 4
            for half in range(2):
                bk = self.bank()
                for kk in range(4):
                    k = half * 4 + kk
                    self.tr(self.ps[bk][:, kk * 128:(kk + 1) * 128], self.xv(k, tt_ * 128, 128),
                            self.cpc("ident"), [self.xb(k, blk), bident], [self.psb[bk]])
                if half == 0:
                    self.cpy(st[:, 0:512], self.ps[bk][:, :], [self.psb[bk]], [stb])
                else:
                    self.act(st[:, 512:1024], self.ps[bk][:, :], AF.Identity, [self.psb[bk]], [stb])
            self.dma("sp", dst[tt_ * 128:(tt_ + 1) * 128, :], st, f"xout{s}", [stb], [self.b("outd", s)])
        self.new_phase()

    def rms_rstd(self, blk, sq_eng="act", defer=False):
        bk = 6 + (blk % 2)
        for k in range(KC):
            sq, sqb = self.getsq()
            xs = self.xv(k, blk * 512, 512)
            if sq_eng == "act":
                self.act(sq[:], xs, AF.Square, [self.xb(k, blk)], [sqb])
            else:
                self.tt(sq[:], xs, xs, ALU.mult, [self.xb(k, blk)], [sqb])
            self.mm(self.ps[bk][:], self.ones_b[:], sq[:], k == 0, k == KC - 1,
                    [sqb, self.b("ones_b")], [self.psb[bk]])
        ri = self.rstd_i
        self.rstd_i = 1 - ri
        rs, rsb = self.rstd[ri], self.rstdb[ri]

        def fin():
            self.rsqrt(rs[:], self.ps[bk][:], [self.psb[bk]], rsb)
        if defer:
            return rs, rsb, fin
        fin()
        return rs, rsb

    def norm_mod(self, blk, l, jn, grp, hT, hoff, hbuf, rs=None):
        rs, rsb = rs if rs is not None else self.rms_rstd(blk)
        for k in range(KC):
            t, tb = self.gettmp()
            self.tt(t[:], self.xv(k, blk * 512, 512), rs[:], ALU.mult, [self.xb(k, blk), rsb], [tb])
            self.act(hT(k, hoff), t[:], AF.Identity, [tb, self.b("gs", l), self.b("mod", l)], [hbuf(k)],
                     scale=self.gscol(l, jn, k, grp), bias=self.modcol(l, 3 * jn, k, grp))

    def ffn(self, l, f, nblk, grp):
        jn = 0 if f == 0 else 2
        lf = l * 2 + f
        HT = 0
        GT = 16 * 1024
        W1 = 60 * 1024
        W3 = 68 * 1024
        W2 = 76 * 1024
        for g0 in range(0, nblk, 2):
            blks = list(range(g0, min(g0 + 2, nblk)))
            hTf = lambda k, off: self.scr_bf(HT + k * 2048 + off * 2, 512)
            rs0 = self.rms_rstd(blks[0])
            if len(blks) > 1:
                rs1, rs1b, fin1 = self.rms_rstd(blks[1], sq_eng="dve", defer=True)
            self.norm_mod(blks[0], l, jn, grp, hTf, 0, lambda k: self.sb("h", k, 0), rs=rs0)
            if len(blks) > 1:
                fin1()
                self.norm_mod(blks[1], l, jn, grp, hTf, 512, lambda k: self.sb("h", k, 1), rs=(rs1, rs1b))
            for cg in range(11):
                s = self.slot("w13")
                w1t = self.scr_bf(W1 + s * 4096, 2048)
                w3t = self.scr_bf(W3 + s * 4096, 2048)
                self.dma("pool", w1t, self.w1_d[lf * 11 + cg], f"w1{s}", [], [self.sb("w1", s)])
                self.dma("pool", w3t, self.w3_d[lf * 11 + cg], f"w3{s}", [], [self.sb("w3", s)])
                for bi, blk in enumerate(blks):
                    for m in range(2):
                        pa = self.bank()
                        pb = self.bank()
                        for k in range(KC):
                            self.mm(self.ps[pa][:], w1t[:, k * 256 + m * 128: k * 256 + (m + 1) * 128],
                                    self.scr_bf(HT + k * 2048 + bi * 1024, 512), k == 0, k == KC - 1,
                                    [self.sb("w1", s), self.sb("h", k, bi)], [self.psb[pa]])
                        for k in range(KC):
                            self.mm(self.ps[pb][:], w3t[:, k * 256 + m * 128: k * 256 + (m + 1) * 128],
                                    self.scr_bf(HT + k * 2048 + bi * 1024, 512), k == 0, k == KC - 1,
                                    [self.sb("w3", s), self.sb("h", k, bi)], [self.psb[pb]])
                        t, tb = self.gettmp()
                        self.act(t[:], self.ps[pa][:], AF.Silu, [self.psb[pa]], [tb])
                        j = cg * 2 + m
                        self.tt(self.scr_bf(GT + j * 2048 + bi * 1024, 512), t[:], self.ps[pb][:], ALU.mult,
                                [tb, self.psb[pb]], [self.sb("g", j, bi)])
            for dg in range(KC):
                s = self.slot("w2")
                w2t = self.scr_bf(W2 + s * 5632, 2816)
                self.dma("pool", w2t, self.w2_d[lf * 8 + dg], f"w2{s}", [], [self.sb("w2", s)])
                for bi, blk in enumerate(blks):
                    pc = self.bank()
                    for j in range(FC):
                        self.mm(self.ps[pc][:], w2t[:, j * 128:(j + 1) * 128],
                                self.scr_bf(GT + j * 2048 + bi * 1024, 512), j == 0, j == FC - 1,
                                [self.sb("w2", s), self.sb("g", j, bi)], [self.psb[pc]])
                    xs = self.xv(dg, blk * 512, 512)
                    self.stt(xs, self.ps[pc][:], self.hgcol(l, jn, dg, grp), xs, ALU.mult, ALU.add,
                             [self.psb[pc], self.b("hg", l), self.xb(dg, blk)], [self.xb(dg, blk)])
        self.new_phase()

    def slot(self, name):
        v = self.slot_ctr.get(name, 0)
        self.slot_ctr[name] = v + 1
        return v % 2

    def mixer(self, l, nblk, grp, seqs, rope, full, bidx):
        T = nblk * 512
        HT = 0
        PH = 32 * 1024
        WI = 77 * 1024
        hT = lambda k, off: self.scr_bf(HT + k * (2 * T) + off * 2, 512)
        pend = {}
        for blk in range(min(2, nblk)):
            pend[blk] = self.rms_rstd(blk, sq_eng="act" if blk % 2 == 0 else "dve", defer=True)
        for blk in range(nblk):
            rs_, rsb_, fin_ = pend.pop(blk)
            fin_()
            self.norm_mod(blk, l, 1, grp, hT, blk * 512, lambda k, blk=blk: self.sb("h", k, blk), rs=(rs_, rsb_))
            if blk + 2 < nblk:
                pend[blk + 2] = self.rms_rstd(blk + 2, sq_eng="act" if blk % 2 == 0 else "dve", defer=True)
        KEEP = ("h", "wi")

        def load_wi(slot, chunk):
            wt = self.scr_bf(WI + slot * 2048, 1024)
            self.dma("pool", wt, self.win_d[l * 22 + chunk], f"wi{slot}", [], [self.sb("wi", slot)])
            return wt

        def load_wo(chunk, slot=4):
            wt = self.scr_bf(WI + slot * 2048, 1024)
            self.dma("pool", wt, self.wout_d[l * 8 + chunk], f"wi{slot}", [], [self.sb("wi", slot)])
            return wt

        def proj_fm(wt, slot, blk, bk):
            for k in range(KC):
                self.mm(self.ps[bk][:], wt[:, k * 128:(k + 1) * 128], hT(k, blk * 512), k == 0, k == KC - 1,
                        [self.sb("wi", slot), self.sb("h", k, blk)], [self.psb[bk]])

        def wout_partial(terms, blk):
            for dg in range(KC):
                pc = self.bank()
                for ti, (wo, wslot, cat_ap, catb) in enumerate(terms):
                    self.mm(self.ps[pc][:], wo[:, dg * 128:(dg + 1) * 128], cat_ap, ti == 0, ti == len(terms) - 1,
                            [self.sb("wi", wslot), catb], [self.psb[pc]])
                xs = self.xv(dg, blk * 512, 512)
                self.stt(xs, self.ps[pc][:], self.hgcol(l, 1, dg, grp), xs, ALU.mult, ALU.add,
                         [self.psb[pc], self.b("hg", l), self.xb(dg, blk)], [self.xb(dg, blk)])

        bcp = self.b("cp")
        if full:
            for ch in range(2):
                self.new_phase(KEEP + ("pcat",))
                PADL = 8
                LP = T + 16 * len(seqs)
                pp = self.scr_f32(PH, LP)
                ta = self.scr_f32(PH + 4 * LP, LP)
                tb_ = self.scr_f32(PH + 8 * LP, LP)
                pooled = self.scr_bf(PH + 12 * LP, T)
                CATP = PH + 45 * 1024 - 4 * T
                assert PH + 12 * LP + 2 * T <= CATP
                pcat = [self.scr_bf(CATP + c_ * 2 * T, T) for c_ in range(2)]
                cat = pcat[ch]
                bpp, bta, btb = self.sb("pp"), self.sb("ta"), self.sb("tb")
                for si_, (c0_, ncn_, _) in enumerate(seqs):
                    b0_ = c0_ * 128 + 16 * si_
                    L_ = ncn_ * 128
                    self.memset(pp[:, b0_: b0_ + 8], 0.0, [bpp])
                    self.memset(pp[:, b0_ + 8 + L_: b0_ + 16 + L_], 0.0, [bpp])
                    self.memset(ta[:, b0_: b0_ + 1], 0.0, [bta])
                    self.memset(tb_[:, b0_: b0_ + 1], 0.0, [btb])
                    self.memset(tb_[:, b0_ + L_ + 15: b0_ + L_ + 16], 0.0, [btb])
                wt = load_wi(0, ch)
                wo_p = load_wo(ch, 4 + ch)
                if ch == 0:
                    wo_p0 = wo_p

                def ppos(tok):
                    si = 0
                    for i, (c0, ncn, _) in enumerate(seqs):
                        if tok >= c0 * 128:
                            si = i
                    return tok + 16 * si + PADL

                for blk in range(nblk):
                    bk = self.bank()
                    proj_fm(wt, 0, blk, bk)
                    for piece in range(2):
                        t0 = blk * 512 + piece * 256
                        p0 = ppos(t0)
                        self.act(pp[:, p0:p0 + 256], self.ps[bk][:, piece * 256:(piece + 1) * 256], AF.Identity,
                                 [self.psb[bk]], [bpp])
                for (c0, ncn, _) in seqs:
                    L = ncn * 128
                    base = ppos(c0 * 128) - PADL
                    LL = L + 16
                    wins = (POOL_WINDOWS[ch * 2], POOL_WINDOWS[ch * 2 + 1])
                    full = slice(0, 128)
                    hi_half = slice(64, 128)
                    self.tt(ta[full, base + 1: base + LL], pp[full, base: base + LL - 1], pp[full, base + 1: base + LL],
                            ALU.add, [bpp], [bta])
                    have = {2: (ta, bta)}
                    cur, curb, other, otherb = ta, bta, tb_, btb
                    sh = 1
                    ww = 2
                    while ww < max(wins):
                        part = full if min(wins) >= 2 * ww else hi_half
                        self.tt(other[part, base + sh: base + LL - sh], cur[part, base: base + LL - 2 * sh],
                                cur[part, base + 2 * sh: base + LL], ALU.add, [curb], [otherb])
                        cur, curb, other, otherb = other, otherb, cur, curb
                        sh *= 2
                        ww *= 2
                        have[ww] = (cur, curb)
                    for half in range(2):
                        gi = ch * 2 + half
                        w = POOL_WINDOWS[gi]
                        ps_ = slice(half * 64, half * 64 + 64)
                        fb, fbb = have[w]
                        e0 = gi * 16
                        self.tt(fb[ps_, base + PADL: base + PADL + 8], fb[ps_, base + PADL: base + PADL + 8],
                                self.cpc("edge", e0, 8)[ps_, :], ALU.mult, [fbb, bcp], [fbb])
                        self.tt(fb[ps_, base + PADL + L - 8: base + PADL + L], fb[ps_, base + PADL + L - 8: base + PADL + L],
                                self.cpc("edge", e0 + 8, 8)[ps_, :], ALU.mult, [fbb, bcp], [fbb])
                        self.stt(pooled[ps_, c0 * 128: c0 * 128 + L], fb[ps_, base + PADL: base + PADL + L], 1.0 / w,
                                 pp[ps_, base + PADL: base + PADL + L], ALU.mult, ALU.subtract,
                                 [fbb, bpp], [self.sb("pooled")])
                for blk in range(nblk):
                    bk = self.bank()
                    self.mm(self.ps[bk][:], self.poolw_b[:, l * 256 + ch * 128: l * 256 + (ch + 1) * 128],
                            pooled[:, blk * 512:(blk + 1) * 512], True, True,
                            [self.b("poolw_b"), self.sb("pooled")], [self.psb[bk]])
                    self.act(cat[:, blk * 512:(blk + 1) * 512], self.ps[bk][:], AF.Identity, [self.psb[bk], bcp],
                             [self.sb("pcat", ch, blk)], scale=self.cpc(f"pool_scale{l}", ch, 1))
                    if ch == 1:
                        wout_partial([(wo_p0, 4, pcat[0][:, blk * 512:(blk + 1) * 512], self.sb("pcat", 0, blk)),
                                      (wo_p, 5, pcat[1][:, blk * 512:(blk + 1) * 512], self.sb("pcat", 1, blk))], blk)
            self.new_phase(KEEP)
            nsq = len(seqs)
            ZP = T + 30 * nsq
            if (2 * ZP) % 4:
                ZP += 1
            ZB = PH
            DG = ZB + 4 * ZP
            AC = DG + 2 * 31 * 256
            AB = AC + 2 * 2 * 2048
            CC = AB + 4 * 1024
            assert CC + 4 * T <= WI, (CC + 4 * T, WI)
            zb_ = [self.scr_bf(ZB + c * 2 * ZP, ZP) for c in range(2)]
            cat = [self.scr_bf(CC + c * 2 * T, T) for c in range(2)]

            def zpos(tok):
                si = 0
                for i, (c0, ncn, _) in enumerate(seqs):
                    if tok >= c0 * 128:
                        si = i
                return tok + 30 * si + 15

            for c in range(2):
                self.memset(zb_[c], 0.0, [self.sb("z", c)])
                for kk in range(31):
                    dgm = self.scr_bf(DG + (c * 31 + kk) * 256, 128)
                    self.ts(dgm, self.ident_b[:], self.cpc(f"conv_dw{l}", c * 31 + kk, 1), None, ALU.mult, None,
                            [self.b("ident_b"), bcp], [self.sb("dg", c)])
                wa = load_wi(0, 18 + c)
                wg = load_wi(1, 20 + c)
                for blk in range(nblk):
                    ba = self.bank()
                    bg = self.bank()
                    proj_fm(wa, 0, blk, ba)
                    proj_fm(wg, 1, blk, bg)
                    t, tb = self.gettmp()
                    self.act(t[:], self.ps[bg][:], AF.Sigmoid, [self.psb[bg]], [tb])
                    for piece in range(2):
                        p0 = zpos(blk * 512 + piece * 256)
                        self.tt(zb_[c][:, p0:p0 + 256], self.ps[ba][:, piece * 256:(piece + 1) * 256],
                                t[:, piece * 256:(piece + 1) * 256], ALU.mult, [self.psb[ba], tb], [self.sb("z", c)])
            for blk in range(nblk):
                par = blk % 2
                accs = []
                for c in range(2):
                    bk = self.bank()
                    if nsq == 1:
                        pieces = [(0, 512, zpos(blk * 512) - 15)]
                    else:
                        pieces = [(0, 256, zpos(blk * 512) - 15), (256, 256, zpos(blk * 512 + 256) - 15)]
                    for (co, n, zs) in pieces:
                        for kk in range(31):
                            dgm = self.scr_bf(DG + (c * 31 + kk) * 256, 128)
                            self.mm(self.ps[bk][:, co:co + n], dgm, zb_[c][:, zs + kk: zs + kk + n], kk == 0, kk == 30,
                                    [self.sb("dg", c), self.sb("z", c)], [self.psb[bk]])
                    ab = self.scr_bf(AB + (c * 2) * 1024, 512)
                    sq = self.scr_bf(AB + (c * 2 + 1) * 1024, 512)
                    abb = self.sb("ab", c)
                    cb = self.cpc(f"conv_b{l}", c, 1)
                    self.act(ab, self.ps[bk][:], AF.Identity, [self.psb[bk], bcp], [abb], bias=cb)
                    accs.append((ab, sq, abb))
                for c in range(2):
                    pc_ = 6 + c
                    self.mm(self.ps[pc_][:], self.cen256_b[:], accs[c][0], True, False,
                            [self.b("ones_b"), accs[c][2]], [self.psb[pc_]])
                    self.mm(self.ps[pc_][:], self.neg256_b[:], accs[1 - c][0], False, True,
                            [self.b("ones_b"), accs[1 - c][2]], [self.psb[pc_]])
                for c in range(2):
                    self.act(accs[c][1], self.ps[6 + c][:], AF.Square, [self.psb[6 + c]], [self.sb("sqc", c)])
                bv = self.bank()
                for c in range(2):
                    self.mm(self.ps[bv][:], self.ones256_b[:], accs[c][1], c == 0, c == 1,
                            [self.b("ones_b"), self.sb("sqc", c)], [self.psb[bv]])
                var, varb = self.gettmp()
                self.rsqrt(var[:], self.ps[bv][:], [self.psb[bv]], varb)
                for c in range(2):
                    d, db = self.gettmp()
                    self.tt(d[:], self.ps[6 + c][:], var[:], ALU.mult, [self.psb[6 + c], varb], [db])
                    self.act(cat[c][:, blk * 512:(blk + 1) * 512], d[:], AF.Silu, [db, bcp], [self.sb("ccat", c, blk)],
                             scale=self.cpc(f"ln_g{l}", c, 1), bias=self.cpc(f"ln_b{l}", c, 1))
            wo_c = [load_wo(6 + c, 4 + c) for c in range(2)]
            for blk in range(nblk):
                wout_partial([(wo_c[c], 4 + c, cat[c][:, blk * 512:(blk + 1) * 512], self.sb("ccat", c, blk))
                              for c in range(2)], blk)

        nch = T // 128
        QF, QB, KF, KB = PH, PH + 2 * T, PH + 4 * T, PH + 6 * T
        KFT, KBT = PH + 8 * T, PH + 10 * T
        SF, SB = PH + 12 * T, PH + 14 * T
        VT = PH + 16 * T
        RF = PH + 18 * T
        GT = RF + 1024
        ST = GT + 2048
        CAT = ST + 2048
        OB = CAT + 2048
        assert OB + 2048 <= WI, (OB, WI)
        for h in range(4):
            self.new_phase(KEEP)
            qf = self.scr_bf(QF, T)
            qb = self.scr_bf(QB, T)
            kf = self.scr_bf(KF, T)
            kb = self.scr_bf(KB, T)
            kfT = self.scr_bf(KFT, T)
            kbT = self.scr_bf(KBT, T)
            sf = self.scr_bf(SF, T)
            sbk = self.scr_bf(SB, T)
            vT = self.scr_bf(VT, T)
            Rst = self.scr_f32(RF, 256)
            gt = self.scr_f32(GT, 512)
            bgt = self.sb("gt")
            lgf = self.lg[:, l * 8 + h: l * 8 + h + 1]
            lgb = self.lg[:, l * 8 + 4 + h: l * 8 + 4 + h + 1]
            nlgf = self.nlg[:, l * 8 + h: l * 8 + h + 1]
            nlgb = self.nlg[:, l * 8 + 4 + h: l * 8 + 4 + h + 1]
            blg = self.b("lg")
            self.act(gt[:, 0:128], self.cpc("pos1"), AF.Exp, [bcp, blg], [bgt], scale=lgf)
            self.act(gt[:, 128:256], self.cpc("posr"), AF.Exp, [bcp, blg], [bgt], scale=lgb)
            self.act(gt[:, 256:257], self.cpc("pos1c"), AF.Exp, [bcp, blg], [bgt], scale=nlgf)
            self.act(gt[:, 257:258], self.cpc("posrc"), AF.Exp, [bcp, blg], [bgt], scale=nlgb)
            ksc = 128.0 ** -0.5
            self.ts(gt[:, 256:258], gt[:, 256:258], ksc, None, ALU.mult, None, [bgt], [bgt])
            wkcol = (gt[:, 256:257], gt[:, 257:258])
            wq = load_wi(0, 2 + h)
            wk = load_wi(1, 6 + h)
            wv = load_wi(2, 10 + h)
            for blk in range(nblk):
                if rope:
                    rs_ = self.slot("rope")
                    rt = self.ropes[rs_]
                    rtb = self.b("rope", rs_)
                    self.dma("sp", rt[:, :].rearrange("p (a t) -> p a t", a=2),
                             self.rope_d[:, :, blk * 512:(blk + 1) * 512], f"rope{rs_}", [], [rtb])
                items = []
                for which, wt, slot, outs in (("q", wq, 0, ((qf, 0), (qb, 128))), ("k", wk, 1, ())):
                    if not full and which == "q":
                        continue
                    bk = self.bank()
                    proj_fm(wt, slot, blk, bk)
                    if rope:
                        q32, q32b = self.gettmp()
                        self.act(q32[:], self.ps[bk][:], AF.Identity, [self.psb[bk]], [q32b])
                        items.append((which, outs, q32, q32b))
                    else:
                        items.append((which, outs, self.ps[bk], self.psb[bk]))
                for (which, outs, q32, q32b) in items:
                    if rope:
                        bp = self.bank()
                        self.mm(self.ps[bp][:], self.cpc("perm"), q32[:], True, True, [bcp, q32b], [self.psb[bp]])
                        t2, t2b = self.gettmp()
                        self.tt(t2[:], self.ps[bp][:], rt[:, 512:1024], ALU.mult, [self.psb[bp], rtb], [t2b])
                        self.tt(q32[:], q32[:], rt[:, 0:512], ALU.mult, [q32b, rtb], [q32b])
                        if which == "k":
                            self.tt(kf[:, blk * 512:(blk + 1) * 512], q32[:], t2[:], ALU.add, [q32b, t2b],
                                    [self.sb("k", 256, blk)])
                            continue
                        self.tt(q32[:], q32[:], t2[:], ALU.add, [q32b, t2b], [q32b])
                    elif which == "k":
                        self.cpy(kf[:, blk * 512:(blk + 1) * 512], q32[:], [q32b], [self.sb("k", 256, blk)])
                        continue
                    src, srcb = q32[:], q32b
                    for (dst, go) in outs:
                        o3 = dst[:, blk * 512:(blk + 1) * 512].rearrange("p (c i) -> p c i", c=4)
                        i3 = src.rearrange("p (c i) -> p c i", c=4)
                        g3 = gt[:, go:go + 128].unsqueeze(1).to_broadcast([128, 4, 128])
                        self.tt(o3, i3, g3, ALU.mult, [srcb, bgt], [self.sb(which, go, blk)])
            for cgp in range(nch // 4):
                bk = self.bank()
                for cc in range(4):
                    n = cgp * 4 + cc
                    blk = n // 4
                    for k in range(KC):
                        self.mm(self.ps[bk][:, cc * 128:(cc + 1) * 128], self.scr_bf(HT + k * (2 * T) + n * 256, 128), wv[:, k * 128:(k + 1) * 128],
                                k == 0, k == KC - 1, [self.sb("h", k, blk), self.sb("wi", 2)], [self.psb[bk]])
                self.act(vT[:, cgp * 512:(cgp + 1) * 512], self.ps[bk][:], AF.Identity, [self.psb[bk]], [self.sb("vT", cgp)])
            for cgp in range(nch // 4):
                bk = self.bank()
                pst = self.ps[bk][:, :].bitcast(BF16)
                for cc in range(4):
                    n = cgp * 4 + cc
                    self.tr(pst[:, cc * 128:(cc + 1) * 128], kf[:, n * 128:(n + 1) * 128], self.ident_b[:],
                            [self.sb("k", 256, n // 4), self.b("ident_b")], [self.psb[bk]])
                self.act(kfT[:, cgp * 512:(cgp + 1) * 512], pst[:, 0:512], AF.Identity, [self.psb[bk], bgt],
                         [self.sb("kfT", cgp)], scale=wkcol[0])
                self.act(kbT[:, cgp * 512:(cgp + 1) * 512], pst[:, 0:512], AF.Identity, [self.psb[bk], bgt],
                         [self.sb("kbT", cgp)], scale=wkcol[1])
            if full:
                wg = load_wi(3, 14 + h)
                for blk in range(nblk):
                    bg = self.bank()
                    proj_fm(wg, 3, blk, bg)
                    self.act(self.sgall[:, blk * 512:(blk + 1) * 512], self.ps[bg][:], AF.Silu, [self.psb[bg]],
                             [self.b("sg", blk)])
            for (c0, ncn, bi_) in seqs:
                for idx in range(ncn):
                    for d_ in range(2):
                        kT, nm = (kfT, "kfT") if d_ == 0 else (kbT, "kbT")
                        sdst = sf if d_ == 0 else sbk
                        g128c = self.g128[:, l * 8 + d_ * 4 + h: l * 8 + d_ * 4 + h + 1]
                        R = Rst[:, d_ * 128:(d_ + 1) * 128]
                        Rb = self.sb("R", d_)
                        st_off = (((l * 2 + bi_) * 2 + d_) * 4 + h) * 128
                        S0 = self.states[:, st_off: st_off + 128]
                        S0b = self.b("state", l, bi_, d_, h)
                        n = c0 + idx if d_ == 0 else c0 + ncn - 1 - idx
                        if full:
                            if idx == 0:
                                if rope:
                                    self.cpy(sdst[:, n * 128:(n + 1) * 128], S0, [S0b], [self.sb("S", d_, n)])
                                else:
                                    self.memset(sdst[:, n * 128:(n + 1) * 128], 0.0, [self.sb("S", d_, n)])
                            else:
                                self.act(sdst[:, n * 128:(n + 1) * 128], R, AF.Identity, [Rb, blg], [self.sb("S", d_, n)],
                                         scale=g128c)
                        last = idx == ncn - 1
                        if last and rope:
                            continue
                        bk = self.bank()
                        self.mm(self.ps[bk][:, 0:128], kT[:, n * 128:(n + 1) * 128], vT[:, n * 128:(n + 1) * 128], True, True,
                                [self.sb(nm, n // 4), self.sb("vT", n // 4)], [self.psb[bk]])
                        if idx == 0:
                            if rope:
                                self.tt(R, self.ps[bk][:, 0:128], S0, ALU.add, [self.psb[bk], S0b], [Rb])
                            else:
                                self.cpy(R, self.ps[bk][:, 0:128], [self.psb[bk]], [Rb])
                        else:
                            self.stt(R, R, g128c, self.ps[bk][:, 0:128], ALU.mult, ALU.add, [Rb, blg, self.psb[bk]], [Rb])
                        if last and not rope:
                            self.act(S0, R, AF.Identity, [Rb, blg], [S0b], scale=g128c)
            if not full:
                continue
            odd = h % 2 == 1
            if odd:
                wo_prev = load_wo(2 + h - 1, 4)
                wo_cur = load_wo(2 + h, 5)

            def stageA1(blk):
                sts = []
                for d_ in range(2):
                    kk_, kgo = kf, 256
                    qq_, qgo = (qf, 0) if d_ == 0 else (qb, 128)
                    bk = self.bank()
                    for cc in range(4):
                        n = blk * 4 + cc
                        self.mm(self.ps[bk][:, cc * 128:(cc + 1) * 128], kk_[:, n * 128:(n + 1) * 128],
                                qq_[:, n * 128:(n + 1) * 128], True, True,
                                [self.sb("k", kgo, blk), self.sb("q", qgo, blk)], [self.psb[bk]])
                    sT = self.scr_bf(ST + d_ * 1024, 512)
                    mk = self.cpc("maskf" if d_ == 0 else "maskb").unsqueeze(1).to_broadcast([128, 4, 128])
                    self.stt(sT.rearrange("p (c i) -> p c i", c=4), self.ps[bk][:, :].rearrange("p (c i) -> p c i", c=4),
                             wkcol[d_], mk, ALU.mult, ALU.mult, [self.psb[bk], bcp, bgt], [self.sb("sT", d_)])
                    sts.append(sT)
                bo = 4 + blk % 2
                for cc in range(4):
                    n = blk * 4 + cc
                    oc = self.ps[bo][:, cc * 128:(cc + 1) * 128]
                    vch = vT[:, n * 128:(n + 1) * 128]
                    self.mm(oc, vch, sts[0][:, cc * 128:(cc + 1) * 128], True, False,
                            [self.sb("vT", n // 4), self.sb("sT", 0)], [self.psb[bo]])
                    self.mm(oc, vch, sts[1][:, cc * 128:(cc + 1) * 128], False, False,
                            [self.sb("vT", n // 4), self.sb("sT", 1)], [self.psb[bo]])
                    self.mm(oc, sf[:, n * 128:(n + 1) * 128], qf[:, n * 128:(n + 1) * 128], False, False,
                            [self.sb("S", 0, n), self.sb("q", 0, blk)], [self.psb[bo]])
                    self.mm(oc, sbk[:, n * 128:(n + 1) * 128], qb[:, n * 128:(n + 1) * 128], False, True,
                            [self.sb("S", 1, n), self.sb("q", 128, blk)], [self.psb[bo]])
                return bo

            def stageA2(blk, bo):
                par = blk % 2
                rb = self.b("rope", par)
                o32 = self.ropes[par][:, 0:512]
                mu = self.ropes[par][:, 512:1024]
                var, varb = self.rstd[par], self.rstdb[par]
                ob = self.scr_bf(OB, 512)
                osq = self.scr_bf(OB + 1024, 512)
                obb = self.sb("ob")
                pc_ = 6 + par
                self.act(ob, self.ps[bo][:], AF.Identity, [self.psb[bo]], [obb])
                self.mm(self.ps[pc_][:], self.cen_b[:], ob, True, True, [self.b("ones_b"), obb], [self.psb[pc_]])
                self.act(osq, self.ps[pc_][:], AF.Square, [self.psb[pc_]], [obb])
                bv = self.bank()
                self.mm(self.ps[bv][:], self.ones128_b[:], osq, True, True, [self.b("ones_b"), obb], [self.psb[bv]])
                self.rsqrt(var[:], self.ps[bv][:], [self.psb[bv]], varb)

            def stageB(blk):
                par = blk % 2
                rb = self.b("rope", par)
                o32 = self.ropes[par][:, 0:512]
                mu = self.ropes[par][:, 512:1024]
                var, varb = self.rstd[par], self.rstdb[par]
                pc_ = 6 + par
                self.tt(o32, self.ps[pc_][:], var[:], ALU.mult, [self.psb[pc_], varb], [rb])
                if odd:
                    cat, catb = self.scr_bf(CAT + par * 1024, 512), self.sb("hcat", par)
                else:
                    cat, catb = self.hprev[:, blk * 512:(blk + 1) * 512], self.b("hprev", blk)
                self.stt(cat, o32, self.cpc(f"gn_g{l}", h, 1), self.sgall[:, blk * 512:(blk + 1) * 512], ALU.mult, ALU.mult,
                         [rb, bcp, self.b("sg", blk)], [catb])

            def stageC(blk):
                if not odd:
                    return
                par = blk % 2
                wout_partial([(wo_prev, 4, self.hprev[:, blk * 512:(blk + 1) * 512], self.b("hprev", blk)),
                              (wo_cur, 5, self.scr_bf(CAT + par * 1024, 512), self.sb("hcat", par))], blk)

            self.bank_set = [0, 1, 2, 3]
            bo_ = stageA1(0)
            stageA2(0, bo_)
            for blk in range(nblk):
                if blk + 1 < nblk:
                    bo_ = stageA1(blk + 1)
                stageB(blk)
                stageC(blk)
                if blk + 1 < nblk:
                    stageA2(blk + 1, bo_)
            self.bank_set = [0, 1, 2, 3, 4, 5]
        self.new_phase()

    def final_norm(self, nblk):
        for blk in range(nblk):
            rs, rsb = self.rms_rstd(blk, sq_eng="act" if blk % 2 == 0 else "dve")
            for k in range(KC):
                xs = self.xv(k, blk * 512, 512)
                self.stt(xs, xs, self.cpc("final_g", k, 1), rs[:], ALU.mult, ALU.mult,
                         [self.xb(k, blk), self.b("cp"), rsb], [self.xb(k, blk)])

    def build(self):
        stop = self.dbg_stop
        self.prologue()
        ctx_seqs = [(0, 2, 0), (2, 2, 1)]
        self.load_seg(self.ctx_d, 512)
        stage = 0
        done = False
        for l in range(NL):
            last = l == NL - 1
            self.ffn(l, 0, 1, 0)
            stage += 1
            if stop == ("c", stage):
                done = True
                break
            self.mixer(l, 1, 0, ctx_seqs, rope=False, full=not last, bidx=None)
            stage += 1
            if stop == ("c", stage):
                done = True
                break
            if not last:
                self.ffn(l, 1, 1, 0)
                stage += 1
                if stop == ("c", stage):
                    done = True
                    break
        if stop is not None and stop[0] == "c":
            self.store_seg(self.dbgy_d, 512)
        else:
            for bi in range(2):
                self.load_seg(self.x_d[bi], SEQ)
                stage = 0
                done = False
                for l in range(NL):
                    self.ffn(l, 0, 4, 1 + bi)
                    stage += 1
                    if stop == ("x", stage):
                        done = True
                        break
                    self.mixer(l, 4, 1 + bi, [(0, 16, bi)], rope=True, full=True, bidx=bi)
                    stage += 1
                    if stop == ("x", stage):
                        done = True
                        break
                    self.ffn(l, 1, 4, 1 + bi)
                    stage += 1
                    if stop == ("x", stage):
                        done = True
                        break
                if not done:
                    self.final_norm(4)
                self.store_seg(self.out_d[bi], SEQ)
        outs = [self.b("outd", i) for i in range(8)]
        self.P.op("sp", lambda h: h.nop(), reads=outs)
        self.P.emit()
        return self.nc


_CACHE = {}


def kernel(x, c, ctx, c_ctx, w_mod, b_mod, norm_g, ffn_w1, ffn_w3, ffn_w2, w_in, w_out,
           pool_w, pool_scale, ret_decay_fwd, ret_decay_bwd, ret_gn_g, conv_dw, conv_b,
           conv_ln_g, conv_ln_b, final_g, _dbg_stop=None, _cores=None):
    inp = dict(x=x, c=c, ctx=ctx, c_ctx=c_ctx, w_mod=w_mod, b_mod=b_mod, norm_g=norm_g, ffn_w1=ffn_w1,
               ffn_w3=ffn_w3, ffn_w2=ffn_w2, w_in=w_in, w_out=w_out, pool_w=pool_w, pool_scale=pool_scale,
               ret_decay_fwd=ret_decay_fwd, ret_decay_bwd=ret_decay_bwd, ret_gn_g=ret_gn_g, conv_dw=conv_dw,
               conv_b=conv_b, conv_ln_g=conv_ln_g, conv_ln_b=conv_ln_b, final_g=final_g)
    inp = {k: np.asarray(v) for k, v in inp.items()}
    cores = list(range(NCORES)) if _cores is None else _cores
    W = relayout_weights(inp)
    rope = build_rope()
    poolw = build_poolw(inp)
    k = K(dbg_stop=_dbg_stop)
    nc = k.build()
    in_maps = []
    xs = np.asarray(inp["x"], np.float32)
    cs = np.asarray(inp["ctx"], np.float32)
    for core in cores:
        m = dict(W)
        m["x"] = np.ascontiguousarray(xs[2 * core: 2 * core + 2])
        m["ctx"] = np.ascontiguousarray(cs[2 * core: 2 * core + 2].reshape(2 * CTX, D))
        m["cp"], m["cp2"] = build_cp(inp, core)
        m["rope"] = rope
        m["poolw"] = poolw
        in_maps.append(m)
    res = run_bass_kernel_spmd(nc, in_maps, core_ids=list(range(len(cores))))
    if _dbg_stop is not None:
        return res.results
    out = np.concatenate([np.asarray(r["out"], np.float32) for r in res.results], axis=0)
    return out
```
